# Optimizing a Trainium2 kernel written in Bass

```python
import math
import jax, jax.numpy as jnp
from jax import lax
import numpy as np

D_MODEL = 1024
BATCH = 8
SEQ = 2048
DEPTH = 1
DEC_BATCH = 128
DEC_SEQ = 4
PAST_LEN = 2048
PAGE_SIZE = 128

RWKV_HEAD = 64
D_RWKV = D_MODEL
RWKV_HEADS = D_RWKV // RWKV_HEAD
R_DECAY = 64
R_ICL = 64
LNX_EPS = 64e-5
ATT_HEAD = 64
D_ATT = D_MODEL
ATT_HEADS = D_ATT // ATT_HEAD
ATT_KV_HEADS = 2
ATT_GROUP = ATT_HEADS // ATT_KV_HEADS
IDX_HEADS = 8
IDX_DIM = 64
TOPK_MAX = 256
QBLOCK = 128
REL_BUCKETS = 32
REL_MAX_DIST = 128
NORM_EPS = 1e-6
POOL_FACTOR = 1.25

RWKV_SIZES = (D_RWKV, D_RWKV, D_RWKV, D_RWKV, R_DECAY, R_ICL)
RWKV_COLS = sum(RWKV_SIZES)
ATT_SIZES = (D_ATT, ATT_KV_HEADS * ATT_HEAD, ATT_KV_HEADS * ATT_HEAD,
             IDX_HEADS * IDX_DIM, IDX_DIM, IDX_HEADS, D_ATT)
ATT_COLS = sum(ATT_SIZES)
GATE_SIZES = (D_MODEL, D_MODEL)
N_COLS = RWKV_COLS + ATT_COLS + sum(GATE_SIZES)

kernel_name = 'rwkv7_dsa_gated_hybrid_step'


def _split(z, sizes):
    idx = np.cumsum(sizes)[:-1].tolist()
    return jnp.split(z, idx, axis=-1)


def _rmsnorm(x, g, eps):
    xf = x.astype(jnp.float32)
    y = xf * lax.rsqrt(jnp.mean(xf * xf, axis=-1, keepdims=True) + eps)
    return (y * g.astype(jnp.float32)).astype(x.dtype)


def _t5_bucket(dist):
    max_exact = REL_BUCKETS // 2
    d = jnp.maximum(dist, 0)
    df = jnp.maximum(d, 1).astype(jnp.float32)
    large = max_exact + (jnp.log(df / max_exact) / math.log(REL_MAX_DIST / max_exact)
                         * (REL_BUCKETS - max_exact)).astype(jnp.int32)
    large = jnp.minimum(large, REL_BUCKETS - 1)
    return jnp.where(d < max_exact, d, large)


def _wkv7_scan(r, w, k, v, kk, a, s0):
    def step(s, inp):
        r_t, w_t, k_t, v_t, kk_t, a_t = inp
        sa = jnp.einsum('bhij,bhj->bhi', s, -kk_t)
        s = (s * w_t[:, :, None, :] + sa[..., None] * (kk_t * a_t)[:, :, None, :]
             + v_t[..., None] * k_t[:, :, None, :])
        return s, jnp.einsum('bhij,bhj->bhi', s, r_t)
    xs = tuple(jnp.swapaxes(t, 0, 1) for t in (r, w, k, v, kk, a))
    s_T, out = lax.scan(step, s0, xs)
    return jnp.swapaxes(out, 0, 1), s_T


def _rwkv_branch(zr, shift_prev, s0, mu, w0, w2, a0, a2, k_k, k_a, r_k, lnx_g, lnx_b):
    B, T, _ = zr.shape
    f32 = jnp.float32
    prev = jnp.concatenate([shift_prev[:, None, :].astype(zr.dtype), zr[:, :-1]], axis=1)
    zs = zr + (prev - zr) * mu
    r, k, v, g, wd, ad = _split(zs, RWKV_SIZES)
    heads = lambda t: t.astype(f32).reshape(B, T, RWKV_HEADS, RWKV_HEAD)
    w_log = -jax.nn.softplus(-(w0 + jnp.tanh(wd) @ w2).astype(f32)) - 0.5
    decay = jnp.exp(-jnp.exp(w_log))
    a = jax.nn.sigmoid((a0 + ad @ a2).astype(f32))
    kk = heads(k * k_k)
    kk = kk / jnp.maximum(jnp.sqrt(jnp.sum(kk * kk, axis=-1, keepdims=True)), 1e-12)
    k_mod = heads(k.astype(f32) * (1.0 + (a - 1.0) * k_a.astype(f32)))
    r_h, v_h, a_h = heads(r), heads(v), heads(a)
    out, s_T = _wkv7_scan(r_h, heads(decay), k_mod, v_h, kk, a_h, s0.astype(f32))
    mean = jnp.mean(out, axis=-1, keepdims=True)
    var = jnp.mean(jnp.square(out - mean), axis=-1, keepdims=True)
    o = ((out - mean) * lax.rsqrt(var + LNX_EPS)).reshape(B, T, D_RWKV)
    o = o * lnx_g.astype(f32) + lnx_b.astype(f32)
    bonus = jnp.sum(r_h * k_mod * r_k.astype(f32), axis=-1, keepdims=True) * v_h
    o = (o + bonus.reshape(B, T, D_RWKV)) * jax.nn.silu(g.astype(f32))
    return o.astype(zr.dtype), s_T.astype(s0.dtype), zr[:, -1]


def _sparse_attend_block(q, qi, wi, qpos, k, v, ki, topk, rel_bias):
    B, Tq = q.shape[:2]
    L = k.shape[1]
    f32 = jnp.float32
    idx_logits = jnp.einsum('bthd,bsd->bths', qi.astype(f32), ki.astype(f32))
    score = jnp.einsum('bths,bth->bts', jax.nn.relu(idx_logits), wi.astype(f32))
    kpos = jnp.arange(L, dtype=jnp.int32)
    visible = kpos[None, :] <= qpos[:, None]
    score = jnp.where(visible[None], score, -jnp.inf)
    _, sel = lax.top_k(score, topk)
    gather = jax.vmap(lambda rows, ids: rows[ids])
    k_sel = gather(k, sel)
    v_sel = gather(v, sel)
    qg = q.reshape(B, Tq, ATT_KV_HEADS, ATT_GROUP, ATT_HEAD)
    logits = jnp.einsum('btkgd,btskd->btkgs', qg.astype(f32), k_sel.astype(f32)) * (ATT_HEAD ** -0.5)
    dist = qpos[None, :, None] - sel
    bias = rel_bias.astype(f32)[_t5_bucket(dist)]
    bias = bias.reshape(B, Tq, topk, ATT_KV_HEADS, ATT_GROUP).transpose(0, 1, 3, 4, 2)
    valid = (dist >= 0)[:, :, None, None, :]
    logits = jnp.where(valid, logits + bias, -jnp.inf)
    p = jax.nn.softmax(logits, axis=-1)
    o = jnp.einsum('btkgs,btskd->btkgd', p, v_sel.astype(f32))
    return o.reshape(B, Tq, D_ATT).astype(q.dtype)


def _attn_branch(za, k_past, v_past, ki_past, pos0, topk, qn_g, kn_g, rel_bias):
    B, T, _ = za.shape
    q, kn, vn, qi, kin, wi, g = _split(za, ATT_SIZES)
    q = _rmsnorm(q.reshape(B, T, ATT_HEADS, ATT_HEAD), qn_g, NORM_EPS)
    kn = _rmsnorm(kn.reshape(B, T, ATT_KV_HEADS, ATT_HEAD), kn_g, NORM_EPS)
    vn = vn.reshape(B, T, ATT_KV_HEADS, ATT_HEAD)
    qi = qi.reshape(B, T, IDX_HEADS, IDX_DIM) * (IDX_DIM ** -0.5)
    wi = wi * (IDX_HEADS ** -0.5)
    if k_past is None:
        k_all, v_all, ki_all = kn, vn, kin
    else:
        k_all = jnp.concatenate([k_past.astype(kn.dtype), kn], axis=1)
        v_all = jnp.concatenate([v_past.astype(vn.dtype), vn], axis=1)
        ki_all = jnp.concatenate([ki_past.astype(kin.dtype), kin], axis=1)
    blk = min(QBLOCK, T)
    nblk = T // blk
    qpos = pos0 + jnp.arange(T, dtype=jnp.int32)

    def to_blocks(t):
        return jnp.moveaxis(t.reshape(B, nblk, blk, *t.shape[2:]), 1, 0)

    def body(args):
        qb, qib, wib, pb = args
        return _sparse_attend_block(qb, qib, wib, pb, k_all, v_all, ki_all, topk, rel_bias)

    ob = lax.map(body, (to_blocks(q), to_blocks(qi), to_blocks(wi), qpos.reshape(nblk, blk)))
    o = jnp.moveaxis(ob, 0, 1).reshape(B, T, D_ATT)
    return o * jax.nn.silu(g), kn, vn, kin


def _layer(x, shift_prev, s0, k_past, v_past, ki_past, pos0, topk,
           norm_g, w_in, mu, w0, w2, a0, a2, k_k, k_a, r_k, lnx_g, lnx_b,
           qn_g, kn_g, rel_bias, w_pa, w_pb, w_out):
    xn = _rmsnorm(x, norm_g, NORM_EPS)
    z = xn @ w_in
    zr, za, zg = _split(z, (RWKV_COLS, ATT_COLS, sum(GATE_SIZES)))
    oa, s_T, last = _rwkv_branch(zr, shift_prev, s0, mu, w0, w2, a0, a2, k_k, k_a, r_k, lnx_g, lnx_b)
    ob, kn, vn, kin = _attn_branch(za, k_past, v_past, ki_past, pos0, topk, qn_g, kn_g, rel_bias)
    ga, gb = _split(zg, GATE_SIZES)
    merged = jax.nn.sigmoid(ga) * (oa @ w_pa) + jax.nn.sigmoid(gb) * (ob @ w_pb)
    return x + merged @ w_out, (kn, vn, kin, s_T, last)


def setup_inputs(seed: int = 0) -> dict:
    key = jax.random.key(seed)
    ks = jax.random.split(key, 32)
    f32 = jnp.float32
    n_pages = PAST_LEN // PAGE_SIZE
    n_phys = int(math.ceil(POOL_FACTOR * DEC_BATCH * n_pages))
    nrm = lambda k, shape, s: jax.random.normal(k, shape, f32) * s
    x_prompt = nrm(ks[0], (BATCH, SEQ, D_MODEL), 1.0)
    x_sample = nrm(ks[1], (DEC_BATCH, DEC_SEQ, D_MODEL), 1.0)
    cache_k = nrm(ks[2], (DEPTH, n_phys, PAGE_SIZE, ATT_KV_HEADS, ATT_HEAD), 1.0)
    cache_v = nrm(ks[3], (DEPTH, n_phys, PAGE_SIZE, ATT_KV_HEADS, ATT_HEAD), 1.0)
    cache_kidx = nrm(ks[4], (DEPTH, n_phys, PAGE_SIZE, IDX_DIM), 1.0)
    state_wkv = nrm(ks[5], (DEPTH, DEC_BATCH, RWKV_HEADS, RWKV_HEAD, RWKV_HEAD), 0.3)
    state_shift = nrm(ks[6], (DEPTH, DEC_BATCH, RWKV_COLS), 1.0)
    page_table = jax.random.permutation(ks[7], n_phys)[:DEC_BATCH * n_pages]
    page_table = page_table.reshape(DEC_BATCH, n_pages).astype(jnp.int32)
    norm_g = 1.0 + nrm(ks[8], (DEPTH, D_MODEL), 0.02)
    w_in = nrm(ks[9], (DEPTH, D_MODEL, N_COLS), D_MODEL ** -0.5)
    shift_mu = jax.random.uniform(ks[10], (DEPTH, RWKV_COLS), f32, 0.1, 0.9)
    w0 = jax.random.uniform(ks[11], (DEPTH, D_RWKV), f32, -5.0, 1.0)
    w2 = nrm(ks[12], (DEPTH, R_DECAY, D_RWKV), 0.1 * R_DECAY ** -0.5)
    a0 = nrm(ks[13], (DEPTH, D_RWKV), 0.1)
    a2 = nrm(ks[14], (DEPTH, R_ICL, D_RWKV), 0.5 * R_ICL ** -0.5)
    k_k = 0.85 + nrm(ks[15], (DEPTH, D_RWKV), 0.02)
    k_a = 1.0 + nrm(ks[16], (DEPTH, D_RWKV), 0.02)
    r_k = nrm(ks[17], (DEPTH, RWKV_HEADS, RWKV_HEAD), 0.1)
    lnx_g = 1.0 + nrm(ks[18], (DEPTH, D_RWKV), 0.02)
    lnx_b = nrm(ks[19], (DEPTH, D_RWKV), 0.02)
    q_norm_g = 1.0 + nrm(ks[20], (DEPTH, ATT_HEAD), 0.02)
    k_norm_g = 1.0 + nrm(ks[21], (DEPTH, ATT_HEAD), 0.02)
    rel_bias = nrm(ks[22], (REL_BUCKETS, ATT_HEADS), 0.5)
    w_pa = nrm(ks[23], (DEPTH, D_RWKV, D_MODEL), D_RWKV ** -0.5)
    w_pb = nrm(ks[24], (DEPTH, D_ATT, D_MODEL), D_ATT ** -0.5)
    w_out = nrm(ks[25], (DEPTH, D_MODEL, D_MODEL), D_MODEL ** -0.5)
    return {'x_prompt': x_prompt, 'x_sample': x_sample, 'cache_k': cache_k, 'cache_v': cache_v,
            'cache_kidx': cache_kidx, 'state_wkv': state_wkv, 'state_shift': state_shift,
            'page_table': page_table, 'norm_g': norm_g, 'w_in': w_in, 'shift_mu': shift_mu,
            'w0': w0, 'w2': w2, 'a0': a0, 'a2': a2, 'k_k': k_k, 'k_a': k_a, 'r_k': r_k,
            'lnx_g': lnx_g, 'lnx_b': lnx_b, 'q_norm_g': q_norm_g, 'k_norm_g': k_norm_g,
            'rel_bias': rel_bias, 'w_pa': w_pa, 'w_pb': w_pb, 'w_out': w_out}


def reference(x_prompt, x_sample, cache_k, cache_v, cache_kidx, state_wkv, state_shift, page_table,
              norm_g, w_in, shift_mu, w0, w2, a0, a2, k_k, k_a, r_k, lnx_g, lnx_b,
              q_norm_g, k_norm_g, rel_bias, w_pa, w_pb, w_out):
    bsz, seq = x_prompt.shape[0], x_prompt.shape[1]
    dec_bsz, dec_seq = x_sample.shape[0], x_sample.shape[1]
    past_len = page_table.shape[1] * cache_k.shape[2]
    topk_p = min(TOPK_MAX, seq // 4)
    topk_s = min(TOPK_MAX, (past_len + dec_seq) // 4)
    zero_shift = jnp.zeros((bsz, RWKV_COLS), x_prompt.dtype)
    zero_wkv = jnp.zeros((bsz, RWKV_HEADS, RWKV_HEAD, RWKV_HEAD), x_prompt.dtype)
    hp, hs = x_prompt, x_sample
    outs_p, outs_s = [], []
    for l in range(DEPTH):
        params = (norm_g[l], w_in[l], shift_mu[l], w0[l], w2[l], a0[l], a2[l], k_k[l], k_a[l], r_k[l],
                  lnx_g[l], lnx_b[l], q_norm_g[l], k_norm_g[l], rel_bias, w_pa[l], w_pb[l], w_out[l])
        hp, st_p = _layer(hp, zero_shift, zero_wkv, None, None, None, 0, topk_p, *params)
        gather_past = lambda pool: pool[l][page_table].reshape(dec_bsz, past_len, *pool.shape[3:])
        hs, st_s = _layer(hs, state_shift[l], state_wkv[l], gather_past(cache_k), gather_past(cache_v),
                          gather_past(cache_kidx), past_len, topk_s, *params)
        outs_p.append(st_p)
        outs_s.append(st_s)
    stk = lambda outs, i: jnp.stack([o[i] for o in outs], axis=0)
    return (hp, hs,
            stk(outs_p, 0), stk(outs_p, 1), stk(outs_p, 2), stk(outs_p, 3), stk(outs_p, 4),
            stk(outs_s, 0), stk(outs_s, 1), stk(outs_s, 2), stk(outs_s, 3), stk(outs_s, 4))
```

```python
import math
from contextlib import ExitStack
import numpy as np
import concourse.bass as bass
import concourse.mybir as mybir
from concourse.bass_utils import run_bass_kernel_spmd

F32 = mybir.dt.float32
BF16 = mybir.dt.bfloat16
I32 = mybir.dt.int32
AF = mybir.ActivationFunctionType
ALU = mybir.AluOpType
AX = mybir.AxisListType

NCORES = 8
D = 1024
NP_ = 2048
NS = 64
NT = NP_ + NS
NCOLS = 9160
C_R, C_K, C_V, C_G, C_WD, C_AD = 0, 1024, 2048, 3072, 4096, 4160
A0 = 4224
C_Q, C_AK, C_AV, C_QI, C_KI, C_WI, C_AG = A0, A0 + 1024, A0 + 1152, A0 + 1280, A0 + 1792, A0 + 1856, A0 + 1864
C_GA, C_GB = 7112, 8136
NPHYS = 2560
import os
STAGE = int(os.environ.get('KSTAGE', '99'))

import os as _os
_SES = _os.environ.get("SES", "act,dve,pool")
SAME_ENGINE_SYNC = {"pe": False, "act": "act" in _SES, "dve": "dve" in _SES, "pool": "pool" in _SES, "sp": True}


class _Op:
    __slots__ = ("eng", "fn", "deps", "is_dma", "dsem", "count", "needs_inc", "waits", "idx")


class Prog:
    def __init__(self, nc):
        self.nc = nc
        self.ops = {e: [] for e in ("pe", "act", "dve", "pool", "sp")}
        self.last_w = {}
        self.readers = {}
        self.dma_sem_names = {}
        self.all_ops = []
        self.store_ops = []
        self.bar = []
        self.last_dma = {}
        self.cur = None
        self.threads = {}

    def barrier(self):
        self.bar = [v[-1] for v in self.ops.values() if v] + list(self.last_dma.values())

    def _add(self, eng, fn, reads, writes, is_dma=False, dsem=None):
        op = _Op()
        op.eng = eng; op.fn = fn; op.is_dma = is_dma; op.dsem = dsem
        op.needs_inc = False; op.count = None; op.waits = []
        deps = []
        writes = writes + [r for r in reads if isinstance(r, tuple) and r[0] == "ps" and r not in writes]
        for r in reads:
            w = self.last_w.get(r)
            if w is not None:
                deps.append(w)
        for r in writes:
            w = self.last_w.get(r)
            if w is not None:
                deps.append(w)
            deps.extend(self.readers.get(r, ()))
        deps.extend(self.bar)
        op.deps = deps
        for r in reads:
            self.readers.setdefault(r, []).append(op)
        for r in writes:
            self.last_w[r] = op
            self.readers[r] = []
        self.ops[eng].append(op)
        self.all_ops.append(op)
        return op

    def op(self, eng, fn, reads=(), writes=()):
        if self.cur is not None:
            self.cur.append(("op", eng, fn, list(reads), list(writes), None, False))
            return None
        return self._add(eng, fn, list(reads), list(writes))

    def thread(self, name):
        self.cur = [] if name is not None else None
        if name is not None:
            self.threads[name] = self.cur

    def merge(self, names):
        self.cur = None
        qs = [self.threads.pop(n) for n in names if n in self.threads]
        idx = [0] * len(qs)
        alive = True
        while alive:
            alive = False
            for k, q in enumerate(qs):
                if idx[k] < len(q):
                    kind, eng, fn, reads, writes, semkey, store = q[idx[k]]
                    idx[k] += 1
                    alive = True
                    if kind == "op":
                        self._add(eng, fn, reads, writes)
                    else:
                        self._dma_now(eng, fn, reads, writes, semkey, store)

    def dma(self, q, fn, reads=(), writes=(), semkey=None, store=False):
        if semkey is None:
            semkey = ("w", writes[0]) if writes else ("r", reads[0])
        if self.cur is not None:
            self.cur.append(("dma", q, fn, list(reads), list(writes), semkey, store))
            return None
        return self._dma_now(q, fn, reads, writes, semkey, store)

    def _dma_now(self, q, fn, reads, writes, semkey, store):
        if semkey not in self.dma_sem_names:
            self.dma_sem_names[semkey] = len(self.dma_sem_names)
        op = self._add(q, fn, list(reads), list(writes), is_dma=True, dsem=semkey)
        self.last_dma[semkey] = op
        if store:
            self.store_ops.append(op)
        return op

    def emit(self):
        nc = self.nc
        def skip(d, op):
            return (not d.is_dma) and d.eng == op.eng and (not op.is_dma) and not SAME_ENGINE_SYNC[op.eng]
        for op in self.all_ops:
            for d in op.deps:
                if d.is_dma or d is op or skip(d, op):
                    continue
                d.needs_inc = True
        cnt = {e: 0 for e in self.ops}
        dcnt = {k: 0 for k in self.dma_sem_names}
        for op in self.all_ops:
            if op.is_dma:
                dcnt[op.dsem] += 16
                op.count = dcnt[op.dsem]
            elif op.needs_inc:
                cnt[op.eng] += 1
                op.count = cnt[op.eng]
        waited = {e: {} for e in self.ops}
        for op in self.all_ops:
            need = {}
            for d in op.deps:
                if d is op or d.count is None:
                    continue
                if d.is_dma:
                    key = ("d", d.dsem)
                else:
                    if skip(d, op):
                        continue
                    key = ("e", d.eng)
                if need.get(key, 0) < d.count:
                    need[key] = d.count
            w = waited[op.eng]
            for key, v in need.items():
                if w.get(key, 0) < v:
                    w[key] = v
                    op.waits.append((key, v))
        final_waits = {}
        for op in self.store_ops:
            key = ("d", op.dsem)
            final_waits[key] = max(final_waits.get(key, 0), op.count)
        with ExitStack() as es:
            sems = {}
            for e in self.ops:
                sems[("e", e)] = es.enter_context(nc.semaphore(f"s_{e}"))
            for k, i in self.dma_sem_names.items():
                sems[("d", k)] = es.enter_context(nc.semaphore(f"sd_{i}"))
            block = es.enter_context(nc.Block())

            def run(engname):
                def body(eng):
                    for op in self.ops[engname]:
                        for key, v in op.waits:
                            eng.wait_ge(sems[key], v)
                        ins = op.fn(eng)
                        if op.is_dma:
                            ins.then_inc(sems[("d", op.dsem)], 16)
                        elif op.needs_inc:
                            ins.then_inc(sems[("e", engname)], 1)
                    if engname == "sp":
                        for key, v in final_waits.items():
                            eng.wait_ge(sems[key], v)
                return body

            block.tensor(run("pe"))
            block.scalar(run("act"))
            block.vector(run("dve"))
            block.gpsimd(run("pool"))
            block.sync(run("sp"))
        return {e: len(v) for e, v in self.ops.items()}


class Arena:
    def __init__(self, nc, es, nbytes):
        self.t = es.enter_context(nc.sbuf_tensor("arena", [128, nbytes // 4], F32))
        self.off = 0
        self.peak = 0
        self.cap = nbytes

    def alloc(self, shape, dt=F32):
        esz = 4 if dt in (F32, I32) else 2
        n = 1
        for d in shape[1:]:
            n *= d
        nb = (n * esz + 31) // 32 * 32
        assert self.off + nb <= self.cap, ("arena overflow", self.off, nb, self.cap)
        v = self.t[0:shape[0], self.off // 4:(self.off + nb) // 4]
        if dt != F32:
            v = v.bitcast(dt)
        v = v[:, 0:n]
        if len(shape) == 3:
            v = v.rearrange("p (a b) -> p a b", a=shape[1])
        elif len(shape) == 4:
            v = v.rearrange("p (a b c) -> p a b c", a=shape[1], b=shape[2])
        self.off += nb
        self.peak = max(self.peak, self.off)
        return v

    def mark(self):
        return self.off

    def release(self, m):
        self.off = m


TT = [(i * 128, 128) for i in range(16)] + [(2048, 64)]
NTL = [(0, 512), (512, 512), (1024, 512), (1536, 512), (2048, 64)]


def _t5_bucket_np(d):
    d = np.maximum(d, 0)
    df = np.maximum(d, 1).astype(np.float32)
    large = 16 + (np.log(df / np.float32(16)) / np.float32(math.log(128 / 16)) * np.float32(16)).astype(np.int32)
    large = np.minimum(large, 31)
    return np.where(d < 16, d, large)


def build_program():
    nc = bass.Bass("TRN2", target_bir_lowering=False)
    di = lambda name, shape, dt=F32: nc.dram_tensor(name, list(shape), dt, kind="ExternalInput").ap()
    do = lambda name, shape, dt=F32: nc.dram_tensor(name, list(shape), dt, kind="ExternalOutput").ap()
    x_p = di("x_p", [NP_, D]); x_s = di("x_s", [NS, D])
    w_in = di("w_in", [D, NCOLS]); norm_g = di("norm_g", [1, D])
    qn_g = di("q_norm_g", [1, 64]); kn_g = di("k_norm_g", [1, 64])
    ident_d = di("ident", [128, 128]); bones_d = di("bones", [128, 128])
    rel_bias_d = di("rel_bias", [32, 16]); oh_d = di("oh_p", [32, 2, 128, 128], BF16)
    caus0_d = di("caus0", [128, 128]); causS_d = di("causS", [128, 128])
    dbg_ob = do("dbg_ob", [128, 8, NT], BF16) if os.environ.get("DBG") else None
    dbg_oa = do("dbg_oa", [128, 8, NT], BF16) if os.environ.get("DBG") else None
    w_pa = di("w_pa", [D, D]); w_pb = di("w_pb", [D, D]); w_out = di("w_out", [D, D])
    mu_d = di("shift_mu", [1, 4224]); w0_d = di("w0", [1, 1024]); a0_d = di("a0", [1, 1024]); w2_d = di("w2", [64, 1024]); a2_d = di("a2", [64, 1024])
    kk_d = di("k_k", [1, 1024]); ka_d = di("k_a", [1, 1024]); rk_d = di("r_k", [1, 1024]); lg_d = di("lnx_g", [1, 1024]); lb_d = di("lnx_b", [1, 1024])
    swkv_d = di("state_wkv", [16, 16, 64, 64]); ssh_d = di("state_shift", [16, 4224])
    mu64_d = di("mask_u", [64, 64]); mui64_d = di("mask_ui", [64, 64]); ml64_d = di("mask_l", [64, 64]); seg_d = di("segmask", [64, 2, 256])
    if STAGE >= 6:
        cache_k = di("cache_k", [NPHYS * 8, 2048]); cache_v = di("cache_v", [NPHYS * 8, 2048]); cache_ki = di("cache_ki", [NPHYS * 8, 1024])
    ptrep_d = di("ptrep", [128, 16], I32); cmod_d = di("cmod", [128, 1], I32)
    ohs_d = di("ohs", [32, 16, 4, 128], BF16); ohn_d = di("ohn", [32, 4, 64], BF16)
    vm_d = di("vm", [64, 16, 4]); selh_d = di("selh", [128, 16, 32]); masks_d = di("masks", [64, 64])
    y_p = do("y_p", [NP_, D]); y_s = do("y_s", [NS, D])
    k_p = do("k_p", [NP_, 128]); v_p = do("v_p", [NP_, 128]); ki_p = do("ki_p", [NP_, 64])
    k_s = do("k_s", [NS, 128]); v_s = do("v_s", [NS, 128]); ki_s = do("ki_s", [NS, 64])
    sh_p = do("sh_p", [1, 4224]); sh_s = do("sh_s", [16, 4224])
    wkv_p = do("wkv_p", [16, 64, 64]); wkv_s = do("wkv_s", [16, 16, 64, 64])

    es = ExitStack()
    with es:
        A = Arena(nc, es, 207 * 1024)
        ps = es.enter_context(nc.psum_tensor("ps", [128, 7, 512], F32))
        psb = es.enter_context(nc.psum_tensor("psb", [128, 8, 128], BF16))
        P = Prog(nc)

        dumps = {}

        def dbgdump(name, ap, shape):
            if not os.environ.get("DBG") or name in dumps:
                return
            d_ = do("dd_" + name, shape, BF16 if name == "oodd" else F32)
            dumps[name] = d_
            P.dma("sp", lambda e: e.dma_start(out=d_, in_=ap), reads=[name], store=True, semkey=("dd", name))

        def finish():
            counts = P.emit()
            print("ops:", counts, "sbuf peak", A.peak)
            return nc

        ident = A.alloc([128, 128]); ident_b = A.alloc([128, 128], BF16)
        bones = A.alloc([128, 128])
        gcol = A.alloc([128, 8])
        gq2 = A.alloc([128, 1]); gk2 = A.alloc([128, 1]); gkq = A.alloc([128, 1]); eps6 = A.alloc([128, 1])
        P.dma("sp", lambda e: e.dma_start(out=ident[:, :], in_=ident_d), writes=["ident"])
        P.dma("sp", lambda e: e.dma_start(out=bones[:, :], in_=bones_d), writes=["bones"])
        P.dma("sp", lambda e: e.dma_start(out=gcol[:, :], in_=norm_g.rearrange("o (k p) -> p (o k)", p=128), allow_slow_non_contiguous=True), writes=["gcol"])
        for hh in range(2):
            P.dma("sp", lambda e, hh=hh: e.dma_start(out=gq2[hh * 64:(hh + 1) * 64, :], in_=qn_g.rearrange("o d -> d o"), allow_slow_non_contiguous=True), writes=["gq2"], semkey="gq2")
            P.dma("sp", lambda e, hh=hh: e.dma_start(out=gk2[hh * 64:(hh + 1) * 64, :], in_=kn_g.rearrange("o d -> d o"), allow_slow_non_contiguous=True), writes=["gk2"], semkey="gk2")
        eps24 = A.alloc([128, 1]); epsln = A.alloc([128, 1])
        P.op("dve", lambda e: e.memset(eps24[:, :], 1e-24), writes=["eps24"])
        P.op("dve", lambda e: e.memset(epsln[:, :], 64e-5), writes=["epsln"])
        P.op("dve", lambda e: e.memset(eps6[:, :], 1e-6), writes=["eps6"])
        P.op("dve", lambda e: e.tensor_copy(out=ident_b[:, :], in_=ident[:, :]), reads=["ident"], writes=["ident_b"])
        P.op("dve", lambda e: e.tensor_scalar(out=gkq[:, :], in0=gk2[:, :], scalar1=gq2[:, 0:1], scalar2=0.125, op0=ALU.mult, op1=ALU.mult), reads=["gk2", "gq2"], writes=["gkq"])

        xnT = A.alloc([128, 8, NT], BF16)
        wst = A.alloc([128, 2, 8, 128]); wb = A.alloc([128, 2, 8, 128], BF16)
        m0 = A.mark()
        xin = A.alloc([128, 2, D]); xh = A.alloc([128, 2, D], BF16); junk = A.alloc([128, D])
        ss = A.alloc([128, 17]); rstd = A.alloc([128, 17])
        P.op("dve", lambda e: e.memset(ss[:, :], 0.0), writes=["ss"])
        for ti, (t0, n) in enumerate(TT):
            sl = ti % 2
            src = x_p[t0:t0 + n, :] if ti < 16 else x_s[:, :]
            P.dma("sp", lambda e, sl=sl, n=n, src=src: e.dma_start(out=xin[0:n, sl, :], in_=src), writes=[("xin", sl)])
            P.op("act", lambda e, sl=sl, n=n, ti=ti: e.activation(out=junk[0:n, :], in_=xin[0:n, sl, :], func=AF.Square, accum_out=ss[0:n, ti:ti + 1]),
                 reads=[("xin", sl), "ss"], writes=["junk", ("ss", ti)])
            P.op("act", lambda e, n=n, ti=ti: e.activation(out=rstd[0:n, ti:ti + 1], in_=ss[0:n, ti:ti + 1], func=AF.Sqrt, bias=eps6[0:n, 0:1], scale=1.0 / D),
                 reads=[("ss", ti), "eps6"], writes=[("rstd", ti)])
            P.op("dve", lambda e, n=n, ti=ti: e.reciprocal(out=rstd[0:n, ti:ti + 1], in_=rstd[0:n, ti:ti + 1]),
                 reads=[("rstd", ti)], writes=[("rstd", ti)])
            P.op("dve", lambda e, sl=sl, n=n, ti=ti: e.tensor_scalar(out=xh[0:n, sl, :], in0=xin[0:n, sl, :], scalar1=rstd[0:n, ti:ti + 1], scalar2=None, op0=ALU.mult),
                 reads=[("xin", sl), ("rstd", ti)], writes=[("xh", sl)])
            for kc in range(8):
                P.op("pe", lambda e, sl=sl, n=n, kc=kc: e.transpose(psb[:, kc, 0:n], xh[0:n, sl, kc * 128:(kc + 1) * 128], ident_b[0:n, 0:n]),
                     reads=[("xh", sl), "ident_b"], writes=["psb"])
            P.op("act", lambda e, t0=t0, n=n: e.copy(out=xnT[:, :, t0:t0 + n], in_=psb[:, :, 0:n]), reads=["psb"], writes=["xnT"])
        A.release(m0); P.barrier()
        if STAGE < 1:
            return finish()

        wcnt = [0]

        def load_w(parts, m, src=None):
            sl = wcnt[0] % 2
            wcnt[0] += 1
            srcw = w_in if src is None else src
            for (dc, sc, wd_) in parts:
                P.dma("sp", lambda e, sl=sl, dc=dc, sc=sc, wd_=wd_, srcw=srcw: e.dma_start(
                    out=wst[:, sl, :, dc:dc + wd_], in_=srcw[:, sc:sc + wd_].rearrange("(k p) m -> p k m", p=128)),
                    writes=[("wst", sl)])
            if src is None:
                P.op("pool", lambda e, sl=sl, m=m: e.tensor_tensor(out=wb[:, sl, :, 0:m], in0=wst[:, sl, :, 0:m],
                                                                  in1=gcol[:, :].unsqueeze(2).to_broadcast([128, 8, m]), op=ALU.mult),
                     reads=[("wst", sl), "gcol"], writes=[("wb", sl)])
            else:
                P.op("pool", lambda e, sl=sl, m=m: e.tensor_copy(out=wb[:, sl, :, 0:m], in_=wst[:, sl, :, 0:m]), reads=[("wst", sl)], writes=[("wb", sl)])
            return sl

        bankc = [0]

        bank_sets = {None: (0, 1, 2, 3, 4, 5, 6), "tb": (0, 1), "ind": (2, 3, 4), "dep": (5, 6)}
        bank_ctr = {}
        bank_grp = [None]

        def nbank():
            g_ = bank_grp[0]
            s_ = bank_sets[g_]
            k_ = bank_ctr.get(g_, 0)
            bank_ctr[g_] = k_ + 1
            return s_[k_ % len(s_)]

        def proj_fm(sl, m, n0, nn, bank):
            for kc in range(8):
                P.op("pe", lambda e, kc=kc: e.matmul(ps[0:m, bank, 0:nn], lhsT=wb[:, sl, kc, 0:m], rhs=xnT[:, kc, n0:n0 + nn], start=(kc == 0), stop=(kc == 7)),
                     reads=[("wb", sl), "xnT"], writes=[("ps", bank)])

        def proj_tm(sl, m, t0, n, bank, col0=0):
            for kc in range(8):
                P.op("pe", lambda e, kc=kc: e.matmul(ps[0:n, bank, col0:col0 + m], lhsT=xnT[:, kc, t0:t0 + n], rhs=wb[:, sl, kc, 0:m], start=(kc == 0), stop=(kc == 7)),
                     reads=[("wb", sl), "xnT"], writes=[("ps", bank)])


        mT = A.alloc([128, 8, NT], BF16)
        mA = A.mark()
        oaT = A.alloc([128, 8, NT], BF16)
        mR = A.mark()
        HG = 4; NG = 16 // HG; NB = 4 * HG + 2
        ones64 = A.alloc([64, 64]); MU = A.alloc([64, 64]); MUI = A.alloc([64, 64]); ML = A.alloc([64, 64]); segm = A.alloc([64, 2, HG * 64])
        P.op("dve", lambda e: e.memset(ones64[:, :], 1.0), writes=["ones64"])
        P.dma("sp", lambda e: e.dma_start(out=MU[:, :], in_=mu64_d), writes=["MU"])
        P.dma("sp", lambda e: e.dma_start(out=MUI[:, :], in_=mui64_d), writes=["MUI"])
        P.dma("sp", lambda e: e.dma_start(out=ML[:, :], in_=ml64_d), writes=["ML"])
        P.dma("sp", lambda e: e.dma_start(out=segm[:, :, :], in_=seg_d), writes=["segm"])
        wr = A.alloc([128, 8, NB * 64], BF16)
        w2g = A.alloc([64, HG * 64]); a2g = A.alloc([64, HG * 64])
        prm = A.alloc([64, HG, 8])
        mug = A.alloc([64, NB])
        zbuf = A.alloc([64, NB, 65]); dsh = A.alloc([64, HG, 64]); zraw = A.alloc([64, NB, 64])
        shs = A.alloc([16, NB * 64]); shT = A.alloc([64, NB, 16])
        NARR = 22
        MD = BF16
        arrs = [A.alloc([64, HG, 64], MD if i_ in (10, 11, 12, 13, 14, 15, 19, 20, 21) else F32) for i_ in range(NARR)]
        (sg, aa, clr, Ein, Einv, Eex, EC, kk_, kkn, kmod, at_, bt_, kt_, rt_, bh_, kh_, bon, t1, t2, Vt, Bt, Kt) = arrs
        vb2 = [A.alloc([64, HG, 64], MD), A.alloc([64, HG, 64], MD)]; Pb = A.alloc([64, HG, 64], MD)
        mats = [A.alloc([64, HG, 64], MD) for _ in range(8)]
        (N0, N1, NT0, NT1, R0, R1, U0, Uf) = mats
        outT = A.alloc([64, HG, 64])
        pm = lambda dt_=MD: [A.alloc([64, HG, 64], dt_), A.alloc([64, HG, 64], dt_)]
        at2 = [at_, A.alloc([64, HG, 64], MD), A.alloc([64, HG, 64], MD)]; rt2 = [rt_, A.alloc([64, HG, 64], MD), A.alloc([64, HG, 64], MD)]
        bon2 = [bon, A.alloc([64, HG, 64]), A.alloc([64, HG, 64])]; sg2 = pm(F32) + [A.alloc([64, HG, 64])]
        bt2 = [bt_, A.alloc([64, HG, 64], MD)]; kt2 = [kt_, A.alloc([64, HG, 64], MD)]; bh2 = [bh_, A.alloc([64, HG, 64], MD)]; kh2 = [kh_, A.alloc([64, HG, 64], MD)]
        f1 = A.alloc([64, HG, 64]); f2_ = A.alloc([64, HG, 64])
        Vt2 = [Vt, A.alloc([64, HG, 64], MD)]; Bt2 = [Bt, A.alloc([64, HG, 64], MD)]; Kt2 = [Kt, A.alloc([64, HG, 64], MD)]
        TT2 = pm(); Aak2 = pm(); Arb2 = pm(); Ark2 = pm()
        GC2 = [A.alloc([64, HG, 16]), A.alloc([64, HG, 16]), A.alloc([64, HG, 16])]
        Pst = A.alloc([64, HG, 64]); Ssb = A.alloc([64, HG, 64]); tw = A.alloc([64, 64]); adc = A.alloc([64, 64])
        oodd = A.alloc([64, HG // 2, 64], BF16)
        NCH = int(os.environ.get("NCH", "32"))
        NSEQ = int(os.environ.get("NSQ", "16"))
        LD = -0.6065306597126334

        def bc(ap2, C=64):
            return ap2.unsqueeze(2).to_broadcast([64, HG, C])

        for G in range(NG):
            blocks = [(kind * 1024 + (HG * G + h) * 64) for kind in range(4) for h in range(HG)]
            for bi in range(2 * HG + 1):
                if bi < 2 * HG:
                    parts = [(0, blocks[2 * bi], 64), (64, blocks[2 * bi + 1], 64)]
                else:
                    parts = [(0, C_WD, 128)]
                sl = load_w(parts, 128)
                P.op("act", lambda e, sl=sl, bi=bi: e.copy(out=wr[:, :, bi * 128:(bi + 1) * 128], in_=wb[:, sl, :, :]), reads=[("wb", sl)], writes=["wr"])
            P.dma("sp", lambda e, G=G: e.dma_start(out=w2g[:, :], in_=w2_d[:, G * HG * 64:(G + 1) * HG * 64]), writes=["w2g"])
            P.dma("sp", lambda e, G=G: e.dma_start(out=a2g[:, :], in_=a2_d[:, G * HG * 64:(G + 1) * HG * 64]), writes=["a2g"])
            for wi_, pd in enumerate((w0_d, a0_d, kk_d, ka_d, rk_d, lg_d, lb_d)):
                P.dma("sp", lambda e, wi_=wi_, pd=pd, G=G: e.dma_start(out=prm[:, :, wi_], in_=pd[0:1, G * HG * 64:(G + 1) * HG * 64].rearrange("o (h j) -> j (o h)", j=64), allow_slow_non_contiguous=True),
                      writes=["prm"], semkey="prm")
            for kind in range(4):
                P.dma("sp", lambda e, kind=kind, G=G: e.dma_start(out=mug[:, kind * HG:(kind + 1) * HG], in_=mu_d[0:1, kind * 1024 + G * HG * 64:kind * 1024 + (G + 1) * HG * 64].rearrange("o (h j) -> j (o h)", j=64),
                                                                allow_slow_non_contiguous=True), writes=["mug"], semkey="mug")
            P.dma("sp", lambda e: e.dma_start(out=mug[:, 4 * HG:4 * HG + 2], in_=mu_d[0:1, C_WD:C_WD + 128].rearrange("o (h j) -> j (o h)", j=64), allow_slow_non_contiguous=True), writes=["mug"], semkey="mug")
            for kind in range(4):
                P.dma("sp", lambda e, kind=kind, G=G: e.dma_start(out=shs[:, kind * HG * 64:(kind + 1) * HG * 64], in_=ssh_d[:, kind * 1024 + G * HG * 64:kind * 1024 + (G + 1) * HG * 64]), writes=["shs"], semkey="shs")
            P.dma("sp", lambda e: e.dma_start(out=shs[:, 4 * HG * 64:NB * 64], in_=ssh_d[:, C_WD:C_WD + 128]), writes=["shs"], semkey="shs")
            bk = nbank()
            for blk in range(NB):
                P.op("pe", lambda e, bk=bk, blk=blk: e.transpose(ps[0:64, bk, blk * 16:(blk + 1) * 16], shs[:, blk * 64:(blk + 1) * 64], ident[0:16, 0:16]), reads=["shs", "ident"], writes=[("ps", bk)])
            P.op("act", lambda e, bk=bk: e.copy(out=shT[:, :, :], in_=ps[0:64, bk, 0:NB * 16].rearrange("p (q b) -> p q b", b=16)), reads=[("ps", bk)], writes=["shT"])
            zraw_valid = set()
            P.op("dve", lambda e: e.memset(zbuf[:, :, 0:1], 0.0), writes=["zbuf"])
            P.op("dve", lambda e: e.memset(Pst[:, :, :], 0.0), writes=["Pst"])
            P.op("dve", lambda e: e.memset(Pb[:, :, :], 0.0), writes=["Pb"])

            def token_batch(col0, L, nseg, p, q2=0):
                sgi = 0 if L == 64 else 1
                at_ = at2[p]; rt_ = rt2[p]; bon = bon2[p]; GC = GC2[p]; sgate = sg2[p]
                bt_ = bt2[q2]; kt_ = kt2[q2]; bh_ = bh2[q2]; kh_ = kh2[q2]; vb = vb2[q2]
                n_bt = f"bt_{q2}"; n_kt = f"kt_{q2}"; n_bh = f"bh_{q2}"; n_kh = f"kh_{q2}"; n_vb = f"vb{q2}"
                n_at = f"at_{p}"; n_rt = f"rt_{p}"; n_bon = f"bon{p}"; n_gc = f"GC{p}"; n_sgt = f"sgate{p}"
                chi = col0 // 64
                if L == 64 and chi % 2 == 1 and chi >= 1 and (chi - 1) in zraw_valid:
                    P.op("act", lambda e: e.copy(out=zbuf[:, :, 1:65], in_=zraw[:, :, :]), reads=["zraw"], writes=["zbuf"])
                else:
                    two = (L == 64 and chi % 2 == 0 and chi + 1 < NCH)
                    W_ = 128 if two else 64
                    for q4 in range(5):
                        nb = HG if q4 < 4 else 2
                        bk = nbank()
                        for hh in range(nb):
                            blk = q4 * HG + hh
                            for kc in range(8):
                                P.op("pe", lambda e, bk=bk, hh=hh, blk=blk, kc=kc, W_=W_: e.matmul(ps[0:64, bk, hh * W_:(hh + 1) * W_], lhsT=wr[:, kc, blk * 64:(blk + 1) * 64], rhs=xnT[:, kc, col0:col0 + W_],
                                                                                             start=(kc == 0), stop=(kc == 7)), reads=["wr", "xnT"], writes=[("ps", bk)])
                        pv_ = ps[0:64, bk, 0:nb * W_].rearrange("p (h t) -> p h t", t=W_)
                        P.op("act", lambda e, q4=q4, nb=nb, pv_=pv_: e.copy(out=zbuf[:, q4 * HG:q4 * HG + nb, 1:65], in_=pv_[:, :, 0:64]), reads=[("ps", bk)], writes=["zbuf"])
                        if two:
                            P.op("act", lambda e, q4=q4, nb=nb, pv_=pv_: e.copy(out=zraw[:, q4 * HG:q4 * HG + nb, :], in_=pv_[:, :, 64:128]), reads=[("ps", bk)], writes=["zraw"])
                    if two:
                        zraw_valid.add(chi)
                if L == 4:
                    zv4 = zbuf[:, :, 1:65].rearrange("p q (b t) -> p q b t", t=4)
                    for q4 in range(5):
                        nb = HG if q4 < 4 else 2
                        sl_ = slice(q4 * HG, q4 * HG + nb)
                        P.op("pool", lambda e, sl_=sl_: e.tensor_copy(out=dsh[:, 0:sl_.stop - sl_.start, :].rearrange("p q (b t) -> p q b t", t=4)[:, :, :, 1:4], in_=zv4[:, sl_, :, 0:3]), reads=["zbuf"], writes=["dsh"])
                        P.op("pool", lambda e, sl_=sl_: e.tensor_copy(out=dsh[:, 0:sl_.stop - sl_.start, :].rearrange("p q (b t) -> p q b t", t=4)[:, :, :, 0], in_=shT[:, sl_, :]), reads=["shT"], writes=["dsh"])
                        n_ = sl_.stop - sl_.start
                        P.op("dve", lambda e, sl_=sl_, n_=n_: e.tensor_tensor(out=dsh[:, 0:n_, :], in0=dsh[:, 0:n_, :], in1=zbuf[:, sl_, 1:65], op=ALU.subtract), reads=["dsh", "zbuf"], writes=["dsh"])
                        P.op("dve", lambda e, sl_=sl_, n_=n_: e.tensor_tensor(out=dsh[:, 0:n_, :], in0=dsh[:, 0:n_, :], in1=mug[:, sl_].unsqueeze(2).to_broadcast([64, n_, 64]), op=ALU.mult), reads=["dsh", "mug"], writes=["dsh"])
                        P.op("dve", lambda e, sl_=sl_, n_=n_: e.tensor_tensor(out=zbuf[:, sl_, 1:65], in0=zbuf[:, sl_, 1:65], in1=dsh[:, 0:n_, :], op=ALU.add), reads=["dsh", "zbuf"], writes=["zbuf"])
                else:
                    for q4 in range(5):
                        nb = HG if q4 < 4 else 2
                        sl_ = slice(q4 * HG, q4 * HG + nb)
                        n_ = nb
                        P.op("dve", lambda e, sl_=sl_, n_=n_: e.tensor_tensor(out=dsh[:, 0:n_, :], in0=zbuf[:, sl_, 0:64], in1=zbuf[:, sl_, 1:65], op=ALU.subtract), reads=["zbuf"], writes=["dsh"])
                        P.op("dve", lambda e, sl_=sl_, n_=n_: e.tensor_tensor(out=dsh[:, 0:n_, :], in0=dsh[:, 0:n_, :], in1=mug[:, sl_].unsqueeze(2).to_broadcast([64, n_, 64]), op=ALU.mult), reads=["dsh", "mug"], writes=["dsh"])
                        P.op("pool", lambda e, sl_=sl_: e.tensor_copy(out=zbuf[:, sl_, 0:1], in_=zbuf[:, sl_, 64:65]), reads=["zbuf", "dsh"], writes=["zbuf"])
                        P.op("dve", lambda e, sl_=sl_, n_=n_: e.tensor_tensor(out=zbuf[:, sl_, 1:65], in0=zbuf[:, sl_, 1:65], in1=dsh[:, 0:n_, :], op=ALU.add), reads=["dsh", "zbuf"], writes=["zbuf"])
                zr = zbuf[:, 0:HG, 1:65]; zk = zbuf[:, HG:2 * HG, 1:65]; zv = zbuf[:, 2 * HG:3 * HG, 1:65]
                P.op("pool", lambda e: e.tensor_copy(out=vb[:, :, :], in_=zv), reads=["zbuf"], writes=[n_vb])
                P.op("act", lambda e: e.activation(out=tw[:, :], in_=zbuf[:, 4 * HG, 1:65], func=AF.Tanh), reads=["zbuf"], writes=["tw"])
                P.op("act", lambda e: e.copy(out=adc[:, :], in_=zbuf[:, 4 * HG + 1, 1:65]), reads=["zbuf"], writes=["adc"])
                bu = nbank(); ba = nbank()
                for h in range(HG):
                    P.op("pe", lambda e, h=h, bu=bu: e.matmul(ps[0:64, bu, h * 64:(h + 1) * 64], lhsT=w2g[:, h * 64:(h + 1) * 64], rhs=tw[:, :], start=True, stop=True), reads=["w2g", "tw"], writes=[("ps", bu)])
                for h in range(HG):
                    P.op("pe", lambda e, h=h, ba=ba: e.matmul(ps[0:64, ba, h * 64:(h + 1) * 64], lhsT=a2g[:, h * 64:(h + 1) * 64], rhs=adc[:, :], start=True, stop=True), reads=["a2g", "adc"], writes=[("ps", ba)])
                v3 = lambda bk_: ps[0:64, bk_, 0:HG * 64].rearrange("p (h t) -> p h t", t=64)
                P.op("dve", lambda e, bu=bu: e.tensor_tensor(out=t1[:, :, :], in0=v3(bu), in1=bc(prm[:, :, 0]), op=ALU.add), reads=[("ps", bu), "prm"], writes=["t1"])
                P.op("act", lambda e: e.activation(out=sg[:, :, :], in_=t1[:, :, :], func=AF.Sigmoid), reads=["t1"], writes=["sg"])
                P.op("dve", lambda e, ba=ba: e.tensor_tensor(out=t2[:, :, :], in0=v3(ba), in1=bc(prm[:, :, 1]), op=ALU.add), reads=[("ps", ba), "prm"], writes=["t2"])
                P.op("act", lambda e: e.activation(out=aa[:, :, :], in_=t2[:, :, :], func=AF.Sigmoid), reads=["t2"], writes=["aa"])
                f2 = lambda a_: a_[:, :, :].rearrange("p h t -> p (h t)")
                P.op("dve", lambda e: e.tensor_tensor_scan(out=f2(clr), data0=segm[:, sgi, :], data1=f2(sg), initial=0.0, op0=ALU.mult, op1=ALU.add), reads=["sg", "segm"], writes=["clr"])
                P.op("act", lambda e: e.activation(out=Ein[:, :, :], in_=clr[:, :, :], func=AF.Exp, scale=LD), reads=["clr"], writes=["Ein"])
                P.op("act", lambda e: e.activation(out=Einv[:, :, :], in_=clr[:, :, :], func=AF.Exp, scale=-LD), reads=["clr"], writes=["Einv"])
                P.op("pool", lambda e: e.tensor_tensor(out=t1[:, :, :], in0=clr[:, :, :], in1=sg[:, :, :], op=ALU.subtract), reads=["clr", "sg", "t1"], writes=["t1"])
                P.op("act", lambda e: e.activation(out=Eex[:, :, :], in_=t1[:, :, :], func=AF.Exp, scale=LD), reads=["t1"], writes=["Eex"])
                seg4 = lambda a_: a_[:, :, :].rearrange("p h (s l) -> p h s l", l=L)
                P.op("pool", lambda e: e.tensor_tensor(out=seg4(t2), in0=seg4(clr)[:, :, :, L - 1:L].to_broadcast([64, HG, nseg, L]), in1=seg4(clr), op=ALU.subtract), reads=["clr", "t2"], writes=["t2"])
                P.op("act", lambda e: e.activation(out=EC[:, :, :], in_=t2[:, :, :], func=AF.Exp, scale=LD), reads=["t2"], writes=["EC"])
                P.op("act", lambda e: e.activation(out=GC[:, :, 0:nseg], in_=seg4(clr)[:, :, :, L - 1], func=AF.Exp, scale=LD), reads=["clr"], writes=[n_gc])
                P.op("act", lambda e: e.activation(out=sgate[:, :, :], in_=zbuf[:, 3 * HG:4 * HG, 1:65], func=AF.Silu), reads=["zbuf"], writes=[n_sgt])
                P.op("pool", lambda e: e.tensor_tensor(out=kk_[:, :, :], in0=zk, in1=bc(prm[:, :, 2]), op=ALU.mult), reads=["zbuf", "prm"], writes=["kk_"])
                P.op("act", lambda e: e.activation(out=t1[:, :, :], in_=kk_[:, :, :], func=AF.Square), reads=["kk_", "t1"], writes=["t1"])
                bs = nbank()
                P.op("pe", lambda e, bs=bs: e.matmul(ps[0:64, bs, 0:HG * 64], lhsT=ones64[:, :], rhs=f2(t1), start=True, stop=True), reads=["ones64", "t1"], writes=[("ps", bs)])
                P.op("act", lambda e, bs=bs: e.activation(out=t2[:, :, :], in_=v3(bs), func=AF.Sqrt, bias=eps24[0:64, 0:1], scale=1.0), reads=[("ps", bs), "eps24", "t2"], writes=["t2"])
                P.op("dve", lambda e: e.reciprocal(out=t2[:, :, :], in_=t2[:, :, :]), reads=["t2"], writes=["t2"])
                P.op("dve", lambda e: e.tensor_tensor(out=kkn[:, :, :], in0=kk_[:, :, :], in1=t2[:, :, :], op=ALU.mult), reads=["kk_", "t2"], writes=["kkn"])
                P.op("dve", lambda e: e.scalar_tensor_tensor(out=t1[:, :, :], in0=aa[:, :, :], scalar=-1.0, in1=bc(prm[:, :, 3]), op0=ALU.add, op1=ALU.mult), reads=["aa", "prm", "t1"], writes=["t1"])
                P.op("dve", lambda e: e.scalar_tensor_tensor(out=kmod[:, :, :], in0=t1[:, :, :], scalar=1.0, in1=zk, op0=ALU.add, op1=ALU.mult), reads=["t1", "zbuf"], writes=["kmod"])
                P.op("dve", lambda e: e.scalar_tensor_tensor(out=at_[:, :, :], in0=kkn[:, :, :], scalar=-1.0, in1=Eex[:, :, :], op0=ALU.mult, op1=ALU.mult), reads=["kkn", "Eex"], writes=[n_at])
                P.op("pool", lambda e: e.tensor_tensor(out=t2[:, :, :], in0=kkn[:, :, :], in1=aa[:, :, :], op=ALU.mult), reads=["kkn", "aa", "t2"], writes=["t2"])
                P.op("pool", lambda e: e.tensor_tensor(out=bt_[:, :, :], in0=t2[:, :, :], in1=Einv[:, :, :], op=ALU.mult), reads=["t2", "Einv"], writes=[n_bt])
                P.op("pool", lambda e: e.tensor_tensor(out=bh_[:, :, :], in0=t2[:, :, :], in1=EC[:, :, :], op=ALU.mult), reads=["t2", "EC"], writes=[n_bh])
                P.op("dve", lambda e: e.tensor_tensor(out=kt_[:, :, :], in0=kmod[:, :, :], in1=Einv[:, :, :], op=ALU.mult), reads=["kmod", "Einv"], writes=[n_kt])
                P.op("pool", lambda e: e.tensor_tensor(out=kh_[:, :, :], in0=kmod[:, :, :], in1=EC[:, :, :], op=ALU.mult), reads=["kmod", "EC"], writes=[n_kh])
                P.op("dve", lambda e: e.tensor_tensor(out=rt_[:, :, :], in0=zr, in1=Ein[:, :, :], op=ALU.mult), reads=["zbuf", "Ein"], writes=[n_rt])
                P.op("pool", lambda e: e.tensor_tensor(out=t1[:, :, :], in0=zr, in1=kmod[:, :, :], op=ALU.mult), reads=["zbuf", "kmod", "t1"], writes=["t1"])
                P.op("pool", lambda e: e.tensor_tensor(out=t1[:, :, :], in0=t1[:, :, :], in1=bc(prm[:, :, 4]), op=ALU.mult), reads=["t1", "prm"], writes=["t1"])
                bb_ = nbank()
                P.op("pe", lambda e, bb_=bb_: e.matmul(ps[0:64, bb_, 0:HG * 64], lhsT=ones64[:, :], rhs=f2(t1), start=True, stop=True), reads=["ones64", "t1"], writes=[("ps", bb_)])
                P.op("dve", lambda e, bb_=bb_: e.tensor_tensor(out=bon[:, :, :], in0=v3(bb_), in1=zv, op=ALU.mult), reads=[("ps", bb_), "zbuf"], writes=[n_bon])

            mmv = lambda bk_, rows: ps[0:rows, bk_, 0:HG * 64].rearrange("p (h t) -> p h t", t=64)

            def mm8(bk_, rows, cols, lhs_fn, rhs_fn, rd):
                for h in range(HG):
                    P.op("pe", lambda e, h=h: e.matmul(mmv(bk_, rows)[:, h, 0:cols], lhsT=lhs_fn(h), rhs=rhs_fn(h), start=True, stop=True), reads=rd, writes=[("ps", bk_)])

            def indep(c0, C, nsq, p, gci, pa=None, q2=0):
                cs = slice(c0, c0 + C)
                pa = p if pa is None else pa
                at_ = at2[pa]; rt_ = rt2[pa]; Vt = Vt2[p]; Bt = Bt2[p]; Kt = Kt2[p]; TT = TT2[p]; AakT = Aak2[p]; ArbT = Arb2[p]; ArkT = Ark2[p]; GC = GC2[pa]
                n_at = f"at_{pa}"; n_rt = f"rt_{pa}"
                bt_ = bt2[q2]; kt_ = kt2[q2]; bh_ = bh2[q2]; kh_ = kh2[q2]; vb = vb2[q2]
                n_bt = f"bt_{q2}"; n_kt = f"kt_{q2}"; n_bh = f"bh_{q2}"; n_kh = f"kh_{q2}"; n_vb = f"vb{q2}"
                for si, (src, dst, nm) in enumerate(((vb, Vt, f"Vt{p}"), (bh_, Bt, f"Bt{p}"), (kh_, Kt, f"Kt{p}"))):
                    hs, cs_ = (0, si * 64) if si < 2 else (4, 0)
                    for h in range(HG):
                        P.op("pe", lambda e, h=h, src=src, hs=hs, cs_=cs_: e.transpose(psb[0:C, hs + h, cs_:cs_ + 64], src[:, h, cs], ident_b[0:64, 0:64]), reads=[n_vb, n_bh, n_kh, "ident_b"], writes=["psb"])
                    P.op("act", lambda e, dst=dst, hs=hs, cs_=cs_: e.copy(out=dst[0:C, :, :], in_=psb[0:C, hs:hs + HG, cs_:cs_ + 64]), reads=["psb"], writes=[nm])
                A_ = lambda a_: (lambda h: a_[:, h, cs])
                mb = lambda m_: m_[0:C, 0:C].unsqueeze(1).to_broadcast([C, HG, C])
                bk = nbank(); mm8(bk, C, C, A_(bt_), A_(at_), [n_bt, n_at])
                P.op("dve", lambda e, bk=bk: e.tensor_tensor(out=NT0[0:C, :, 0:C], in0=mmv(bk, C)[:, :, 0:C], in1=mb(MU), op=ALU.mult), reads=[("ps", bk), "MU"], writes=["NT0"])
                bk = nbank(); mm8(bk, C, C, A_(at_), A_(bt_), [n_bt, n_at])
                P.op("dve", lambda e, bk=bk: e.tensor_tensor(out=N0[0:C, :, 0:C], in0=mmv(bk, C)[:, :, 0:C], in1=mb(ML), op=ALU.mult), reads=[("ps", bk), "ML"], writes=["N0"])
                for (lf, rf, dst, nm, msk, mn) in ((kt_, at_, AakT, f"AakT{p}", MU, "MU"), (bt_, rt_, ArbT, f"ArbT{p}", MUI, "MUI"), (kt_, rt_, ArkT, f"ArkT{p}", MUI, "MUI")):
                    bk = nbank(); mm8(bk, C, C, A_(lf), A_(rf), [n_kt, n_bt, n_at, n_rt])
                    P.op("dve", lambda e, bk=bk, dst=dst, msk=msk: e.tensor_tensor(out=dst[0:C, :, 0:C], in0=mmv(bk, C)[:, :, 0:C], in1=mb(msk), op=ALU.mult), reads=[("ps", bk), mn], writes=[nm])
                Rs = [(R0, "R0"), (R1, "R1")]
                Ns = [(N0, "N0"), (N1, "N1")]; NTs_ = [(NT0, "NT0"), (NT1, "NT1")]
                rdst0 = TT if nsq == 0 else R0
                P.op("pool", lambda e, rdst0=rdst0: e.tensor_tensor(out=rdst0[0:C, :, 0:C], in0=NT0[0:C, :, 0:C], in1=ident[0:C, 0:C].unsqueeze(1).to_broadcast([C, HG, C]), op=ALU.add),
                     reads=["NT0", "ident"], writes=[f"TT{p}" if nsq == 0 else "R0"])
                for k in range(1, nsq + 1):
                    (Nc, Ncn), (Nn, Nnn) = Ns[(k - 1) % 2], Ns[k % 2]
                    (NTc, NTcn), (NTn, NTnn) = NTs_[(k - 1) % 2], NTs_[k % 2]
                    (Rc, Rcn) = Rs[(k - 1) % 2]
                    (Rn, Rnn) = (TT, f"TT{p}") if k == nsq else Rs[k % 2]
                    bk = nbank()
                    mm8(bk, C, C, lambda h, NTc=NTc: NTc[0:C, h, 0:C], lambda h, Nc=Nc: Nc[0:C, h, 0:C], [NTcn, Ncn])
                    P.op("act", lambda e, bk=bk, Nn=Nn: e.copy(out=Nn[0:C, :, 0:C], in_=mmv(bk, C)[:, :, 0:C]), reads=[("ps", bk)], writes=[Nnn])
                    if k < nsq:
                        bk = nbank()
                        mm8(bk, C, C, lambda h, Nc=Nc: Nc[0:C, h, 0:C], lambda h, NTc=NTc: NTc[0:C, h, 0:C], [NTcn, Ncn])
                        P.op("act", lambda e, bk=bk, NTn=NTn: e.copy(out=NTn[0:C, :, 0:C], in_=mmv(bk, C)[:, :, 0:C]), reads=[("ps", bk)], writes=[NTnn])
                    bk = nbank()
                    mm8(bk, C, C, lambda h, Nn=Nn: Nn[0:C, h, 0:C], lambda h, Rc=Rc: Rc[0:C, h, 0:C], [Nnn, Rcn])
                    P.op("dve", lambda e, bk=bk, Rc=Rc, Rn=Rn: e.tensor_tensor(out=Rn[0:C, :, 0:C], in0=mmv(bk, C)[:, :, 0:C], in1=Rc[0:C, :, 0:C], op=ALU.add), reads=[("ps", bk), Rcn], writes=[Rnn])

            def dep(c0, C, p, pa=None, gci=0):
                cs = slice(c0, c0 + C)
                pa = p if pa is None else pa
                at_ = at2[pa]; rt_ = rt2[pa]; Vt = Vt2[p]; Bt = Bt2[p]; Kt = Kt2[p]; TT = TT2[p]; AakT = Aak2[p]; ArbT = Arb2[p]; ArkT = Ark2[p]; GC = GC2[pa]
                n_at = f"at_{pa}"; n_rt = f"rt_{pa}"
                bk = nbank()
                for h in range(HG):
                    P.op("pe", lambda e, h=h, bk=bk: e.matmul(mmv(bk, C)[:, h, :], lhsT=at_[:, h, cs], rhs=Pb[:, h, :], start=True, stop=False), reads=[n_at, "Pb"], writes=[("ps", bk)])
                    P.op("pe", lambda e, h=h, bk=bk: e.matmul(mmv(bk, C)[:, h, :], lhsT=AakT[0:C, h, 0:C], rhs=Vt[0:C, h, :], start=False, stop=True), reads=[f"AakT{p}", f"Vt{p}"], writes=[("ps", bk)])
                P.op("act", lambda e, bk=bk: e.copy(out=U0[0:C, :, :], in_=mmv(bk, C)), reads=[("ps", bk)], writes=["U0"])
                bk = nbank()
                mm8(bk, C, 64, lambda h: TT[0:C, h, 0:C], lambda h: U0[0:C, h, :], [f"TT{p}", "U0"])
                P.op("dve", lambda e, bk=bk: e.tensor_copy(out=Uf[0:C, :, :], in_=mmv(bk, C)), reads=[("ps", bk)], writes=["Uf"])
                bk = nbank(); bk2 = nbank()
                for h in range(HG):
                    P.op("pe", lambda e, h=h, bk=bk: e.matmul(mmv(bk, 64)[:, h, 0:C], lhsT=Pb[:, h, :], rhs=rt_[:, h, cs], start=True, stop=False), reads=["Pb", n_rt], writes=[("ps", bk)])
                    P.op("pe", lambda e, h=h, bk=bk: e.matmul(mmv(bk, 64)[:, h, 0:C], lhsT=Uf[0:C, h, :], rhs=ArbT[0:C, h, 0:C], start=False, stop=False), reads=["Uf", f"ArbT{p}"], writes=[("ps", bk)])
                    P.op("pe", lambda e, h=h, bk=bk: e.matmul(mmv(bk, 64)[:, h, 0:C], lhsT=Vt[0:C, h, :], rhs=ArkT[0:C, h, 0:C], start=False, stop=True), reads=[f"Vt{p}", f"ArkT{p}"], writes=[("ps", bk)])
                for h in range(HG):
                    P.op("pe", lambda e, h=h, bk2=bk2: e.matmul(mmv(bk2, 64)[:, h, :], lhsT=Bt[0:C, h, :], rhs=Uf[0:C, h, :], start=True, stop=False), reads=[f"Bt{p}", "Uf"], writes=[("ps", bk2)])
                    P.op("pe", lambda e, h=h, bk2=bk2: e.matmul(mmv(bk2, 64)[:, h, :], lhsT=Kt[0:C, h, :], rhs=Vt[0:C, h, :], start=False, stop=True), reads=[f"Kt{p}", f"Vt{p}"], writes=[("ps", bk2)])
                P.op("dve", lambda e: e.tensor_tensor(out=Pst[:, :, :], in0=Pst[:, :, :], in1=GC[:, :, gci:gci + 1].to_broadcast([64, HG, 64]), op=ALU.mult), reads=["Pst", f"GC{pa}"], writes=["Pst"])
                P.op("dve", lambda e, bk2=bk2: e.tensor_tensor(out=Pst[:, :, :], in0=Pst[:, :, :], in1=mmv(bk2, 64), op=ALU.add), reads=[("ps", bk2), "Pst"], writes=["Pst"])
                P.op("act", lambda e: e.copy(out=Pb[:, :, :], in_=Pst[:, :, :]), reads=["Pst"], writes=["Pb"])
                P.op("dve", lambda e, bk=bk: e.tensor_copy(out=outT[:, :, cs], in_=mmv(bk, 64)[:, :, 0:C]), reads=[("ps", bk)], writes=["outT"])

            def finish_batch(col0, p, G=G):
                bon = bon2[p]; sgate = sg2[p]
                f2 = lambda a_: a_[:, :, :].rearrange("p h t -> p (h t)")
                v3 = lambda bk_: ps[0:64, bk_, 0:HG * 64].rearrange("p (h t) -> p h t", t=64)
                b1 = nbank()
                P.op("pe", lambda e, b1=b1: e.matmul(ps[0:64, b1, 0:HG * 64], lhsT=ones64[:, :], rhs=f2(outT), start=True, stop=True), reads=["ones64", "outT"], writes=[("ps", b1)])
                P.op("dve", lambda e, b1=b1: e.scalar_tensor_tensor(out=f1[:, :, :], in0=v3(b1), scalar=-1.0 / 64, in1=outT[:, :, :], op0=ALU.mult, op1=ALU.add), reads=[("ps", b1), "outT", "f1"], writes=["f1"])
                P.op("act", lambda e: e.activation(out=f2_[:, :, :], in_=f1[:, :, :], func=AF.Square), reads=["f1", "f2_"], writes=["f2_"])
                b2 = nbank()
                P.op("pe", lambda e, b2=b2: e.matmul(ps[0:64, b2, 0:HG * 64], lhsT=ones64[:, :], rhs=f2(f2_), start=True, stop=True), reads=["ones64", "f2_"], writes=[("ps", b2)])
                P.op("act", lambda e, b2=b2: e.activation(out=f2_[:, :, :], in_=v3(b2), func=AF.Sqrt, bias=epsln[0:64, 0:1], scale=1.0 / 64), reads=[("ps", b2), "epsln", "f2_"], writes=["f2_"])
                P.op("dve", lambda e: e.reciprocal(out=f2_[:, :, :], in_=f2_[:, :, :]), reads=["f2_"], writes=["f2_"])
                P.op("dve", lambda e: e.tensor_tensor(out=f1[:, :, :], in0=f1[:, :, :], in1=f2_[:, :, :], op=ALU.mult), reads=["f1", "f2_"], writes=["f1"])
                P.op("pool", lambda e: e.tensor_tensor(out=f1[:, :, :], in0=f1[:, :, :], in1=bc(prm[:, :, 5]), op=ALU.mult), reads=["f1", "prm"], writes=["f1"])
                P.op("pool", lambda e: e.tensor_tensor(out=f1[:, :, :], in0=f1[:, :, :], in1=bc(prm[:, :, 6]), op=ALU.add), reads=["f1", "prm"], writes=["f1"])
                P.op("pool", lambda e: e.tensor_tensor(out=f1[:, :, :], in0=f1[:, :, :], in1=bon[:, :, :], op=ALU.add), reads=["f1", f"bon{p}"], writes=["f1"])
                ev = lambda a_: a_[:, :, :].rearrange("p (q e) t -> p q e t", e=2)
                P.op("dve", lambda e: e.tensor_tensor(out=oaT[0:64, (HG // 2) * G:(HG // 2) * (G + 1), col0:col0 + 64], in0=ev(f1)[:, :, 0, :], in1=ev(sgate)[:, :, 0, :], op=ALU.mult), reads=["f1", f"sgate{p}"], writes=["oaT"])
                P.op("dve", lambda e: e.tensor_tensor(out=oodd[:, :, :], in0=ev(f1)[:, :, 1, :], in1=ev(sgate)[:, :, 1, :], op=ALU.mult), reads=["f1", f"sgate{p}"], writes=["oodd"])
                P.dma("sp", lambda e: e.dma_start(out=oaT[64:128, (HG // 2) * G:(HG // 2) * (G + 1), col0:col0 + 64], in_=oodd[:, :, :]), reads=["oodd"], writes=["oaT"], semkey="oodd")

            def TB(c):
                bank_grp[0] = "tb"; token_batch(c * 64, 64, 1, c % 3, c % 2); bank_grp[0] = None

            def IND(c):
                bank_grp[0] = "ind"; indep(0, 64, 5, c % 2, 0, pa=c % 3, q2=c % 2); bank_grp[0] = None

            def DEPF(c):
                bank_grp[0] = "dep"; dep(0, 64, c % 2, pa=c % 3); finish_batch(c * 64, c % 3); bank_grp[0] = None

            if NCH > 0:
                TB(0); IND(0)
            if NCH > 1:
                TB(1)
            for ch in range(NCH):
                names = []
                if ch + 2 < NCH:
                    P.thread("tb"); TB(ch + 2); names.append("tb")
                if ch + 1 < NCH:
                    P.thread("ind"); IND(ch + 1); names.append("ind")
                P.thread("dep"); DEPF(ch); names.append("dep")
                P.merge(names)
            bk = nbank()
            for h in range(HG):
                P.op("pe", lambda e, h=h, bk=bk: e.transpose(ps[0:64, bk, h * 64:(h + 1) * 64], Pst[:, h, :], ident[0:64, 0:64]), reads=["Pst", "ident"], writes=[("ps", bk)])
            P.op("act", lambda e, bk=bk: e.copy(out=Ssb[:, :, :], in_=ps[0:64, bk, 0:HG * 64].rearrange("p (h t) -> p h t", t=64)), reads=[("ps", bk)], writes=["Ssb"])
            P.dma("sp", lambda e, G=G: e.dma_start(out=wkv_p[HG * G:HG * G + HG, :, :].rearrange("h i j -> i h j"), in_=Ssb[:, :, :]), reads=["Ssb"], store=True, semkey="wkvp")
            if NSEQ > 0:
                token_batch(2048, 4, 16, 0, 0)
            for bq in range(NSEQ):
                p = bq % 2
                indep(4 * bq, 4, 1, p, bq, pa=0)
                P.dma("sp", lambda e, bq=bq, G=G: e.dma_start(out=Ssb[:, :, :], in_=swkv_d[bq, HG * G:HG * G + HG, :, :].rearrange("h i j -> i h j")), writes=["Ssb"], semkey="ssb")
                bk = nbank()
                for h in range(HG):
                    P.op("pe", lambda e, h=h, bk=bk: e.transpose(ps[0:64, bk, h * 64:(h + 1) * 64], Ssb[:, h, :], ident[0:64, 0:64]), reads=["Ssb", "ident"], writes=[("ps", bk)])
                P.op("act", lambda e, bk=bk: e.copy(out=Pst[:, :, :], in_=ps[0:64, bk, 0:HG * 64].rearrange("p (h t) -> p h t", t=64)), reads=[("ps", bk)], writes=["Pst"])
                P.op("act", lambda e: e.copy(out=Pb[:, :, :], in_=Pst[:, :, :]), reads=["Pst"], writes=["Pb"])
                dep(4 * bq, 4, p, pa=0, gci=bq)
                bk = nbank()
                for h in range(HG):
                    P.op("pe", lambda e, h=h, bk=bk: e.transpose(ps[0:64, bk, h * 64:(h + 1) * 64], Pst[:, h, :], ident[0:64, 0:64]), reads=["Pst", "ident"], writes=[("ps", bk)])
                P.op("act", lambda e, bk=bk: e.copy(out=Ssb[:, :, :], in_=ps[0:64, bk, 0:HG * 64].rearrange("p (h t) -> p h t", t=64)), reads=[("ps", bk)], writes=["Ssb"])
                P.dma("sp", lambda e, bq=bq, G=G: e.dma_start(out=wkv_s[bq, HG * G:HG * G + HG, :, :].rearrange("h i j -> i h j"), in_=Ssb[:, :, :]), reads=["Ssb"], store=True, semkey="wkvs")
            if NSEQ > 0:
                finish_batch(2048, 0)
        if dbg_oa is not None:
            P.dma("sp", lambda e: e.dma_start(out=dbg_oa, in_=oaT[:, :, :]), reads=["oaT"], store=True)
        A.release(mR); P.barrier()
        sgm = A.alloc([128, 2, 512]); mtmp = A.alloc([128, 512])

        def merge_branch(srcT, srck, wproj, gcol0, first, sgm, mtmp):
            for cb in range(8):
                slg = load_w([(0, gcol0 + cb * 128, 128)], 128)
                slp = load_w([(0, cb * 128, 128)], 128, src=wproj)
                for ni, (n0, nn) in enumerate(NTL):
                    bg = nbank(); bp = nbank(); ssl = ni % 2
                    proj_fm(slg, 128, n0, nn, bg)
                    P.op("act", lambda e, bg=bg, nn=nn, ssl=ssl, sgm=sgm: e.activation(out=sgm[:, ssl, 0:nn], in_=ps[:, bg, 0:nn], func=AF.Sigmoid), reads=[("ps", bg)], writes=[("sgm", ssl)])
                    for kc in range(8):
                        P.op("pe", lambda e, kc=kc, bp=bp, slp=slp, n0=n0, nn=nn: e.matmul(ps[:, bp, 0:nn], lhsT=wb[:, slp, kc, 0:128], rhs=srcT[:, kc, n0:n0 + nn], start=(kc == 0), stop=(kc == 7)),
                             reads=[("wb", slp)] + srck, writes=[("ps", bp)])
                    if first:
                        P.op("dve", lambda e, bp=bp, nn=nn, ssl=ssl, cb=cb, n0=n0, sgm=sgm: e.tensor_tensor(out=mT[:, cb, n0:n0 + nn], in0=ps[:, bp, 0:nn], in1=sgm[:, ssl, 0:nn], op=ALU.mult),
                             reads=[("ps", bp), ("sgm", ssl)], writes=["mT"])
                    else:
                        P.op("dve", lambda e, bp=bp, nn=nn, ssl=ssl, sgm=sgm, mtmp=mtmp: e.tensor_tensor(out=mtmp[:, 0:nn], in0=ps[:, bp, 0:nn], in1=sgm[:, ssl, 0:nn], op=ALU.mult),
                             reads=[("ps", bp), ("sgm", ssl)], writes=["mtmp"])
                        P.op("dve", lambda e, nn=nn, cb=cb, n0=n0, mtmp=mtmp: e.tensor_tensor(out=mT[:, cb, n0:n0 + nn], in0=mT[:, cb, n0:n0 + nn], in1=mtmp[:, 0:nn], op=ALU.add),
                             reads=["mtmp", "mT"], writes=["mT"])

        merge_branch(oaT, ["oaT"], w_pa, C_GA, True, sgm, mtmp)
        A.release(mA); P.barrier()
        if STAGE < 3:
            return finish()
        KTd = A.alloc([128, 2, NT], BF16)
        Vaug = A.alloc([128, 17, 2, 66], BF16)
        wi_t = A.alloc([128, 17, 8])
        m1 = A.mark()
        knT = A.alloc([128, NT]); sq = A.alloc([128, 512]); rs = A.alloc([128, 512])
        otok = A.alloc([128, 2, 128]); vtok = A.alloc([128, 2, 128]); kitok = A.alloc([128, 2, 64])
        xl = A.alloc([128, 8, 17], BF16); shrow = A.alloc([17, 2, 128])

        cur = {"sq": sq, "rs": rs}

        def normed_block(parts, dst_fn, scale_ap, key):
            sq, rs = cur["sq"], cur["rs"]
            sl = load_w(parts, 128)
            for (n0, nn) in NTL:
                b = nbank()
                proj_fm(sl, 128, n0, nn, b)
                P.op("act", lambda e, b=b, nn=nn, sq=sq: e.activation(out=sq[:, 0:nn], in_=ps[:, b, 0:nn], func=AF.Square), reads=[("ps", b)], writes=["sq"])
                b2 = nbank()
                P.op("pe", lambda e, b2=b2, nn=nn, sq=sq: e.matmul(ps[:, b2, 0:nn], lhsT=bones[:, :], rhs=sq[:, 0:nn], start=True, stop=True), reads=["bones", "sq"], writes=[("ps", b2)])
                P.op("act", lambda e, b2=b2, nn=nn, rs=rs: e.activation(out=rs[:, 0:nn], in_=ps[:, b2, 0:nn], func=AF.Sqrt, bias=eps6[:, 0:1], scale=1.0 / 64), reads=[("ps", b2), "eps6"], writes=["rs"])
                P.op("dve", lambda e, nn=nn, rs=rs: e.reciprocal(out=rs[:, 0:nn], in_=rs[:, 0:nn]), reads=["rs"], writes=["rs"])
                P.op("dve", lambda e, b=b, n0=n0, nn=nn, rs=rs: e.scalar_tensor_tensor(out=dst_fn(n0, nn), in0=ps[:, b, 0:nn], scalar=scale_ap, in1=rs[:, 0:nn], op0=ALU.mult, op1=ALU.mult), reads=[("ps", b), "rs"], writes=[key])

        normed_block([(0, C_AK, 128)], lambda n0, nn: knT[:, n0:n0 + nn], gk2[:, 0:1], "knT")
        for kvh in range(2):
            normed_block([(0, C_AK + kvh * 64, 64), (64, C_AK + kvh * 64, 64)], lambda n0, nn, kvh=kvh: KTd[:, kvh, n0:n0 + nn], gkq[:, 0:1], ("KTd", kvh))
        for ti, (t0, n) in enumerate(TT):
            b = nbank(); sl = ti % 2
            P.op("pe", lambda e, b=b, t0=t0, n=n: e.transpose(ps[0:n, b, 0:128], knT[:, t0:t0 + n], ident[:, :]), reads=["knT", "ident"], writes=[("ps", b)])
            P.op("act", lambda e, b=b, n=n, sl=sl: e.copy(out=otok[0:n, sl, :], in_=ps[0:n, b, 0:128]), reads=[("ps", b)], writes=[("otok", sl)])
            dst = k_p[t0:t0 + n, :] if ti < 16 else k_s[:, :]
            P.dma("sp", lambda e, dst=dst, n=n, sl=sl: e.dma_start(out=dst, in_=otok[0:n, sl, :]), reads=[("otok", sl)], store=True)
        P.op("dve", lambda e: e.memset(Vaug[:, :, :, :], 1.0), writes=[("Vaug", ti) for ti in range(17)])
        sl_v = load_w([(0, C_AV, 128)], 128)
        for ti, (t0, n) in enumerate(TT):
            b = nbank(); sl = ti % 2
            proj_tm(sl_v, 128, t0, n, b)
            P.op("act", lambda e, b=b, n=n, sl=sl: e.copy(out=vtok[0:n, sl, :], in_=ps[0:n, b, 0:128]), reads=[("ps", b)], writes=[("vtok", sl)])
            P.op("act", lambda e, b=b, n=n, ti=ti: e.copy(out=Vaug[0:n, ti, :, 0:64], in_=ps[0:n, b, 0:128].rearrange("p (k d) -> p k d", k=2)), reads=[("ps", b)], writes=[("Vaug", ti)])
            dst = v_p[t0:t0 + n, :] if ti < 16 else v_s[:, :]
            P.dma("sp", lambda e, dst=dst, n=n, sl=sl: e.dma_start(out=dst, in_=vtok[0:n, sl, :]), reads=[("vtok", sl)], store=True)
        sl_k = load_w([(0, C_KI, 72)], 72)
        for ti, (t0, n) in enumerate(TT):
            b = nbank(); sl = ti % 2
            proj_tm(sl_k, 72, t0, n, b)
            P.op("act", lambda e, b=b, n=n, sl=sl: e.copy(out=kitok[0:n, sl, :], in_=ps[0:n, b, 0:64]), reads=[("ps", b)], writes=[("kitok", sl)])
            P.op("act", lambda e, b=b, n=n, ti=ti: e.copy(out=wi_t[0:n, ti, :], in_=ps[0:n, b, 64:72]), reads=[("ps", b)], writes=[("wi_t", ti)])
            dst = ki_p[t0:t0 + n, :] if ti < 16 else ki_s[:, :]
            P.dma("sp", lambda e, dst=dst, n=n, sl=sl: e.dma_start(out=dst, in_=kitok[0:n, sl, :]), reads=[("kitok", sl)], store=True)
        P.op("dve", lambda e: e.tensor_copy(out=xl[:, :, 0:1], in_=xnT[:, :, 2047:2048]), reads=["xnT"], writes=["xl"])
        P.op("dve", lambda e: e.tensor_copy(out=xl[:, :, 1:17], in_=xnT[:, :, 2048:2112].rearrange("p k (b t) -> p k b t", t=4)[:, :, :, 3]), reads=["xnT"], writes=["xl"])
        for cb in range(33):
            sl = load_w([(0, cb * 128, 128)], 128)
            b = nbank(); ssl = cb % 2
            for kc in range(8):
                P.op("pe", lambda e, kc=kc, sl=sl, b=b: e.matmul(ps[0:17, b, 0:128], lhsT=xl[:, kc, :], rhs=wb[:, sl, kc, 0:128], start=(kc == 0), stop=(kc == 7)),
                     reads=[("wb", sl), "xl"], writes=[("ps", b)])
            P.op("act", lambda e, b=b, ssl=ssl: e.copy(out=shrow[:, ssl, :], in_=ps[0:17, b, 0:128]), reads=[("ps", b)], writes=[("shrow", ssl)])
            P.dma("sp", lambda e, cb=cb, ssl=ssl: e.dma_start(out=sh_p[:, cb * 128:(cb + 1) * 128], in_=shrow[0:1, ssl, :]), reads=[("shrow", ssl)], store=True, semkey=("shp", ssl))
            P.dma("sp", lambda e, cb=cb, ssl=ssl: e.dma_start(out=sh_s[:, cb * 128:(cb + 1) * 128], in_=shrow[1:17, ssl, :]), reads=[("shrow", ssl)], store=True, semkey=("shs", ssl))
        A.release(m1); P.barrier()
        if STAGE < 6:
            return finish()

        m2 = A.mark()
        qT = A.alloc([128, 8, NT], BF16)
        obT = qT
        QK = [("qT", p) for p in range(8)]
        mq = A.mark()
        cur["sq"] = A.alloc([128, 512]); cur["rs"] = A.alloc([128, 512])
        for p8 in range(8):
            normed_block([(0, C_Q + p8 * 128, 128)], lambda n0, nn, p8=p8: qT[:, p8, n0:n0 + nn], 1.0, ("qT", p8))
        A.release(mq); P.barrier()
        qiTs = A.alloc([128, 4, 64], BF16); kiT2s = A.alloc([128, 64], BF16)
        relb = A.alloc([32, 16]); rb31 = A.alloc([32, 1, 16]); rbd = A.alloc([32, 16]); rbt = A.alloc([32, 16])
        rbh = A.alloc([32, 16], BF16); rbl = A.alloc([32, 16], BF16)
        negI = A.alloc([128, 128], BF16)
        m3 = A.mark()
        qiT = A.alloc([128, 4, NT], BF16)
        for p4 in range(4):
            sl = load_w([(0, C_QI + p4 * 128, 128)], 128)
            for (n0, nn) in NTL:
                b = nbank(); proj_fm(sl, 128, n0, nn, b)
                P.op("act", lambda e, b=b, p4=p4, n0=n0, nn=nn: e.copy(out=qiT[:, p4, n0:n0 + nn], in_=ps[:, b, 0:nn]), reads=[("ps", b)], writes=[("qiT", p4)])
        kiT2 = A.alloc([128, NT], BF16)
        sl = load_w([(0, C_KI, 64), (64, C_KI, 64)], 128)
        for (n0, nn) in NTL:
            b = nbank(); proj_fm(sl, 128, n0, nn, b)
            P.op("act", lambda e, b=b, n0=n0, nn=nn: e.copy(out=kiT2[:, n0:n0 + nn], in_=ps[:, b, 0:nn]), reads=[("ps", b)], writes=["kiT2"])
        P.op("act", lambda e: e.copy(out=qiTs[:, :, :], in_=qiT[:, :, 2048:2112]), reads=[("qiT", p) for p in range(4)], writes=["qiTs"])
        P.op("act", lambda e: e.copy(out=kiT2s[:, :], in_=kiT2[:, 2048:2112]), reads=["kiT2"], writes=["kiT2s"])
        P.dma("sp", lambda e: e.dma_start(out=relb[:, :], in_=rel_bias_d), writes=["relb"])
        P.dma("sp", lambda e: e.dma_start(out=rb31[:, :, :], in_=rel_bias_d[31:32, :].partition_broadcast(32)), writes=["rb31"])
        P.op("dve", lambda e: e.tensor_tensor(out=rbd[:, :], in0=relb[:, :], in1=rb31[:, 0, :], op=ALU.subtract), reads=["relb", "rb31"], writes=["rbd"])
        P.op("dve", lambda e: e.tensor_copy(out=rbh[:, :], in_=rbd[:, :]), reads=["rbd"], writes=["rbh"])
        P.op("dve", lambda e: e.tensor_copy(out=rbt[:, :], in_=rbh[:, :]), reads=["rbh"], writes=["rbt"])
        P.op("dve", lambda e: e.tensor_tensor(out=rbl[:, :], in0=rbd[:, :], in1=rbt[:, :], op=ALU.subtract), reads=["rbd", "rbt"], writes=["rbl"])
        TzT = A.alloc([128, 2, 128, 16])
        caus0 = A.alloc([128, 128]); causS = A.alloc([128, 128])
        acc = A.alloc([128, 2048]); work = A.alloc([128, 2048]); relu_t = A.alloc([128, 2, 512])
        ns2 = [A.alloc([128, 2048], BF16), A.alloc([128, 2048], BF16)]; mx = A.alloc([128, 8])
        PT = A.alloc([128, 2, 8, 128], BF16); tmpl = A.alloc([128, 8, 128])
        obn = A.alloc([128, 16, 64], BF16); rec = A.alloc([128, 16])
        ohst = work.bitcast(BF16)[0:32, 0:4096].rearrange("p (s q k) -> p s q k", s=2, q=16)
        P.dma("sp", lambda e: e.dma_start(out=caus0[:, :], in_=caus0_d), writes=["caus0"])
        P.dma("sp", lambda e: e.dma_start(out=causS[:, :], in_=causS_d), writes=["causS"])
        P.op("act", lambda e: e.mul(out=negI[:, :], in_=ident[:, :], mul=-30000.0), reads=["ident"], writes=["negI"])
        for ty in range(2):
            for qc in range(8):
                slot = (ty * 8 + qc) % 2
                bank = 3 + slot
                P.dma("sp", lambda e, ty=ty, qc=qc, slot=slot: e.dma_start(out=ohst[:, slot, :, :], in_=oh_d[:, ty, qc * 16:(qc + 1) * 16, :]), writes=["work"], semkey=("oh", slot))
                for ql in range(16):
                    P.op("pe", lambda e, slot=slot, bank=bank, ql=ql: e.matmul(ps[:, bank, ql * 16:(ql + 1) * 16], lhsT=ohst[:, slot, ql, :], rhs=rbh[:, :], start=True, stop=False),
                         reads=["work", "rbh"], writes=[("ps", bank)])
                    P.op("pe", lambda e, slot=slot, bank=bank, ql=ql: e.matmul(ps[:, bank, ql * 16:(ql + 1) * 16], lhsT=ohst[:, slot, ql, :], rhs=rbl[:, :], start=False, stop=True),
                         reads=["work", "rbl"], writes=[("ps", bank)])
                if ty == 0:
                    P.op("dve", lambda e, bank=bank, qc=qc: e.tensor_tensor(out=TzT[:, 0, qc * 16:(qc + 1) * 16, :], in0=ps[:, bank, 0:256].rearrange("s (q h) -> s q h", h=16),
                                                                            in1=caus0[:, qc * 16:(qc + 1) * 16].unsqueeze(2).to_broadcast([128, 16, 16]), op=ALU.add),
                         reads=[("ps", bank), "caus0"], writes=["TzT"])
                else:
                    P.op("dve", lambda e, bank=bank, qc=qc: e.tensor_copy(out=TzT[:, 1, qc * 16:(qc + 1) * 16, :], in_=ps[:, bank, 0:256].rearrange("s (q h) -> s q h", h=16)),
                         reads=[("ps", bank)], writes=["TzT"])

        NQB = int(os.environ.get("NQB", "16"))
        PV_FIRST = (0, 7, 14)
        icnt = [0]
        def idx_topk(i):
            q0 = i * 128
            notsel = ns2[i % 2]; nsn = f"notsel{i % 2}"
            nk = 128 * (i + 1)
            for c0 in range(0, nk, 512):
                w = min(512, nk - c0)
                ck = ("acc", c0 // 512)
                for h in range(8):
                    half, p4 = h % 2, h // 2
                    bank = 3 + (icnt[0] % 4); rsl = icnt[0] % 2; icnt[0] += 1
                    P.op("pe", lambda e, bank=bank, w=w, half=half, p4=p4, q0=q0, c0=c0: e.matmul(ps[:, bank, 0:w], lhsT=qiT[half * 64:(half + 1) * 64, p4, q0:q0 + 128],
                                                                                              rhs=kiT2[half * 64:(half + 1) * 64, c0:c0 + w], start=True, stop=True),
                         reads=[("qiT", p4), "kiT2"], writes=[("ps", bank)])
                    P.op("act", lambda e, bank=bank, w=w, rsl=rsl: e.activation(out=relu_t[:, rsl, 0:w], in_=ps[:, bank, 0:w], func=AF.Relu), reads=[("ps", bank)], writes=[("relu_t", rsl)])
                    if h == 0:
                        P.op("dve", lambda e, w=w, rsl=rsl, c0=c0, i=i: e.tensor_scalar(out=acc[:, c0:c0 + w], in0=relu_t[:, rsl, 0:w], scalar1=wi_t[:, i, 0:1], scalar2=None, op0=ALU.mult),
                             reads=[("relu_t", rsl), ("wi_t", i)], writes=[ck])
                    else:
                        P.op("dve", lambda e, w=w, rsl=rsl, c0=c0, i=i, h=h: e.scalar_tensor_tensor(out=acc[:, c0:c0 + w], in0=relu_t[:, rsl, 0:w], scalar=wi_t[:, i, h:h + 1], in1=acc[:, c0:c0 + w],
                                                                                                 op0=ALU.mult, op1=ALU.add),
                             reads=[("relu_t", rsl), ("wi_t", i), ck], writes=[ck])
            acck = [("acc", c) for c in range((nk + 511) // 512)]
            P.op("dve", lambda e, i=i: e.tensor_tensor(out=acc[:, i * 128:(i + 1) * 128], in0=acc[:, i * 128:(i + 1) * 128], in1=causS[:, :], op=ALU.add),
                 reads=acck + ["causS"], writes=acck)
            masked = i >= 2
            if masked:
                for r in range(32):
                    src = acc if r == 0 else work
                    P.op("dve", lambda e, src=src, nk=nk: e.max(out=mx[:, :], in_=src[:, 0:nk]), reads=acck + ["work"], writes=["mx"])
                    if r < 31:
                        P.op("dve", lambda e, src=src, nk=nk: e.match_replace(out=work[:, 0:nk], in_to_replace=mx[:, :], in_values=src[:, 0:nk], imm_value=-1e30),
                             reads=acck + ["mx", "work"], writes=["work"])
                P.op("dve", lambda e, nk=nk: e.tensor_scalar(out=notsel[:, 0:nk], in0=acc[:, 0:nk], scalar1=mx[:, 7:8], scalar2=None, op0=ALU.is_lt),
                     reads=acck + ["mx"], writes=[nsn])
        def attend(i):
            q0 = i * 128
            masked = i >= 2
            notsel = ns2[i % 2]; nsn = f"notsel{i % 2}"
            for j in range(i + 1):
                ty = i - j
                for g in range(2):
                    slot = g
                    bA, bB = (3, 4) if slot == 0 else (5, 6)
                    for hh in range(8):
                        half, p4 = hh // 4, hh % 4
                        pair = g * 4 + p4
                        bank = bA if half == 0 else bB
                        P.op("pe", lambda e, bank=bank, p4=p4, half=half, g=g, j=j, pair=pair, q0=q0, masked=masked: e.matmul(
                            ps[:, bank, p4 * 128:(p4 + 1) * 128], lhsT=KTd[half * 64:(half + 1) * 64, g, j * 128:(j + 1) * 128],
                            rhs=qT[half * 64:(half + 1) * 64, pair, q0:q0 + 128], start=True, stop=(not masked)),
                            reads=[("KTd", g), ("qT", pair)], writes=[("ps", bank)])
                        if masked:
                            P.op("pe", lambda e, bank=bank, p4=p4, j=j: e.matmul(ps[:, bank, p4 * 128:(p4 + 1) * 128], lhsT=notsel[:, j * 128:(j + 1) * 128], rhs=negI[:, :], start=False, stop=True),
                                 reads=[nsn, "negI"], writes=[("ps", bank)])
                    psv = ps[:, bA:bB + 1, :].rearrange("s e (p q) -> s e p q", q=128)
                    if ty <= 1:
                        P.op("dve", lambda e, psv=psv, ty=ty, g=g: e.tensor_tensor(out=tmpl[:, :, :].rearrange("s (e p) q -> s e p q", e=2), in0=psv,
                                                                                  in1=TzT[:, ty, :, 8 * g:8 * g + 8].rearrange("s q (p e) -> s e p q", e=2), op=ALU.add),
                             reads=[("ps", bA), ("ps", bB), "TzT"], writes=["tmpl"])
                        P.op("act", lambda e, slot=slot: e.activation(out=PT[:, slot, :, :], in_=tmpl[:, :, :], func=AF.Exp), reads=["tmpl"], writes=[("PT", slot)])
                    else:
                        P.op("act", lambda e, slot=slot, psv=psv: e.activation(out=PT[:, slot, :, :].rearrange("s (e p) q -> s e p q", e=2), in_=psv, func=AF.Exp),
                             reads=[("ps", bA), ("ps", bB)], writes=[("PT", slot)])
                    for hh in range(8):
                        head = 2 * (g * 4 + hh % 4) + hh // 4
                        bpv, col = head // 7, (head % 7) * 66
                        first = (j == 0 and head in PV_FIRST)
                        P.op("pe", lambda e, slot=slot, hh=hh, bpv=bpv, col=col, j=j, g=g, first=first, i=i: e.matmul(
                            ps[:, bpv, col:col + 65], lhsT=PT[:, slot, hh, :], rhs=Vaug[:, j, g, 0:65], start=first, stop=(j == i)),
                            reads=[("PT", slot), ("Vaug", j)], writes=[("ps", bpv)])
            for bk in range(3):
                nh = 7 if bk < 2 else 2
                P.op("dve", lambda e, bk=bk, nh=nh: e.reciprocal(out=rec[:, bk * 7:bk * 7 + nh], in_=ps[:, bk, 0:nh * 66].rearrange("q (h c) -> q h c", c=66)[:, :, 64]),
                     reads=[("ps", bk)], writes=["rec"])
                P.op("dve", lambda e, bk=bk, nh=nh: e.tensor_tensor(out=obn[:, bk * 7:bk * 7 + nh, :], in0=ps[:, bk, 0:nh * 66].rearrange("q (h c) -> q h c", c=66)[:, :, 0:64],
                                                                  in1=rec[:, bk * 7:bk * 7 + nh].unsqueeze(2).to_broadcast([128, nh, 64]), op=ALU.mult),
                     reads=[("ps", bk), "rec"], writes=["obn"])
            for p8 in range(8):
                P.op("pe", lambda e, p8=p8: e.transpose(psb[:, p8, :], obn[:, 2 * p8:2 * p8 + 2, :].rearrange("q h d -> q (h d)"), ident_b[:, :]), reads=["obn", "ident_b"], writes=["psb"])
            P.op("act", lambda e, q0=q0: e.copy(out=obT[:, :, q0:q0 + 128], in_=psb[:, :, :]), reads=["psb"], writes=QK)

        if NQB > 0:
            idx_topk(0)
        for i in range(NQB):
            if i + 1 < NQB:
                idx_topk(i + 1)
            attend(i)
        A.release(m3); P.barrier()
        NSB = int(os.environ.get("NSB", "16"))
        ptt = A.alloc([128, 16], I32); cmt = A.alloc([128, 1], I32); idx = A.alloc([128, 16], I32)
        score_all = A.alloc([64, 2176]); work_s = A.alloc([64, 2176]); nots = A.alloc([128, 2176], BF16); mxs = A.alloc([64, 8])
        masks = A.alloc([64, 64]); vm = A.alloc([64, 16, 4]); selh_f = A.alloc([128, 16, 32]); selh = A.alloc([128, 16, 32], BF16)
        wis = A.alloc([4, 16, 8])
        TzS = A.alloc([128, 16, 4, 16]); NBb = A.alloc([64, 4, 16])
        P.dma("sp", lambda e: e.dma_start(out=ptt[:, :], in_=ptrep_d), writes=["ptt"])
        P.dma("sp", lambda e: e.dma_start(out=cmt[:, :], in_=cmod_d), writes=["cmt"])
        P.dma("sp", lambda e: e.dma_start(out=masks[:, :], in_=masks_d), writes=["masks"])
        P.dma("sp", lambda e: e.dma_start(out=vm[:, :, :], in_=vm_d), writes=["vm"])
        P.dma("sp", lambda e: e.dma_start(out=selh_f[:, :, :], in_=selh_d), writes=["selh_f"])
        P.op("act", lambda e: e.copy(out=selh[:, :, :], in_=selh_f[:, :, :]), reads=["selh_f"], writes=["selh"])
        P.op("dve", lambda e: e.tensor_scalar(out=idx[:, :], in0=ptt[:, :], scalar1=8, scalar2=cmt[:, 0:1], op0=ALU.mult, op1=ALU.add), reads=["ptt", "cmt"], writes=["idx"])
        P.op("dve", lambda e: e.memset(nots[:, :], 0.0), writes=["nots"])
        for bq in range(NSB):
            P.dma("sp", lambda e, bq=bq: e.dma_start(out=wis[:, bq, :], in_=wi_t[4 * bq:4 * bq + 4, 16, :]), reads=[("wi_t", 16)], writes=["wis"], semkey="wis")
        m4 = A.mark()
        ohs_t = A.alloc([32, 2, 4, 4, 128], BF16) if False else A.alloc([32, 2, 2048], BF16)
        ohn_t = A.alloc([32, 4, 64], BF16)
        P.dma("sp", lambda e: e.dma_start(out=ohn_t[:, :, :], in_=ohn_d), writes=["ohn_t"])
        for uc in range(4):
            slot = uc % 2; bank = 3 + slot
            P.dma("sp", lambda e, uc=uc, slot=slot: e.dma_start(out=ohs_t[:, slot, :], in_=ohs_d[:, uc * 4:(uc + 1) * 4, :, :].rearrange("k u t p -> k (u t p)")), writes=[("ohs_t", slot)])
            for ut in range(16):
                for (rr, st_, sp_) in ((rbh, True, False), (rbl, False, True)):
                    P.op("pe", lambda e, slot=slot, bank=bank, ut=ut, rr=rr, st_=st_, sp_=sp_: e.matmul(ps[:, bank, ut * 16:(ut + 1) * 16], lhsT=ohs_t[:, slot, ut * 128:(ut + 1) * 128], rhs=rr[:, :], start=st_, stop=sp_),
                         reads=[("ohs_t", slot), "rbh", "rbl"], writes=[("ps", bank)])
            P.op("dve", lambda e, bank=bank, uc=uc: e.tensor_copy(out=TzS[:, uc * 4:(uc + 1) * 4, :, :], in_=ps[:, bank, 0:256].rearrange("p (u t h) -> p u t h", u=4, t=4)), reads=[("ps", bank)], writes=["TzS"])
        for t in range(4):
            for (rr, st_, sp_) in ((rbh, True, False), (rbl, False, True)):
                P.op("pe", lambda e, t=t, rr=rr, st_=st_, sp_=sp_: e.matmul(ps[0:64, 5, t * 16:(t + 1) * 16], lhsT=ohn_t[:, t, :], rhs=rr[:, :], start=st_, stop=sp_),
                     reads=["ohn_t", "rbh", "rbl"], writes=[("ps", 5)])
        P.op("dve", lambda e: e.tensor_copy(out=NBb[:, :, :], in_=ps[0:64, 5, 0:64].rearrange("p (t h) -> p t h", t=4)), reads=[("ps", 5)], writes=["NBb"])
        A.release(m4); P.barrier()
        KIb = A.alloc([128, 16, 64]); KId = A.alloc([128, 16, 2, 64], BF16); kiTs = A.alloc([128, 16, 128], BF16)
        accs = A.alloc([4, 2176]); relus = A.alloc([4, 2, 512])
        cki = cache_ki
        for bq in range(NSB):
            P.dma("pool", lambda e, bq=bq: e.indirect_dma_start(out=KIb[:, :, :].rearrange("p u d -> p (u d)"), out_offset=None, in_=cki,
                                                               in_offset=bass.IndirectOffsetOnAxis(ap=idx[:, bq:bq + 1], axis=0)), reads=["idx"], writes=["KIb"])
            P.op("act", lambda e: e.copy(out=KId[:, :, :, :], in_=KIb[:, :, :].unsqueeze(2).to_broadcast([128, 16, 2, 64])), reads=["KIb"], writes=["KId"])
            for ub in range(2):
                for u8 in range(8):
                    u = ub * 8 + u8
                    P.op("pe", lambda e, u=u, u8=u8: e.transpose(psb[:, u8, :], KId[:, u, :, :].rearrange("p a d -> p (a d)"), ident_b[:, :]), reads=["KId", "ident_b"], writes=["psb"])
                P.op("act", lambda e, ub=ub: e.copy(out=kiTs[:, ub * 8:(ub + 1) * 8, :], in_=psb[:, :, :]), reads=["psb"], writes=["kiTs"])
            for c in range(5):
                w = 512 if c < 4 else 64
                for h in range(8):
                    half, p4 = h % 2, h // 2
                    bank = 3 + (icnt[0] % 4); rsl = icnt[0] % 2; icnt[0] += 1
                    rhs_fn = (lambda c=c, half=half: kiTs[half * 64:(half + 1) * 64, 4 * c:4 * c + 4, :].rearrange("p u k -> p (u k)")) if c < 4 else (lambda half=half: kiT2s[half * 64:(half + 1) * 64, :])
                    P.op("pe", lambda e, bank=bank, w=w, half=half, p4=p4, bq=bq, rhs_fn=rhs_fn: e.matmul(ps[0:4, bank, 0:w], lhsT=qiTs[half * 64:(half + 1) * 64, p4, 4 * bq:4 * bq + 4], rhs=rhs_fn(), start=True, stop=True),
                         reads=["qiTs", "kiTs", "kiT2s"], writes=[("ps", bank)])
                    P.op("act", lambda e, bank=bank, w=w, rsl=rsl: e.activation(out=relus[:, rsl, 0:w], in_=ps[0:4, bank, 0:w], func=AF.Relu), reads=[("ps", bank)], writes=[("relus", rsl)])
                    if h == 0:
                        P.op("dve", lambda e, w=w, rsl=rsl, c=c, bq=bq: e.tensor_scalar(out=accs[:, c * 512:c * 512 + w], in0=relus[:, rsl, 0:w], scalar1=wis[:, bq, 0:1], scalar2=None, op0=ALU.mult),
                             reads=[("relus", rsl), "wis"], writes=["accs"])
                    else:
                        P.op("dve", lambda e, w=w, rsl=rsl, c=c, bq=bq, h=h: e.scalar_tensor_tensor(out=accs[:, c * 512:c * 512 + w], in0=relus[:, rsl, 0:w], scalar=wis[:, bq, h:h + 1], in1=accs[:, c * 512:c * 512 + w],
                                                                                                 op0=ALU.mult, op1=ALU.add),
                             reads=[("relus", rsl), "wis", "accs"], writes=["accs"])
            P.dma("sp", lambda e, bq=bq: e.dma_start(out=score_all[4 * bq:4 * bq + 4, 0:2112], in_=accs[:, 0:2112]), reads=["accs"], writes=["score_all"], semkey="sca")
        P.op("dve", lambda e: e.tensor_tensor(out=score_all[:, 2048:2112], in0=score_all[:, 2048:2112], in1=masks[:, :], op=ALU.add), reads=["score_all", "masks"], writes=["score_all"])
        for r in range(32):
            src = score_all if r == 0 else work_s
            P.op("dve", lambda e, src=src: e.max(out=mxs[:, :], in_=src[:, 0:2112]), reads=["score_all", "work_s"], writes=["mxs"])
            if r < 31:
                P.op("dve", lambda e, src=src: e.match_replace(out=work_s[:, 0:2112], in_to_replace=mxs[:, :], in_values=src[:, 0:2112], imm_value=-1e30),
                     reads=["score_all", "mxs", "work_s"], writes=["work_s"])
        P.op("dve", lambda e: e.tensor_scalar(out=nots[0:64, 0:2112], in0=score_all[:, 0:2112], scalar1=mxs[:, 7:8], scalar2=None, op0=ALU.is_lt), reads=["score_all", "mxs", "nots"], writes=["nots"])
        A.release(m4); P.barrier()
        Kb = A.alloc([128, 16, 128]); Vb = A.alloc([128, 16, 128]); Kd = A.alloc([128, 16, 2, 64], BF16)
        KTs = A.alloc([128, 2, 16, 128], BF16); Vs = A.alloc([128, 16, 2, 66], BF16)
        tmps = A.alloc([128, 2, 17, 32]); PTs = A.alloc([128, 2, 17, 32], BF16)
        recs = A.alloc([16, 4]); obs = A.alloc([16, 4, 64], BF16)
        P.op("dve", lambda e: e.memset(Vs[:, :, :, :], 1.0), writes=["Vs"])
        P.op("dve", lambda e: e.memset(tmps[:, :, :, :], 0.0), writes=["tmps"])
        for bq in range(NSB):
            c0 = 2048 + 4 * bq
            P.dma("pool", lambda e, bq=bq: e.indirect_dma_start(out=Kb[:, :, :].rearrange("p u d -> p (u d)"), out_offset=None, in_=cache_k,
                                                               in_offset=bass.IndirectOffsetOnAxis(ap=idx[:, bq:bq + 1], axis=0)), reads=["idx"], writes=["Kb"])
            P.dma("pool", lambda e, bq=bq: e.indirect_dma_start(out=Vb[:, :, :].rearrange("p u d -> p (u d)"), out_offset=None, in_=cache_v,
                                                               in_offset=bass.IndirectOffsetOnAxis(ap=idx[:, bq:bq + 1], axis=0)), reads=["idx"], writes=["Vb"])
            P.op("act", lambda e: e.copy(out=Vs[:, :, :, 0:64], in_=Vb[:, :, :].rearrange("p u (g d) -> p u g d", g=2)), reads=["Vb"], writes=["Vs"])
            for g in range(2):
                P.op("act", lambda e, g=g: e.copy(out=Kd[:, :, :, :], in_=Kb[:, :, g * 64:(g + 1) * 64].unsqueeze(2).to_broadcast([128, 16, 2, 64])), reads=["Kb"], writes=["Kd"])
                for ub in range(2):
                    for u8 in range(8):
                        u = ub * 8 + u8
                        P.op("pe", lambda e, u=u, u8=u8: e.transpose(psb[:, u8, :], Kd[:, u, :, :].rearrange("p a d -> p (a d)"), ident_b[:, :]), reads=["Kd", "ident_b"], writes=["psb"])
                    P.op("dve", lambda e, ub=ub, g=g: e.tensor_scalar(out=KTs[:, g, ub * 8:(ub + 1) * 8, :], in0=psb[:, :, :], scalar1=gq2[:, 0:1], scalar2=0.125, op0=ALU.mult, op1=ALU.mult),
                         reads=["psb", "gq2"], writes=["KTs"])
            for half in range(2):
                for u in range(17):
                    if u < 16:
                        out_fn = lambda g, half=half, u=u: ps[:, 3 + half, u * 32 + g * 16:u * 32 + g * 16 + 16]
                        mo = ps[:, 3 + half, u * 32:(u + 1) * 32]; wk = ("ps", 3 + half)
                        nl = nots[:, u * 128:(u + 1) * 128]
                    else:
                        out_fn = lambda g, half=half: ps[0:64, 5 + half, g * 16:g * 16 + 16]
                        mo = ps[0:64, 5 + half, 0:32]; wk = ("ps", 5 + half)
                        nl = nots[:, 2048:2112]
                    P.op("pe", lambda e, mo=mo, nl=nl, bq=bq: e.matmul(mo, lhsT=nl, rhs=selh[:, bq, :], start=True, stop=False), reads=["nots", "selh"], writes=[wk])
                    for g in range(2):
                        lt = KTs[half * 64:(half + 1) * 64, g, u, :] if u < 16 else KTd[half * 64:(half + 1) * 64, g, 2048:2112]
                        P.op("pe", lambda e, lt=lt, g=g, half=half, c0=c0, out_fn=out_fn: e.matmul(out_fn(g), lhsT=lt, rhs=qT[half * 64:(half + 1) * 64, g * 4:(g + 1) * 4, c0:c0 + 4], start=False, stop=(g == 1)),
                             reads=["KTs", ("KTd", g)] + QK, writes=[wk])
            for half in range(2):
                P.op("dve", lambda e, half=half: e.tensor_tensor(out=tmps[:, half, 0:16, :].rearrange("p u (a t) -> p u a t", t=4), in0=ps[:, 3 + half, :].rearrange("p (u a t) -> p u a t", u=16, t=4),
                                                                in1=TzS[:, :, :, half::2].rearrange("p u t a -> p u a t"), op=ALU.add), reads=[("ps", 3 + half), "TzS"], writes=["tmps"])
                P.op("dve", lambda e, half=half: e.tensor_tensor(out=tmps[0:64, half, 16, :].rearrange("p (a t) -> p a t", t=4), in0=ps[0:64, 5 + half, 0:32].rearrange("p (a t) -> p a t", t=4),
                                                                in1=NBb[:, :, half::2].rearrange("p t a -> p a t"), op=ALU.add), reads=[("ps", 5 + half), "NBb"], writes=["tmps"])
                P.op("dve", lambda e, half=half, bq=bq: e.tensor_tensor(out=tmps[0:64, half, 16, :].rearrange("p (a t) -> p a t", t=4), in0=tmps[0:64, half, 16, :].rearrange("p (a t) -> p a t", t=4),
                                                                       in1=vm[:, bq, :].unsqueeze(1).to_broadcast([64, 8, 4]), op=ALU.add), reads=["tmps", "vm"], writes=["tmps"])
            P.op("act", lambda e: e.activation(out=PTs[:, :, :, :], in_=tmps[:, :, :, :], func=AF.Exp), reads=["tmps"], writes=["PTs"])
            firstpv = True
            for u in range(17):
                for g in range(2):
                    for half in range(2):
                        if u < 16:
                            lt = PTs[:, half, u, g * 16:(g + 1) * 16]; rv = Vs[:, u, g, 0:65]
                        else:
                            lt = PTs[0:64, half, 16, g * 16:(g + 1) * 16]; rv = Vaug[0:64, 16, g, 0:65]
                        cc = (g * 2 + half) * 66
                        P.op("pe", lambda e, lt=lt, rv=rv, cc=cc, fp=firstpv, u=u: e.matmul(ps[0:16, 2, cc:cc + 65], lhsT=lt, rhs=rv, start=fp, stop=(u == 16)),
                             reads=["PTs", "Vs", ("Vaug", 16)], writes=[("ps", 2)])
                        firstpv = False
            P.op("dve", lambda e: e.reciprocal(out=recs[:, :], in_=ps[0:16, 2, 0:264].rearrange("q (h c) -> q h c", c=66)[:, :, 64]), reads=[("ps", 2)], writes=["recs"])
            P.op("dve", lambda e: e.tensor_tensor(out=obs[:, :, :], in0=ps[0:16, 2, 0:264].rearrange("q (h c) -> q h c", c=66)[:, :, 0:64],
                                                  in1=recs[:, :].unsqueeze(2).to_broadcast([16, 4, 64]), op=ALU.mult), reads=[("ps", 2), "recs"], writes=["obs"])
            for g in range(2):
                P.op("pe", lambda e, g=g: e.transpose(psb[:, g, 0:16], obs[:, 2 * g:2 * g + 2, :].rearrange("q h d -> q (h d)"), ident_b[0:16, 0:16]), reads=["obs", "ident_b"], writes=["psb"])
            P.op("act", lambda e, c0=c0: e.copy(out=obT[:, :, c0:c0 + 4].rearrange("p (g a) t -> p g a t", g=2), in_=psb[:, 0:2, 0:16].rearrange("p g (a t) -> p g a t", t=4)), reads=["psb"], writes=QK)
        if dbg_ob is not None:
            P.dma("sp", lambda e: e.dma_start(out=dbg_ob, in_=obT[:, :, :]), reads=QK, store=True)
        A.release(m3); P.barrier()
        sgm = A.alloc([128, 2, 512]); mtmp = A.alloc([128, 512]); gtmp = A.alloc([128, 2, 512], BF16)
        for p8 in range(8):
            sl = load_w([(0, C_AG + p8 * 128, 128)], 128)
            for ni, (n0, nn) in enumerate(NTL):
                bg = nbank(); ssl = ni % 2
                proj_fm(sl, 128, n0, nn, bg)
                P.op("act", lambda e, bg=bg, nn=nn, ssl=ssl: e.activation(out=gtmp[:, ssl, 0:nn], in_=ps[:, bg, 0:nn], func=AF.Silu), reads=[("ps", bg)], writes=[("gtmp", ssl)])
                P.op("dve", lambda e, nn=nn, ssl=ssl, p8=p8, n0=n0: e.tensor_tensor(out=obT[:, p8, n0:n0 + nn], in0=obT[:, p8, n0:n0 + nn], in1=gtmp[:, ssl, 0:nn], op=ALU.mult),
                     reads=[("gtmp", ssl), ("qT", p8)], writes=[("qT", p8)])
        merge_branch(obT, QK, w_pb, C_GB, False, sgm, mtmp)
        A.release(m2); P.barrier()
        wo = A.alloc([128, 8, D], BF16); xo = A.alloc([128, 2, D]); yo = A.alloc([128, 2, D])
        for cb in range(8):
            sl = load_w([(0, cb * 128, 128)], 128, src=w_out)
            P.op("act", lambda e, sl=sl, cb=cb: e.copy(out=wo[:, :, cb * 128:(cb + 1) * 128], in_=wb[:, sl, :, :]), reads=[("wb", sl)], writes=["wo"])
        for ti, (t0, n) in enumerate(TT):
            sl = ti % 2
            src = x_p[t0:t0 + n, :] if ti < 16 else x_s[:, :]
            P.dma("sp", lambda e, sl=sl, n=n, src=src: e.dma_start(out=xo[0:n, sl, :], in_=src), writes=[("xo", sl)])
            for hf in range(2):
                bk = nbank()
                for kc in range(8):
                    P.op("pe", lambda e, kc=kc, bk=bk, t0=t0, n=n, hf=hf: e.matmul(ps[0:n, bk, :], lhsT=mT[:, kc, t0:t0 + n], rhs=wo[:, kc, hf * 512:(hf + 1) * 512], start=(kc == 0), stop=(kc == 7)),
                         reads=["mT", "wo"], writes=[("ps", bk)])
                P.op("dve", lambda e, bk=bk, n=n, sl=sl, hf=hf: e.tensor_tensor(out=yo[0:n, sl, hf * 512:(hf + 1) * 512], in0=ps[0:n, bk, :], in1=xo[0:n, sl, hf * 512:(hf + 1) * 512], op=ALU.add),
                     reads=[("ps", bk), ("xo", sl)], writes=[("yo", sl)])
            dst = y_p[t0:t0 + n, :] if ti < 16 else y_s[:, :]
            P.dma("sp", lambda e, dst=dst, n=n, sl=sl: e.dma_start(out=dst, in_=yo[0:n, sl, :]), reads=[("yo", sl)], store=True)

        return finish()


_CACHE = {}


def kernel(**inputs):
    f32 = np.float32
    if "nc" not in _CACHE:
        _CACHE["nc"] = build_program()
    nc = _CACHE["nc"]
    x_prompt = np.asarray(inputs["x_prompt"], f32); x_sample = np.asarray(inputs["x_sample"], f32)
    bones = np.zeros((128, 128), f32); bones[:64, :64] = 1; bones[64:, 64:] = 1
    common = {
        "w_in": np.ascontiguousarray(np.asarray(inputs["w_in"], f32)[0]),
        "norm_g": np.asarray(inputs["norm_g"], f32),
        "q_norm_g": np.asarray(inputs["q_norm_g"], f32), "k_norm_g": np.asarray(inputs["k_norm_g"], f32),
        "ident": np.eye(128, dtype=f32), "bones": bones,
        "rel_bias": np.asarray(inputs["rel_bias"], f32),
    }
    import ml_dtypes
    qq = np.arange(128)[None, :, None]; sk = np.arange(128)[None, None, :]; ty = np.arange(2)[:, None, None]
    dist = 128 * ty + qq - sk
    bk = _t5_bucket_np(dist)
    oh = (bk[None] == np.arange(32)[:, None, None, None]) & (dist[None] >= 0)
    common["oh_p"] = oh.astype(f32).astype(ml_dtypes.bfloat16)
    sq_ = np.arange(128)
    common["caus0"] = np.where(sq_[:, None] <= sq_[None, :], 0.0, -30000.0).astype(f32)
    common["causS"] = np.where(sq_[None, :] <= sq_[:, None], 0.0, -1e30).astype(f32)
    uu = np.arange(16)[:, None, None]; tt = np.arange(4)[None, :, None]; pp = np.arange(128)[None, None, :]
    s_key = 1920 + (pp - 120) * 16 + uu
    dist_s = 2048 + tt - s_key
    ohs = (_t5_bucket_np(dist_s)[None] == np.arange(32)[:, None, None, None]) & (pp >= 120)[None]
    common["ohs"] = ohs.astype(f32).astype(ml_dtypes.bfloat16)
    kt = (np.arange(64) % 4)[None, :]; t4 = np.arange(4)[:, None]
    dn = t4 - kt
    ohn = (_t5_bucket_np(dn)[None] == np.arange(32)[:, None, None]) & (dn >= 0)[None]
    common["ohn"] = ohn.astype(f32).astype(ml_dtypes.bfloat16)
    kb_ = (np.arange(64) // 4); ktt = (np.arange(64) % 4)
    vm = np.where((kb_[:, None, None] == np.arange(16)[None, :, None]) & (ktt[:, None, None] <= np.arange(4)[None, None, :]), 0.0, -30000.0)
    common["vm"] = vm.astype(f32)
    selh = np.zeros((128, 16, 8, 4), f32)
    for bb in range(16):
        for t_ in range(4):
            selh[4 * bb + t_, bb, :, t_] = -30000.0
    common["selh"] = selh.reshape(128, 16, 32)
    common["masks"] = np.where((kb_[:, None] == kb_[None, :]) & (ktt[None, :] <= ktt[:, None]), 0.0, -1e30).astype(f32)
    common["cmod"] = (np.arange(128) % 8).astype(np.int32).reshape(128, 1)
    if STAGE >= 6:
        common["cache_k"] = np.asarray(inputs["cache_k"], f32).reshape(NPHYS * 8, 2048)
        common["cache_v"] = np.asarray(inputs["cache_v"], f32).reshape(NPHYS * 8, 2048)
        common["cache_ki"] = np.asarray(inputs["cache_kidx"], f32).reshape(NPHYS * 8, 1024)
    for nm in ("w_pa", "w_pb", "w_out"):
        common[nm] = np.ascontiguousarray(np.asarray(inputs[nm], f32)[0])
    for nm in ("shift_mu", "w0", "a0", "k_k", "k_a", "lnx_g", "lnx_b"):
        common[nm] = np.asarray(inputs[nm], f32).reshape(1, -1)
    common["r_k"] = np.asarray(inputs["r_k"], f32).reshape(1, 1024)
    common["w2"] = np.ascontiguousarray(np.asarray(inputs["w2"], f32)[0]); common["a2"] = np.ascontiguousarray(np.asarray(inputs["a2"], f32)[0])
    i64 = np.arange(64)
    common["mask_u"] = (i64[:, None] < i64[None, :]).astype(f32)
    common["mask_ui"] = (i64[:, None] <= i64[None, :]).astype(f32)
    common["mask_l"] = (i64[None, :] < i64[:, None]).astype(f32)
    seg = np.ones((64, 2, 4, 64), f32); seg[:, 0, :, 0] = 0.0; seg[:, 1, :, 0::4] = 0.0
    common["segmask"] = seg.reshape(64, 2, 256)
    state_wkv = np.asarray(inputs["state_wkv"], f32)[0]; state_shift = np.asarray(inputs["state_shift"], f32)[0]
    page_table = np.asarray(inputs["page_table"], np.int32)
    in_maps = []
    for c in range(NCORES):
        m = dict(common)
        m["state_wkv"] = np.ascontiguousarray(state_wkv[16 * c:16 * c + 16]); m["state_shift"] = np.ascontiguousarray(state_shift[16 * c:16 * c + 16])
        m["ptrep"] = np.ascontiguousarray(np.repeat(page_table[16 * c:16 * c + 16], 8, axis=1).T)
        m["x_p"] = np.ascontiguousarray(x_prompt[c])
        m["x_s"] = np.ascontiguousarray(x_sample[16 * c:16 * c + 16].reshape(64, D))
        in_maps.append(m)
    res = run_bass_kernel_spmd(nc, in_maps, core_ids=list(range(NCORES)))
    R = res.results
    if os.environ.get("DBG"):
        for k_ in R[0]:
            if k_.startswith("dd_") or k_.startswith("dbg_"):
                np.save(k_ + ".npy", np.asarray(R[0][k_]).astype(f32))
    cat = lambda name: np.stack([R[c][name] for c in range(NCORES)], 0)
    y_p = cat("y_p").reshape(8, 2048, 1024)
    y_s = cat("y_s").reshape(128, 4, 1024)
    k_p = cat("k_p").reshape(1, 8, 2048, 2, 64); v_p = cat("v_p").reshape(1, 8, 2048, 2, 64)
    ki_p = cat("ki_p").reshape(1, 8, 2048, 64)
    wkv_p = cat("wkv_p").reshape(1, 8, 16, 64, 64)
    sh_p = cat("sh_p").reshape(1, 8, 4224)
    k_s = cat("k_s").reshape(1, 128, 4, 2, 64); v_s = cat("v_s").reshape(1, 128, 4, 2, 64)
    ki_s = cat("ki_s").reshape(1, 128, 4, 64)
    wkv_s = cat("wkv_s").reshape(1, 128, 16, 64, 64)
    sh_s = cat("sh_s").reshape(1, 128, 4224)
    return (y_p, y_s, k_p, v_p, ki_p, wkv_p, sh_p, k_s, v_s, ki_s, wkv_s, sh_s)
```

```python
import math
from contextlib import ExitStack
import numpy as np
import concourse.bass as bass
import concourse.mybir as mybir
from concourse.bass_utils import run_bass_kernel_spmd

F32 = mybir.dt.float32
BF16 = mybir.dt.bfloat16
I32 = mybir.dt.int32
AF = mybir.ActivationFunctionType
ALU = mybir.AluOpType
AX = mybir.AxisListType

NCORES = 8
D = 1024
NP_ = 2048
NS = 64
NT = NP_ + NS
NCOLS = 9160
C_R, C_K, C_V, C_G, C_WD, C_AD = 0, 1024, 2048, 3072, 4096, 4160
A0 = 4224
C_Q, C_AK, C_AV, C_QI, C_KI, C_WI, C_AG = A0, A0 + 1024, A0 + 1152, A0 + 1280, A0 + 1792, A0 + 1856, A0 + 1864
C_GA, C_GB = 7112, 8136
NPHYS = 2560
import os
STAGE = int(os.environ.get('KSTAGE', '99'))

import os as _os
_SES = _os.environ.get("SES", "act,dve,pool")
SAME_ENGINE_SYNC = {"pe": False, "act": "act" in _SES, "dve": "dve" in _SES, "pool": "pool" in _SES, "sp": True}


BURST = int(_os.environ.get('BURST', '40'))


class _Op:
    __slots__ = ("eng", "fn", "deps", "is_dma", "dsem", "count", "needs_inc", "waits", "idx")


class Prog:
    def __init__(self, nc):
        self.nc = nc
        self.ops = {e: [] for e in ("pe", "act", "dve", "pool", "sp")}
        self.last_w = {}
        self.readers = {}
        self.dma_sem_names = {}
        self.all_ops = []
        self.store_ops = []
        self.bar = []
        self.last_dma = {}
        self.cur = None
        self.threads = {}

    def barrier(self):
        self.bar = [v[-1] for v in self.ops.values() if v] + list(self.last_dma.values())

    def _add(self, eng, fn, reads, writes, is_dma=False, dsem=None):
        op = _Op()
        op.eng = eng; op.fn = fn; op.is_dma = is_dma; op.dsem = dsem
        op.needs_inc = False; op.count = None; op.waits = []
        deps = []
        writes = writes + [r for r in reads if isinstance(r, tuple) and r[0] == "ps" and r not in writes]
        for r in reads:
            w = self.last_w.get(r)
            if w is not None:
                deps.append(w)
        for r in writes:
            w = self.last_w.get(r)
            if w is not None:
                deps.append(w)
            deps.extend(self.readers.get(r, ()))
        deps.extend(self.bar)
        op.deps = deps
        for r in reads:
            self.readers.setdefault(r, []).append(op)
        for r in writes:
            self.last_w[r] = op
            self.readers[r] = []
        self.ops[eng].append(op)
        self.all_ops.append(op)
        return op

    def op(self, eng, fn, reads=(), writes=()):
        if self.cur is not None:
            self.cur.append(("op", eng, fn, list(reads), list(writes), None, False))
            return None
        return self._add(eng, fn, list(reads), list(writes))

    def thread(self, name):
        self.cur = [] if name is not None else None
        if name is not None:
            self.threads[name] = self.cur

    def merge(self, names):
        self.cur = None
        qs = [self.threads.pop(n) for n in names if n in self.threads]
        idx = [0] * len(qs)
        alive = True
        while alive:
            alive = False
            for k, q in enumerate(qs):
                for _ in range(BURST):
                    if idx[k] >= len(q):
                        break
                    kind, eng, fn, reads, writes, semkey, store = q[idx[k]]
                    idx[k] += 1
                    alive = True
                    if kind == "op":
                        self._add(eng, fn, reads, writes)
                    else:
                        self._dma_now(eng, fn, reads, writes, semkey, store)

    def dma(self, q, fn, reads=(), writes=(), semkey=None, store=False):
        if semkey is None:
            semkey = ("w", writes[0]) if writes else ("r", reads[0])
        if self.cur is not None:
            self.cur.append(("dma", q, fn, list(reads), list(writes), semkey, store))
            return None
        return self._dma_now(q, fn, reads, writes, semkey, store)

    def _dma_now(self, q, fn, reads, writes, semkey, store):
        if semkey not in self.dma_sem_names:
            self.dma_sem_names[semkey] = len(self.dma_sem_names)
        op = self._add(q, fn, list(reads), list(writes), is_dma=True, dsem=semkey)
        self.last_dma[semkey] = op
        if store:
            self.store_ops.append(op)
        return op

    def emit(self):
        nc = self.nc
        def skip(d, op):
            return (not d.is_dma) and d.eng == op.eng and (not op.is_dma) and not SAME_ENGINE_SYNC[op.eng]
        for op in self.all_ops:
            for d in op.deps:
                if d.is_dma or d is op or skip(d, op):
                    continue
                d.needs_inc = True
        cnt = {e: 0 for e in self.ops}
        dcnt = {k: 0 for k in self.dma_sem_names}
        for op in self.all_ops:
            if op.is_dma:
                dcnt[op.dsem] += 16
                op.count = dcnt[op.dsem]
            elif op.needs_inc:
                cnt[op.eng] += 1
                op.count = cnt[op.eng]
        waited = {e: {} for e in self.ops}
        for op in self.all_ops:
            need = {}
            for d in op.deps:
                if d is op or d.count is None:
                    continue
                if d.is_dma:
                    key = ("d", d.dsem)
                else:
                    if skip(d, op):
                        continue
                    key = ("e", d.eng)
                if need.get(key, 0) < d.count:
                    need[key] = d.count
            w = waited[op.eng]
            for key, v in need.items():
                if w.get(key, 0) < v:
                    w[key] = v
                    op.waits.append((key, v))
        final_waits = {}
        for op in self.store_ops:
            key = ("d", op.dsem)
            final_waits[key] = max(final_waits.get(key, 0), op.count)
        with ExitStack() as es:
            sems = {}
            for e in self.ops:
                sems[("e", e)] = es.enter_context(nc.semaphore(f"s_{e}"))
            for k, i in self.dma_sem_names.items():
                sems[("d", k)] = es.enter_context(nc.semaphore(f"sd_{i}"))
            block = es.enter_context(nc.Block())

            def run(engname):
                def body(eng):
                    for op in self.ops[engname]:
                        for key, v in op.waits:
                            eng.wait_ge(sems[key], v)
                        ins = op.fn(eng)
                        if op.is_dma:
                            ins.then_inc(sems[("d", op.dsem)], 16)
                        elif op.needs_inc:
                            ins.then_inc(sems[("e", engname)], 1)
                    if engname == "sp":
                        for key, v in final_waits.items():
                            eng.wait_ge(sems[key], v)
                return body

            block.tensor(run("pe"))
            block.scalar(run("act"))
            block.vector(run("dve"))
            block.gpsimd(run("pool"))
            block.sync(run("sp"))
        return {e: len(v) for e, v in self.ops.items()}


class Arena:
    def __init__(self, nc, es, nbytes):
        self.t = es.enter_context(nc.sbuf_tensor("arena", [128, nbytes // 4], F32))
        self.off = 0
        self.peak = 0
        self.cap = nbytes

    def alloc(self, shape, dt=F32):
        esz = 4 if dt in (F32, I32) else 2
        n = 1
        for d in shape[1:]:
            n *= d
        nb = (n * esz + 31) // 32 * 32
        assert self.off + nb <= self.cap, ("arena overflow", self.off, nb, self.cap)
        v = self.t[0:shape[0], self.off // 4:(self.off + nb) // 4]
        if dt != F32:
            v = v.bitcast(dt)
        v = v[:, 0:n]
        if len(shape) == 3:
            v = v.rearrange("p (a b) -> p a b", a=shape[1])
        elif len(shape) == 4:
            v = v.rearrange("p (a b c) -> p a b c", a=shape[1], b=shape[2])
        self.off += nb
        self.peak = max(self.peak, self.off)
        return v

    def mark(self):
        return self.off

    def release(self, m):
        self.off = m


TT = [(i * 128, 128) for i in range(16)] + [(2048, 64)]
NTL = [(0, 512), (512, 512), (1024, 512), (1536, 512), (2048, 64)]


def _t5_bucket_np(d):
    d = np.maximum(d, 0)
    df = np.maximum(d, 1).astype(np.float32)
    large = 16 + (np.log(df / np.float32(16)) / np.float32(math.log(128 / 16)) * np.float32(16)).astype(np.int32)
    large = np.minimum(large, 31)
    return np.where(d < 16, d, large)


def build_program():
    nc = bass.Bass("TRN2", target_bir_lowering=False)
    di = lambda name, shape, dt=F32: nc.dram_tensor(name, list(shape), dt, kind="ExternalInput").ap()
    do = lambda name, shape, dt=F32: nc.dram_tensor(name, list(shape), dt, kind="ExternalOutput").ap()
    x_p = di("x_p", [NP_, D]); x_s = di("x_s", [NS, D])
    w_in = di("w_in", [D, NCOLS]); norm_g = di("norm_g", [1, D])
    qn_g = di("q_norm_g", [1, 64]); kn_g = di("k_norm_g", [1, 64])
    ident_d = di("ident", [128, 128]); bones_d = di("bones", [128, 128])
    rel_bias_d = di("rel_bias", [32, 16]); oh_d = di("oh_p", [32, 2, 128, 128], BF16)
    caus0_d = di("caus0", [128, 128]); causS_d = di("causS", [128, 128])
    dbg_ob = do("dbg_ob", [128, 8, NT], BF16) if os.environ.get("DBG") else None
    dbg_oa = do("dbg_oa", [128, 8, NT], BF16) if os.environ.get("DBG") else None
    w_pa = di("w_pa", [D, D]); w_pb = di("w_pb", [D, D]); w_out = di("w_out", [D, D])
    mu_d = di("shift_mu", [1, 4224]); w0_d = di("w0", [1, 1024]); a0_d = di("a0", [1, 1024]); w2_d = di("w2", [64, 1024]); a2_d = di("a2", [64, 1024])
    kk_d = di("k_k", [1, 1024]); ka_d = di("k_a", [1, 1024]); rk_d = di("r_k", [1, 1024]); lg_d = di("lnx_g", [1, 1024]); lb_d = di("lnx_b", [1, 1024])
    swkv_d = di("state_wkv", [16, 16, 64, 64]); ssh_d = di("state_shift", [16, 4224])
    mu64_d = di("mask_u", [64, 64]); mui64_d = di("mask_ui", [64, 64]); ml64_d = di("mask_l", [64, 64]); seg_d = di("segmask", [64, 2, 256])
    if STAGE >= 6:
        cache_k = di("cache_k", [NPHYS * 8, 2048]); cache_v = di("cache_v", [NPHYS * 8, 2048]); cache_ki = di("cache_ki", [NPHYS * 8, 1024])
    ptrep_d = di("ptrep", [128, 16], I32); cmod_d = di("cmod", [128, 1], I32)
    ohs_d = di("ohs", [32, 16, 4, 128], BF16); ohn_d = di("ohn", [32, 4, 64], BF16)
    vm_d = di("vm", [64, 16, 4]); selh_d = di("selh", [128, 16, 32]); masks_d = di("masks", [64, 64])
    y_p = do("y_p", [NP_, D]); y_s = do("y_s", [NS, D])
    k_p = do("k_p", [NP_, 128]); v_p = do("v_p", [NP_, 128]); ki_p = do("ki_p", [NP_, 64])
    k_s = do("k_s", [NS, 128]); v_s = do("v_s", [NS, 128]); ki_s = do("ki_s", [NS, 64])
    sh_p = do("sh_p", [1, 4224]); sh_s = do("sh_s", [16, 4224])
    wkv_p = do("wkv_p", [16, 64, 64]); wkv_s = do("wkv_s", [16, 16, 64, 64])

    es = ExitStack()
    with es:
        A = Arena(nc, es, 207 * 1024)
        ps = es.enter_context(nc.psum_tensor("ps", [128, 7, 512], F32))
        psb = es.enter_context(nc.psum_tensor("psb", [128, 8, 128], BF16))
        P = Prog(nc)

        dumps = {}

        def dbgdump(name, ap, shape):
            if not os.environ.get("DBG") or name in dumps:
                return
            d_ = do("dd_" + name, shape, BF16 if name == "oodd" else F32)
            dumps[name] = d_
            P.dma("sp", lambda e: e.dma_start(out=d_, in_=ap), reads=[name], store=True, semkey=("dd", name))

        def finish():
            counts = P.emit()
            print("ops:", counts, "sbuf peak", A.peak)
            return nc

        ident = A.alloc([128, 128]); ident_b = A.alloc([128, 128], BF16)
        bones = A.alloc([128, 128])
        gcol = A.alloc([128, 8])
        gq2 = A.alloc([128, 1]); gk2 = A.alloc([128, 1]); gkq = A.alloc([128, 1]); eps6 = A.alloc([128, 1])
        P.dma("sp", lambda e: e.dma_start(out=ident[:, :], in_=ident_d), writes=["ident"])
        P.dma("sp", lambda e: e.dma_start(out=bones[:, :], in_=bones_d), writes=["bones"])
        P.dma("sp", lambda e: e.dma_start(out=gcol[:, :], in_=norm_g.rearrange("o (k p) -> p (o k)", p=128), allow_slow_non_contiguous=True), writes=["gcol"])
        for hh in range(2):
            P.dma("sp", lambda e, hh=hh: e.dma_start(out=gq2[hh * 64:(hh + 1) * 64, :], in_=qn_g.rearrange("o d -> d o"), allow_slow_non_contiguous=True), writes=["gq2"], semkey="gq2")
            P.dma("sp", lambda e, hh=hh: e.dma_start(out=gk2[hh * 64:(hh + 1) * 64, :], in_=kn_g.rearrange("o d -> d o"), allow_slow_non_contiguous=True), writes=["gk2"], semkey="gk2")
        eps24 = A.alloc([128, 1]); epsln = A.alloc([128, 1])
        P.op("dve", lambda e: e.memset(eps24[:, :], 1e-24), writes=["eps24"])
        P.op("dve", lambda e: e.memset(epsln[:, :], 64e-5), writes=["epsln"])
        P.op("dve", lambda e: e.memset(eps6[:, :], 1e-6), writes=["eps6"])
        P.op("dve", lambda e: e.tensor_copy(out=ident_b[:, :], in_=ident[:, :]), reads=["ident"], writes=["ident_b"])
        P.op("dve", lambda e: e.tensor_scalar(out=gkq[:, :], in0=gk2[:, :], scalar1=gq2[:, 0:1], scalar2=0.125, op0=ALU.mult, op1=ALU.mult), reads=["gk2", "gq2"], writes=["gkq"])

        xnT = A.alloc([128, 8, NT], BF16)
        wst = A.alloc([128, 2, 8, 128]); wb = A.alloc([128, 2, 8, 128], BF16)
        m0 = A.mark()
        xin = A.alloc([128, 2, D]); xh = A.alloc([128, 2, D], BF16); junk = A.alloc([128, D])
        ss = A.alloc([128, 17]); rstd = A.alloc([128, 17])
        P.op("dve", lambda e: e.memset(ss[:, :], 0.0), writes=["ss"])
        for ti, (t0, n) in enumerate(TT):
            sl = ti % 2
            src = x_p[t0:t0 + n, :] if ti < 16 else x_s[:, :]
            P.dma("sp", lambda e, sl=sl, n=n, src=src: e.dma_start(out=xin[0:n, sl, :], in_=src), writes=[("xin", sl)])
            P.op("act", lambda e, sl=sl, n=n, ti=ti: e.activation(out=junk[0:n, :], in_=xin[0:n, sl, :], func=AF.Square, accum_out=ss[0:n, ti:ti + 1]),
                 reads=[("xin", sl), "ss"], writes=["junk", ("ss", ti)])
            P.op("act", lambda e, n=n, ti=ti: e.activation(out=rstd[0:n, ti:ti + 1], in_=ss[0:n, ti:ti + 1], func=AF.Sqrt, bias=eps6[0:n, 0:1], scale=1.0 / D),
                 reads=[("ss", ti), "eps6"], writes=[("rstd", ti)])
            P.op("dve", lambda e, n=n, ti=ti: e.reciprocal(out=rstd[0:n, ti:ti + 1], in_=rstd[0:n, ti:ti + 1]),
                 reads=[("rstd", ti)], writes=[("rstd", ti)])
            P.op("dve", lambda e, sl=sl, n=n, ti=ti: e.tensor_scalar(out=xh[0:n, sl, :], in0=xin[0:n, sl, :], scalar1=rstd[0:n, ti:ti + 1], scalar2=None, op0=ALU.mult),
                 reads=[("xin", sl), ("rstd", ti)], writes=[("xh", sl)])
            for kc in range(8):
                P.op("pe", lambda e, sl=sl, n=n, kc=kc: e.transpose(psb[:, kc, 0:n], xh[0:n, sl, kc * 128:(kc + 1) * 128], ident_b[0:n, 0:n]),
                     reads=[("xh", sl), "ident_b"], writes=["psb"])
            P.op("act", lambda e, t0=t0, n=n: e.copy(out=xnT[:, :, t0:t0 + n], in_=psb[:, :, 0:n]), reads=["psb"], writes=["xnT"])
        A.release(m0); P.barrier()
        if STAGE < 1:
            return finish()

        wcnt = [0]

        def load_w(parts, m, src=None):
            sl = wcnt[0] % 2
            wcnt[0] += 1
            srcw = w_in if src is None else src
            for (dc, sc, wd_) in parts:
                P.dma("sp", lambda e, sl=sl, dc=dc, sc=sc, wd_=wd_, srcw=srcw: e.dma_start(
                    out=wst[:, sl, :, dc:dc + wd_], in_=srcw[:, sc:sc + wd_].rearrange("(k p) m -> p k m", p=128)),
                    writes=[("wst", sl)])
            if src is None:
                P.op("pool", lambda e, sl=sl, m=m: e.tensor_tensor(out=wb[:, sl, :, 0:m], in0=wst[:, sl, :, 0:m],
                                                                  in1=gcol[:, :].unsqueeze(2).to_broadcast([128, 8, m]), op=ALU.mult),
                     reads=[("wst", sl), "gcol"], writes=[("wb", sl)])
            else:
                P.op("pool", lambda e, sl=sl, m=m: e.tensor_copy(out=wb[:, sl, :, 0:m], in_=wst[:, sl, :, 0:m]), reads=[("wst", sl)], writes=[("wb", sl)])
            return sl

        bankc = [0]

        bank_sets = {None: (0, 1, 2, 3, 4, 5, 6), "tb": (0, 1), "ind": (2, 3, 4), "dep": (5, 6)}
        bank_ctr = {}
        bank_grp = [None]

        def nbank():
            g_ = bank_grp[0]
            s_ = bank_sets[g_]
            k_ = bank_ctr.get(g_, 0)
            bank_ctr[g_] = k_ + 1
            return s_[k_ % len(s_)]

        def proj_fm(sl, m, n0, nn, bank):
            for kc in range(8):
                P.op("pe", lambda e, kc=kc: e.matmul(ps[0:m, bank, 0:nn], lhsT=wb[:, sl, kc, 0:m], rhs=xnT[:, kc, n0:n0 + nn], start=(kc == 0), stop=(kc == 7)),
                     reads=[("wb", sl), "xnT"], writes=[("ps", bank)])

        def proj_tm(sl, m, t0, n, bank, col0=0):
            for kc in range(8):
                P.op("pe", lambda e, kc=kc: e.matmul(ps[0:n, bank, col0:col0 + m], lhsT=xnT[:, kc, t0:t0 + n], rhs=wb[:, sl, kc, 0:m], start=(kc == 0), stop=(kc == 7)),
                     reads=[("wb", sl), "xnT"], writes=[("ps", bank)])


        mT = A.alloc([128, 8, NT], BF16)
        mA = A.mark()
        oaT = A.alloc([128, 8, NT], BF16)
        mR = A.mark()
        HG = 4; NG = 16 // HG; NB = 4 * HG + 2
        ones64 = A.alloc([64, 64]); MU = A.alloc([64, 64]); MUI = A.alloc([64, 64]); ML = A.alloc([64, 64]); segm = A.alloc([64, 2, HG * 64])
        P.op("dve", lambda e: e.memset(ones64[:, :], 1.0), writes=["ones64"])
        P.dma("sp", lambda e: e.dma_start(out=MU[:, :], in_=mu64_d), writes=["MU"])
        P.dma("sp", lambda e: e.dma_start(out=MUI[:, :], in_=mui64_d), writes=["MUI"])
        P.dma("sp", lambda e: e.dma_start(out=ML[:, :], in_=ml64_d), writes=["ML"])
        P.dma("sp", lambda e: e.dma_start(out=segm[:, :, :], in_=seg_d), writes=["segm"])
        wr = A.alloc([128, 8, NB * 64], BF16)
        w2g = A.alloc([64, HG * 64]); a2g = A.alloc([64, HG * 64])
        prm = A.alloc([64, HG, 8])
        mug = A.alloc([64, NB])
        zbuf = A.alloc([64, NB, 65]); dsh = A.alloc([64, HG, 64]); zraw = A.alloc([64, NB, 64])
        shs = A.alloc([16, NB * 64]); shT = A.alloc([64, NB, 16])
        NARR = 22
        MD = BF16
        arrs = [A.alloc([64, HG, 64], MD if i_ in (10, 11, 12, 13, 14, 15, 19, 20, 21) else F32) for i_ in range(NARR)]
        (sg, aa, clr, Ein, Einv, Eex, EC, kk_, kkn, kmod, at_, bt_, kt_, rt_, bh_, kh_, bon, t1, t2, Vt, Bt, Kt) = arrs
        vb2 = [A.alloc([64, HG, 64], MD), A.alloc([64, HG, 64], MD)]; Pb = A.alloc([64, HG, 64], MD)
        mats = [A.alloc([64, HG, 64], MD) for _ in range(8)]
        (N0, N1, NT0, NT1, R0, R1, U0, Uf) = mats
        outT = A.alloc([64, HG, 64])
        pm = lambda dt_=MD: [A.alloc([64, HG, 64], dt_), A.alloc([64, HG, 64], dt_)]
        at2 = [at_, A.alloc([64, HG, 64], MD), A.alloc([64, HG, 64], MD)]; rt2 = [rt_, A.alloc([64, HG, 64], MD), A.alloc([64, HG, 64], MD)]
        bon2 = [bon, A.alloc([64, HG, 64]), A.alloc([64, HG, 64])]; sg2 = pm(F32) + [A.alloc([64, HG, 64])]
        bt2 = [bt_, A.alloc([64, HG, 64], MD)]; kt2 = [kt_, A.alloc([64, HG, 64], MD)]; bh2 = [bh_, A.alloc([64, HG, 64], MD)]; kh2 = [kh_, A.alloc([64, HG, 64], MD)]
        f1 = A.alloc([64, HG, 64]); f2_ = A.alloc([64, HG, 64])
        Vt2 = [Vt, A.alloc([64, HG, 64], MD)]; Bt2 = [Bt, A.alloc([64, HG, 64], MD)]; Kt2 = [Kt, A.alloc([64, HG, 64], MD)]
        TT2 = pm(); Aak2 = pm(); Arb2 = pm(); Ark2 = pm()
        GC2 = [A.alloc([64, HG, 16]), A.alloc([64, HG, 16]), A.alloc([64, HG, 16])]
        Pst = A.alloc([64, HG, 64]); Ssb = A.alloc([64, HG, 64]); tw = A.alloc([64, 64]); adc = A.alloc([64, 64])
        oodd = A.alloc([64, HG // 2, 64], BF16)
        NCH = int(os.environ.get("NCH", "32"))
        NSEQ = int(os.environ.get("NSQ", "16"))
        LD = -0.6065306597126334

        def bc(ap2, C=64):
            return ap2.unsqueeze(2).to_broadcast([64, HG, C])

        for G in range(NG):
            blocks = [(kind * 1024 + (HG * G + h) * 64) for kind in range(4) for h in range(HG)]
            for bi in range(2 * HG + 1):
                if bi < 2 * HG:
                    parts = [(0, blocks[2 * bi], 64), (64, blocks[2 * bi + 1], 64)]
                else:
                    parts = [(0, C_WD, 128)]
                sl = load_w(parts, 128)
                P.op("act", lambda e, sl=sl, bi=bi: e.copy(out=wr[:, :, bi * 128:(bi + 1) * 128], in_=wb[:, sl, :, :]), reads=[("wb", sl)], writes=["wr"])
            P.dma("sp", lambda e, G=G: e.dma_start(out=w2g[:, :], in_=w2_d[:, G * HG * 64:(G + 1) * HG * 64]), writes=["w2g"])
            P.dma("sp", lambda e, G=G: e.dma_start(out=a2g[:, :], in_=a2_d[:, G * HG * 64:(G + 1) * HG * 64]), writes=["a2g"])
            for wi_, pd in enumerate((w0_d, a0_d, kk_d, ka_d, rk_d, lg_d, lb_d)):
                P.dma("sp", lambda e, wi_=wi_, pd=pd, G=G: e.dma_start(out=prm[:, :, wi_], in_=pd[0:1, G * HG * 64:(G + 1) * HG * 64].rearrange("o (h j) -> j (o h)", j=64), allow_slow_non_contiguous=True),
                      writes=["prm"], semkey="prm")
            for kind in range(4):
                P.dma("sp", lambda e, kind=kind, G=G: e.dma_start(out=mug[:, kind * HG:(kind + 1) * HG], in_=mu_d[0:1, kind * 1024 + G * HG * 64:kind * 1024 + (G + 1) * HG * 64].rearrange("o (h j) -> j (o h)", j=64),
                                                                allow_slow_non_contiguous=True), writes=["mug"], semkey="mug")
            P.dma("sp", lambda e: e.dma_start(out=mug[:, 4 * HG:4 * HG + 2], in_=mu_d[0:1, C_WD:C_WD + 128].rearrange("o (h j) -> j (o h)", j=64), allow_slow_non_contiguous=True), writes=["mug"], semkey="mug")
            for kind in range(4):
                P.dma("sp", lambda e, kind=kind, G=G: e.dma_start(out=shs[:, kind * HG * 64:(kind + 1) * HG * 64], in_=ssh_d[:, kind * 1024 + G * HG * 64:kind * 1024 + (G + 1) * HG * 64]), writes=["shs"], semkey="shs")
            P.dma("sp", lambda e: e.dma_start(out=shs[:, 4 * HG * 64:NB * 64], in_=ssh_d[:, C_WD:C_WD + 128]), writes=["shs"], semkey="shs")
            bk = nbank()
            for blk in range(NB):
                P.op("pe", lambda e, bk=bk, blk=blk: e.transpose(ps[0:64, bk, blk * 16:(blk + 1) * 16], shs[:, blk * 64:(blk + 1) * 64], ident[0:16, 0:16]), reads=["shs", "ident"], writes=[("ps", bk)])
            P.op("act", lambda e, bk=bk: e.copy(out=shT[:, :, :], in_=ps[0:64, bk, 0:NB * 16].rearrange("p (q b) -> p q b", b=16)), reads=[("ps", bk)], writes=["shT"])
            zraw_valid = set()
            P.op("dve", lambda e: e.memset(zbuf[:, :, 0:1], 0.0), writes=["zbuf"])
            P.op("dve", lambda e: e.memset(Pst[:, :, :], 0.0), writes=["Pst"])
            P.op("dve", lambda e: e.memset(Pb[:, :, :], 0.0), writes=["Pb"])

            def token_batch(col0, L, nseg, p, q2=0):
                sgi = 0 if L == 64 else 1
                at_ = at2[p]; rt_ = rt2[p]; bon = bon2[p]; GC = GC2[p]; sgate = sg2[p]
                bt_ = bt2[q2]; kt_ = kt2[q2]; bh_ = bh2[q2]; kh_ = kh2[q2]; vb = vb2[q2]
                n_bt = f"bt_{q2}"; n_kt = f"kt_{q2}"; n_bh = f"bh_{q2}"; n_kh = f"kh_{q2}"; n_vb = f"vb{q2}"
                n_at = f"at_{p}"; n_rt = f"rt_{p}"; n_bon = f"bon{p}"; n_gc = f"GC{p}"; n_sgt = f"sgate{p}"
                chi = col0 // 64
                if L == 64 and chi % 2 == 1 and chi >= 1 and (chi - 1) in zraw_valid:
                    P.op("act", lambda e: e.copy(out=zbuf[:, :, 1:65], in_=zraw[:, :, :]), reads=["zraw"], writes=["zbuf"])
                else:
                    two = (L == 64 and chi % 2 == 0 and chi + 1 < NCH)
                    W_ = 128 if two else 64
                    for q4 in range(5):
                        nb = HG if q4 < 4 else 2
                        bk = nbank()
                        for hh in range(nb):
                            blk = q4 * HG + hh
                            for kc in range(8):
                                P.op("pe", lambda e, bk=bk, hh=hh, blk=blk, kc=kc, W_=W_: e.matmul(ps[0:64, bk, hh * W_:(hh + 1) * W_], lhsT=wr[:, kc, blk * 64:(blk + 1) * 64], rhs=xnT[:, kc, col0:col0 + W_],
                                                                                             start=(kc == 0), stop=(kc == 7)), reads=["wr", "xnT"], writes=[("ps", bk)])
                        pv_ = ps[0:64, bk, 0:nb * W_].rearrange("p (h t) -> p h t", t=W_)
                        P.op("act", lambda e, q4=q4, nb=nb, pv_=pv_: e.copy(out=zbuf[:, q4 * HG:q4 * HG + nb, 1:65], in_=pv_[:, :, 0:64]), reads=[("ps", bk)], writes=["zbuf"])
                        if two:
                            P.op("act", lambda e, q4=q4, nb=nb, pv_=pv_: e.copy(out=zraw[:, q4 * HG:q4 * HG + nb, :], in_=pv_[:, :, 64:128]), reads=[("ps", bk)], writes=["zraw"])
                    if two:
                        zraw_valid.add(chi)
                if L == 4:
                    zv4 = zbuf[:, :, 1:65].rearrange("p q (b t) -> p q b t", t=4)
                    for q4 in range(5):
                        nb = HG if q4 < 4 else 2
                        sl_ = slice(q4 * HG, q4 * HG + nb)
                        P.op("pool", lambda e, sl_=sl_: e.tensor_copy(out=dsh[:, 0:sl_.stop - sl_.start, :].rearrange("p q (b t) -> p q b t", t=4)[:, :, :, 1:4], in_=zv4[:, sl_, :, 0:3]), reads=["zbuf"], writes=["dsh"])
                        P.op("pool", lambda e, sl_=sl_: e.tensor_copy(out=dsh[:, 0:sl_.stop - sl_.start, :].rearrange("p q (b t) -> p q b t", t=4)[:, :, :, 0], in_=shT[:, sl_, :]), reads=["shT"], writes=["dsh"])
                        n_ = sl_.stop - sl_.start
                        P.op("dve", lambda e, sl_=sl_, n_=n_: e.tensor_tensor(out=dsh[:, 0:n_, :], in0=dsh[:, 0:n_, :], in1=zbuf[:, sl_, 1:65], op=ALU.subtract), reads=["dsh", "zbuf"], writes=["dsh"])
                        P.op("dve", lambda e, sl_=sl_, n_=n_: e.tensor_tensor(out=dsh[:, 0:n_, :], in0=dsh[:, 0:n_, :], in1=mug[:, sl_].unsqueeze(2).to_broadcast([64, n_, 64]), op=ALU.mult), reads=["dsh", "mug"], writes=["dsh"])
                        P.op("dve", lambda e, sl_=sl_, n_=n_: e.tensor_tensor(out=zbuf[:, sl_, 1:65], in0=zbuf[:, sl_, 1:65], in1=dsh[:, 0:n_, :], op=ALU.add), reads=["dsh", "zbuf"], writes=["zbuf"])
                else:
                    for q4 in range(5):
                        nb = HG if q4 < 4 else 2
                        sl_ = slice(q4 * HG, q4 * HG + nb)
                        n_ = nb
                        P.op("dve", lambda e, sl_=sl_, n_=n_: e.tensor_tensor(out=dsh[:, 0:n_, :], in0=zbuf[:, sl_, 0:64], in1=zbuf[:, sl_, 1:65], op=ALU.subtract), reads=["zbuf"], writes=["dsh"])
                        P.op("dve", lambda e, sl_=sl_, n_=n_: e.tensor_tensor(out=dsh[:, 0:n_, :], in0=dsh[:, 0:n_, :], in1=mug[:, sl_].unsqueeze(2).to_broadcast([64, n_, 64]), op=ALU.mult), reads=["dsh", "mug"], writes=["dsh"])
                        P.op("pool", lambda e, sl_=sl_: e.tensor_copy(out=zbuf[:, sl_, 0:1], in_=zbuf[:, sl_, 64:65]), reads=["zbuf", "dsh"], writes=["zbuf"])
                        P.op("dve", lambda e, sl_=sl_, n_=n_: e.tensor_tensor(out=zbuf[:, sl_, 1:65], in0=zbuf[:, sl_, 1:65], in1=dsh[:, 0:n_, :], op=ALU.add), reads=["dsh", "zbuf"], writes=["zbuf"])
                zr = zbuf[:, 0:HG, 1:65]; zk = zbuf[:, HG:2 * HG, 1:65]; zv = zbuf[:, 2 * HG:3 * HG, 1:65]
                P.op("pool", lambda e: e.tensor_copy(out=vb[:, :, :], in_=zv), reads=["zbuf"], writes=[n_vb])
                P.op("act", lambda e: e.activation(out=tw[:, :], in_=zbuf[:, 4 * HG, 1:65], func=AF.Tanh), reads=["zbuf"], writes=["tw"])
                P.op("act", lambda e: e.copy(out=adc[:, :], in_=zbuf[:, 4 * HG + 1, 1:65]), reads=["zbuf"], writes=["adc"])
                bu = nbank(); ba = nbank()
                for h in range(HG):
                    P.op("pe", lambda e, h=h, bu=bu: e.matmul(ps[0:64, bu, h * 64:(h + 1) * 64], lhsT=w2g[:, h * 64:(h + 1) * 64], rhs=tw[:, :], start=True, stop=True), reads=["w2g", "tw"], writes=[("ps", bu)])
                for h in range(HG):
                    P.op("pe", lambda e, h=h, ba=ba: e.matmul(ps[0:64, ba, h * 64:(h + 1) * 64], lhsT=a2g[:, h * 64:(h + 1) * 64], rhs=adc[:, :], start=True, stop=True), reads=["a2g", "adc"], writes=[("ps", ba)])
                v3 = lambda bk_: ps[0:64, bk_, 0:HG * 64].rearrange("p (h t) -> p h t", t=64)
                P.op("dve", lambda e, bu=bu: e.tensor_tensor(out=t1[:, :, :], in0=v3(bu), in1=bc(prm[:, :, 0]), op=ALU.add), reads=[("ps", bu), "prm"], writes=["t1"])
                P.op("act", lambda e: e.activation(out=sg[:, :, :], in_=t1[:, :, :], func=AF.Sigmoid), reads=["t1"], writes=["sg"])
                P.op("dve", lambda e, ba=ba: e.tensor_tensor(out=t2[:, :, :], in0=v3(ba), in1=bc(prm[:, :, 1]), op=ALU.add), reads=[("ps", ba), "prm"], writes=["t2"])
                P.op("act", lambda e: e.activation(out=aa[:, :, :], in_=t2[:, :, :], func=AF.Sigmoid), reads=["t2"], writes=["aa"])
                f2 = lambda a_: a_[:, :, :].rearrange("p h t -> p (h t)")
                P.op("dve", lambda e: e.tensor_tensor_scan(out=f2(clr), data0=segm[:, sgi, :], data1=f2(sg), initial=0.0, op0=ALU.mult, op1=ALU.add), reads=["sg", "segm"], writes=["clr"])
                P.op("act", lambda e: e.activation(out=Ein[:, :, :], in_=clr[:, :, :], func=AF.Exp, scale=LD), reads=["clr"], writes=["Ein"])
                P.op("act", lambda e: e.activation(out=Einv[:, :, :], in_=clr[:, :, :], func=AF.Exp, scale=-LD), reads=["clr"], writes=["Einv"])
                P.op("pool", lambda e: e.tensor_tensor(out=t1[:, :, :], in0=clr[:, :, :], in1=sg[:, :, :], op=ALU.subtract), reads=["clr", "sg", "t1"], writes=["t1"])
                P.op("act", lambda e: e.activation(out=Eex[:, :, :], in_=t1[:, :, :], func=AF.Exp, scale=LD), reads=["t1"], writes=["Eex"])
                seg4 = lambda a_: a_[:, :, :].rearrange("p h (s l) -> p h s l", l=L)
                P.op("pool", lambda e: e.tensor_tensor(out=seg4(t2), in0=seg4(clr)[:, :, :, L - 1:L].to_broadcast([64, HG, nseg, L]), in1=seg4(clr), op=ALU.subtract), reads=["clr", "t2"], writes=["t2"])
                P.op("act", lambda e: e.activation(out=EC[:, :, :], in_=t2[:, :, :], func=AF.Exp, scale=LD), reads=["t2"], writes=["EC"])
                P.op("act", lambda e: e.activation(out=GC[:, :, 0:nseg], in_=seg4(clr)[:, :, :, L - 1], func=AF.Exp, scale=LD), reads=["clr"], writes=[n_gc])
                P.op("act", lambda e: e.activation(out=sgate[:, :, :], in_=zbuf[:, 3 * HG:4 * HG, 1:65], func=AF.Silu), reads=["zbuf"], writes=[n_sgt])
                P.op("pool", lambda e: e.tensor_tensor(out=kk_[:, :, :], in0=zk, in1=bc(prm[:, :, 2]), op=ALU.mult), reads=["zbuf", "prm"], writes=["kk_"])
                P.op("act", lambda e: e.activation(out=t1[:, :, :], in_=kk_[:, :, :], func=AF.Square), reads=["kk_", "t1"], writes=["t1"])
                bs = nbank()
                P.op("pe", lambda e, bs=bs: e.matmul(ps[0:64, bs, 0:HG * 64], lhsT=ones64[:, :], rhs=f2(t1), start=True, stop=True), reads=["ones64", "t1"], writes=[("ps", bs)])
                P.op("act", lambda e, bs=bs: e.activation(out=t2[:, :, :], in_=v3(bs), func=AF.Sqrt, bias=eps24[0:64, 0:1], scale=1.0), reads=[("ps", bs), "eps24", "t2"], writes=["t2"])
                P.op("dve", lambda e: e.reciprocal(out=t2[:, :, :], in_=t2[:, :, :]), reads=["t2"], writes=["t2"])
                P.op("dve", lambda e: e.tensor_tensor(out=kkn[:, :, :], in0=kk_[:, :, :], in1=t2[:, :, :], op=ALU.mult), reads=["kk_", "t2"], writes=["kkn"])
                P.op("dve", lambda e: e.scalar_tensor_tensor(out=t1[:, :, :], in0=aa[:, :, :], scalar=-1.0, in1=bc(prm[:, :, 3]), op0=ALU.add, op1=ALU.mult), reads=["aa", "prm", "t1"], writes=["t1"])
                P.op("dve", lambda e: e.scalar_tensor_tensor(out=kmod[:, :, :], in0=t1[:, :, :], scalar=1.0, in1=zk, op0=ALU.add, op1=ALU.mult), reads=["t1", "zbuf"], writes=["kmod"])
                P.op("dve", lambda e: e.scalar_tensor_tensor(out=at_[:, :, :], in0=kkn[:, :, :], scalar=-1.0, in1=Eex[:, :, :], op0=ALU.mult, op1=ALU.mult), reads=["kkn", "Eex"], writes=[n_at])
                P.op("pool", lambda e: e.tensor_tensor(out=t2[:, :, :], in0=kkn[:, :, :], in1=aa[:, :, :], op=ALU.mult), reads=["kkn", "aa", "t2"], writes=["t2"])
                P.op("pool", lambda e: e.tensor_tensor(out=bt_[:, :, :], in0=t2[:, :, :], in1=Einv[:, :, :], op=ALU.mult), reads=["t2", "Einv"], writes=[n_bt])
                P.op("pool", lambda e: e.tensor_tensor(out=bh_[:, :, :], in0=t2[:, :, :], in1=EC[:, :, :], op=ALU.mult), reads=["t2", "EC"], writes=[n_bh])
                P.op("dve", lambda e: e.tensor_tensor(out=kt_[:, :, :], in0=kmod[:, :, :], in1=Einv[:, :, :], op=ALU.mult), reads=["kmod", "Einv"], writes=[n_kt])
                P.op("pool", lambda e: e.tensor_tensor(out=kh_[:, :, :], in0=kmod[:, :, :], in1=EC[:, :, :], op=ALU.mult), reads=["kmod", "EC"], writes=[n_kh])
                P.op("dve", lambda e: e.tensor_tensor(out=rt_[:, :, :], in0=zr, in1=Ein[:, :, :], op=ALU.mult), reads=["zbuf", "Ein"], writes=[n_rt])
                P.op("pool", lambda e: e.tensor_tensor(out=t1[:, :, :], in0=zr, in1=kmod[:, :, :], op=ALU.mult), reads=["zbuf", "kmod", "t1"], writes=["t1"])
                P.op("pool", lambda e: e.tensor_tensor(out=t1[:, :, :], in0=t1[:, :, :], in1=bc(prm[:, :, 4]), op=ALU.mult), reads=["t1", "prm"], writes=["t1"])
                bb_ = nbank()
                P.op("pe", lambda e, bb_=bb_: e.matmul(ps[0:64, bb_, 0:HG * 64], lhsT=ones64[:, :], rhs=f2(t1), start=True, stop=True), reads=["ones64", "t1"], writes=[("ps", bb_)])
                P.op("dve", lambda e, bb_=bb_: e.tensor_tensor(out=bon[:, :, :], in0=v3(bb_), in1=zv, op=ALU.mult), reads=[("ps", bb_), "zbuf"], writes=[n_bon])

            mmv = lambda bk_, rows: ps[0:rows, bk_, 0:HG * 64].rearrange("p (h t) -> p h t", t=64)

            def mm8(bk_, rows, cols, lhs_fn, rhs_fn, rd):
                for h in range(HG):
                    P.op("pe", lambda e, h=h: e.matmul(mmv(bk_, rows)[:, h, 0:cols], lhsT=lhs_fn(h), rhs=rhs_fn(h), start=True, stop=True), reads=rd, writes=[("ps", bk_)])

            def indep(c0, C, nsq, p, gci, pa=None, q2=0):
                cs = slice(c0, c0 + C)
                pa = p if pa is None else pa
                at_ = at2[pa]; rt_ = rt2[pa]; Vt = Vt2[p]; Bt = Bt2[p]; Kt = Kt2[p]; TT = TT2[p]; AakT = Aak2[p]; ArbT = Arb2[p]; ArkT = Ark2[p]; GC = GC2[pa]
                n_at = f"at_{pa}"; n_rt = f"rt_{pa}"
                bt_ = bt2[q2]; kt_ = kt2[q2]; bh_ = bh2[q2]; kh_ = kh2[q2]; vb = vb2[q2]
                n_bt = f"bt_{q2}"; n_kt = f"kt_{q2}"; n_bh = f"bh_{q2}"; n_kh = f"kh_{q2}"; n_vb = f"vb{q2}"
                for si, (src, dst, nm) in enumerate(((vb, Vt, f"Vt{p}"), (bh_, Bt, f"Bt{p}"), (kh_, Kt, f"Kt{p}"))):
                    hs, cs_ = (0, si * 64) if si < 2 else (4, 0)
                    for h in range(HG):
                        P.op("pe", lambda e, h=h, src=src, hs=hs, cs_=cs_: e.transpose(psb[0:C, hs + h, cs_:cs_ + 64], src[:, h, cs], ident_b[0:64, 0:64]), reads=[n_vb, n_bh, n_kh, "ident_b"], writes=["psb"])
                    P.op("act", lambda e, dst=dst, hs=hs, cs_=cs_: e.copy(out=dst[0:C, :, :], in_=psb[0:C, hs:hs + HG, cs_:cs_ + 64]), reads=["psb"], writes=[nm])
                A_ = lambda a_: (lambda h: a_[:, h, cs])
                mb = lambda m_: m_[0:C, 0:C].unsqueeze(1).to_broadcast([C, HG, C])
                bk = nbank(); mm8(bk, C, C, A_(bt_), A_(at_), [n_bt, n_at])
                P.op("dve", lambda e, bk=bk: e.tensor_tensor(out=NT0[0:C, :, 0:C], in0=mmv(bk, C)[:, :, 0:C], in1=mb(MU), op=ALU.mult), reads=[("ps", bk), "MU"], writes=["NT0"])
                bk = nbank(); mm8(bk, C, C, A_(at_), A_(bt_), [n_bt, n_at])
                P.op("dve", lambda e, bk=bk: e.tensor_tensor(out=N0[0:C, :, 0:C], in0=mmv(bk, C)[:, :, 0:C], in1=mb(ML), op=ALU.mult), reads=[("ps", bk), "ML"], writes=["N0"])
                for (lf, rf, dst, nm, msk, mn) in ((kt_, at_, AakT, f"AakT{p}", MU, "MU"), (bt_, rt_, ArbT, f"ArbT{p}", MUI, "MUI"), (kt_, rt_, ArkT, f"ArkT{p}", MUI, "MUI")):
                    bk = nbank(); mm8(bk, C, C, A_(lf), A_(rf), [n_kt, n_bt, n_at, n_rt])
                    P.op("dve", lambda e, bk=bk, dst=dst, msk=msk: e.tensor_tensor(out=dst[0:C, :, 0:C], in0=mmv(bk, C)[:, :, 0:C], in1=mb(msk), op=ALU.mult), reads=[("ps", bk), mn], writes=[nm])
                Rs = [(R0, "R0"), (R1, "R1")]
                Ns = [(N0, "N0"), (N1, "N1")]; NTs_ = [(NT0, "NT0"), (NT1, "NT1")]
                rdst0 = TT if nsq == 0 else R0
                P.op("pool", lambda e, rdst0=rdst0: e.tensor_tensor(out=rdst0[0:C, :, 0:C], in0=NT0[0:C, :, 0:C], in1=ident[0:C, 0:C].unsqueeze(1).to_broadcast([C, HG, C]), op=ALU.add),
                     reads=["NT0", "ident"], writes=[f"TT{p}" if nsq == 0 else "R0"])
                for k in range(1, nsq + 1):
                    (Nc, Ncn), (Nn, Nnn) = Ns[(k - 1) % 2], Ns[k % 2]
                    (NTc, NTcn), (NTn, NTnn) = NTs_[(k - 1) % 2], NTs_[k % 2]
                    (Rc, Rcn) = Rs[(k - 1) % 2]
                    (Rn, Rnn) = (TT, f"TT{p}") if k == nsq else Rs[k % 2]
                    bk = nbank()
                    mm8(bk, C, C, lambda h, NTc=NTc: NTc[0:C, h, 0:C], lambda h, Nc=Nc: Nc[0:C, h, 0:C], [NTcn, Ncn])
                    P.op("act", lambda e, bk=bk, Nn=Nn: e.copy(out=Nn[0:C, :, 0:C], in_=mmv(bk, C)[:, :, 0:C]), reads=[("ps", bk)], writes=[Nnn])
                    if k < nsq:
                        bk = nbank()
                        mm8(bk, C, C, lambda h, Nc=Nc: Nc[0:C, h, 0:C], lambda h, NTc=NTc: NTc[0:C, h, 0:C], [NTcn, Ncn])
                        P.op("act", lambda e, bk=bk, NTn=NTn: e.copy(out=NTn[0:C, :, 0:C], in_=mmv(bk, C)[:, :, 0:C]), reads=[("ps", bk)], writes=[NTnn])
                    bk = nbank()
                    mm8(bk, C, C, lambda h, Nn=Nn: Nn[0:C, h, 0:C], lambda h, Rc=Rc: Rc[0:C, h, 0:C], [Nnn, Rcn])
                    P.op("dve", lambda e, bk=bk, Rc=Rc, Rn=Rn: e.tensor_tensor(out=Rn[0:C, :, 0:C], in0=mmv(bk, C)[:, :, 0:C], in1=Rc[0:C, :, 0:C], op=ALU.add), reads=[("ps", bk), Rcn], writes=[Rnn])

            def dep(c0, C, p, pa=None, gci=0):
                cs = slice(c0, c0 + C)
                pa = p if pa is None else pa
                at_ = at2[pa]; rt_ = rt2[pa]; Vt = Vt2[p]; Bt = Bt2[p]; Kt = Kt2[p]; TT = TT2[p]; AakT = Aak2[p]; ArbT = Arb2[p]; ArkT = Ark2[p]; GC = GC2[pa]
                n_at = f"at_{pa}"; n_rt = f"rt_{pa}"
                bk = nbank()
                for h in range(HG):
                    P.op("pe", lambda e, h=h, bk=bk: e.matmul(mmv(bk, C)[:, h, :], lhsT=at_[:, h, cs], rhs=Pb[:, h, :], start=True, stop=False), reads=[n_at, "Pb"], writes=[("ps", bk)])
                    P.op("pe", lambda e, h=h, bk=bk: e.matmul(mmv(bk, C)[:, h, :], lhsT=AakT[0:C, h, 0:C], rhs=Vt[0:C, h, :], start=False, stop=True), reads=[f"AakT{p}", f"Vt{p}"], writes=[("ps", bk)])
                P.op("act", lambda e, bk=bk: e.copy(out=U0[0:C, :, :], in_=mmv(bk, C)), reads=[("ps", bk)], writes=["U0"])
                bk = nbank()
                mm8(bk, C, 64, lambda h: TT[0:C, h, 0:C], lambda h: U0[0:C, h, :], [f"TT{p}", "U0"])
                P.op("dve", lambda e, bk=bk: e.tensor_copy(out=Uf[0:C, :, :], in_=mmv(bk, C)), reads=[("ps", bk)], writes=["Uf"])
                bk = nbank(); bk2 = nbank()
                for h in range(HG):
                    P.op("pe", lambda e, h=h, bk=bk: e.matmul(mmv(bk, 64)[:, h, 0:C], lhsT=Pb[:, h, :], rhs=rt_[:, h, cs], start=True, stop=False), reads=["Pb", n_rt], writes=[("ps", bk)])
                    P.op("pe", lambda e, h=h, bk=bk: e.matmul(mmv(bk, 64)[:, h, 0:C], lhsT=Uf[0:C, h, :], rhs=ArbT[0:C, h, 0:C], start=False, stop=False), reads=["Uf", f"ArbT{p}"], writes=[("ps", bk)])
                    P.op("pe", lambda e, h=h, bk=bk: e.matmul(mmv(bk, 64)[:, h, 0:C], lhsT=Vt[0:C, h, :], rhs=ArkT[0:C, h, 0:C], start=False, stop=True), reads=[f"Vt{p}", f"ArkT{p}"], writes=[("ps", bk)])
                for h in range(HG):
                    P.op("pe", lambda e, h=h, bk2=bk2: e.matmul(mmv(bk2, 64)[:, h, :], lhsT=Bt[0:C, h, :], rhs=Uf[0:C, h, :], start=True, stop=False), reads=[f"Bt{p}", "Uf"], writes=[("ps", bk2)])
                    P.op("pe", lambda e, h=h, bk2=bk2: e.matmul(mmv(bk2, 64)[:, h, :], lhsT=Kt[0:C, h, :], rhs=Vt[0:C, h, :], start=False, stop=True), reads=[f"Kt{p}", f"Vt{p}"], writes=[("ps", bk2)])
                P.op("dve", lambda e: e.tensor_tensor(out=Pst[:, :, :], in0=Pst[:, :, :], in1=GC[:, :, gci:gci + 1].to_broadcast([64, HG, 64]), op=ALU.mult), reads=["Pst", f"GC{pa}"], writes=["Pst"])
                P.op("dve", lambda e, bk2=bk2: e.tensor_tensor(out=Pst[:, :, :], in0=Pst[:, :, :], in1=mmv(bk2, 64), op=ALU.add), reads=[("ps", bk2), "Pst"], writes=["Pst"])
                P.op("act", lambda e: e.copy(out=Pb[:, :, :], in_=Pst[:, :, :]), reads=["Pst"], writes=["Pb"])
                P.op("dve", lambda e, bk=bk: e.tensor_copy(out=outT[:, :, cs], in_=mmv(bk, 64)[:, :, 0:C]), reads=[("ps", bk)], writes=["outT"])

            def finish_batch(col0, p, G=G):
                bon = bon2[p]; sgate = sg2[p]
                f2 = lambda a_: a_[:, :, :].rearrange("p h t -> p (h t)")
                v3 = lambda bk_: ps[0:64, bk_, 0:HG * 64].rearrange("p (h t) -> p h t", t=64)
                b1 = nbank()
                P.op("pe", lambda e, b1=b1: e.matmul(ps[0:64, b1, 0:HG * 64], lhsT=ones64[:, :], rhs=f2(outT), start=True, stop=True), reads=["ones64", "outT"], writes=[("ps", b1)])
                P.op("dve", lambda e, b1=b1: e.scalar_tensor_tensor(out=f1[:, :, :], in0=v3(b1), scalar=-1.0 / 64, in1=outT[:, :, :], op0=ALU.mult, op1=ALU.add), reads=[("ps", b1), "outT", "f1"], writes=["f1"])
                P.op("act", lambda e: e.activation(out=f2_[:, :, :], in_=f1[:, :, :], func=AF.Square), reads=["f1", "f2_"], writes=["f2_"])
                b2 = nbank()
                P.op("pe", lambda e, b2=b2: e.matmul(ps[0:64, b2, 0:HG * 64], lhsT=ones64[:, :], rhs=f2(f2_), start=True, stop=True), reads=["ones64", "f2_"], writes=[("ps", b2)])
                P.op("act", lambda e, b2=b2: e.activation(out=f2_[:, :, :], in_=v3(b2), func=AF.Sqrt, bias=epsln[0:64, 0:1], scale=1.0 / 64), reads=[("ps", b2), "epsln", "f2_"], writes=["f2_"])
                P.op("dve", lambda e: e.reciprocal(out=f2_[:, :, :], in_=f2_[:, :, :]), reads=["f2_"], writes=["f2_"])
                P.op("dve", lambda e: e.tensor_tensor(out=f1[:, :, :], in0=f1[:, :, :], in1=f2_[:, :, :], op=ALU.mult), reads=["f1", "f2_"], writes=["f1"])
                P.op("pool", lambda e: e.tensor_tensor(out=f1[:, :, :], in0=f1[:, :, :], in1=bc(prm[:, :, 5]), op=ALU.mult), reads=["f1", "prm"], writes=["f1"])
                P.op("pool", lambda e: e.tensor_tensor(out=f1[:, :, :], in0=f1[:, :, :], in1=bc(prm[:, :, 6]), op=ALU.add), reads=["f1", "prm"], writes=["f1"])
                P.op("pool", lambda e: e.tensor_tensor(out=f1[:, :, :], in0=f1[:, :, :], in1=bon[:, :, :], op=ALU.add), reads=["f1", f"bon{p}"], writes=["f1"])
                ev = lambda a_: a_[:, :, :].rearrange("p (q e) t -> p q e t", e=2)
                P.op("dve", lambda e: e.tensor_tensor(out=oaT[0:64, (HG // 2) * G:(HG // 2) * (G + 1), col0:col0 + 64], in0=ev(f1)[:, :, 0, :], in1=ev(sgate)[:, :, 0, :], op=ALU.mult), reads=["f1", f"sgate{p}"], writes=["oaT"])
                P.op("dve", lambda e: e.tensor_tensor(out=oodd[:, :, :], in0=ev(f1)[:, :, 1, :], in1=ev(sgate)[:, :, 1, :], op=ALU.mult), reads=["f1", f"sgate{p}"], writes=["oodd"])
                P.dma("sp", lambda e: e.dma_start(out=oaT[64:128, (HG // 2) * G:(HG // 2) * (G + 1), col0:col0 + 64], in_=oodd[:, :, :]), reads=["oodd"], writes=["oaT"], semkey="oodd")

            def TB(c):
                bank_grp[0] = "tb"; token_batch(c * 64, 64, 1, c % 3, c % 2); bank_grp[0] = None

            def IND(c):
                bank_grp[0] = "ind"; indep(0, 64, 5, c % 2, 0, pa=c % 3, q2=c % 2); bank_grp[0] = None

            def DEPF(c):
                bank_grp[0] = "dep"; dep(0, 64, c % 2, pa=c % 3); finish_batch(c * 64, c % 3); bank_grp[0] = None

            if NCH > 0:
                TB(0); IND(0)
            if NCH > 1:
                TB(1)
            for ch in range(NCH):
                names = []
                if ch + 2 < NCH:
                    P.thread("tb"); TB(ch + 2); names.append("tb")
                if ch + 1 < NCH:
                    P.thread("ind"); IND(ch + 1); names.append("ind")
                P.thread("dep"); DEPF(ch); names.append("dep")
                P.merge(names)
            bk = nbank()
            for h in range(HG):
                P.op("pe", lambda e, h=h, bk=bk: e.transpose(ps[0:64, bk, h * 64:(h + 1) * 64], Pst[:, h, :], ident[0:64, 0:64]), reads=["Pst", "ident"], writes=[("ps", bk)])
            P.op("act", lambda e, bk=bk: e.copy(out=Ssb[:, :, :], in_=ps[0:64, bk, 0:HG * 64].rearrange("p (h t) -> p h t", t=64)), reads=[("ps", bk)], writes=["Ssb"])
            P.dma("sp", lambda e, G=G: e.dma_start(out=wkv_p[HG * G:HG * G + HG, :, :].rearrange("h i j -> i h j"), in_=Ssb[:, :, :]), reads=["Ssb"], store=True, semkey="wkvp")
            if NSEQ > 0:
                token_batch(2048, 4, 16, 0, 0)
            for bq in range(NSEQ):
                p = bq % 2
                indep(4 * bq, 4, 1, p, bq, pa=0)
                P.dma("sp", lambda e, bq=bq, G=G: e.dma_start(out=Ssb[:, :, :], in_=swkv_d[bq, HG * G:HG * G + HG, :, :].rearrange("h i j -> i h j")), writes=["Ssb"], semkey="ssb")
                bk = nbank()
                for h in range(HG):
                    P.op("pe", lambda e, h=h, bk=bk: e.transpose(ps[0:64, bk, h * 64:(h + 1) * 64], Ssb[:, h, :], ident[0:64, 0:64]), reads=["Ssb", "ident"], writes=[("ps", bk)])
                P.op("act", lambda e, bk=bk: e.copy(out=Pst[:, :, :], in_=ps[0:64, bk, 0:HG * 64].rearrange("p (h t) -> p h t", t=64)), reads=[("ps", bk)], writes=["Pst"])
                P.op("act", lambda e: e.copy(out=Pb[:, :, :], in_=Pst[:, :, :]), reads=["Pst"], writes=["Pb"])
                dep(4 * bq, 4, p, pa=0, gci=bq)
                bk = nbank()
                for h in range(HG):
                    P.op("pe", lambda e, h=h, bk=bk: e.transpose(ps[0:64, bk, h * 64:(h + 1) * 64], Pst[:, h, :], ident[0:64, 0:64]), reads=["Pst", "ident"], writes=[("ps", bk)])
                P.op("act", lambda e, bk=bk: e.copy(out=Ssb[:, :, :], in_=ps[0:64, bk, 0:HG * 64].rearrange("p (h t) -> p h t", t=64)), reads=[("ps", bk)], writes=["Ssb"])
                P.dma("sp", lambda e, bq=bq, G=G: e.dma_start(out=wkv_s[bq, HG * G:HG * G + HG, :, :].rearrange("h i j -> i h j"), in_=Ssb[:, :, :]), reads=["Ssb"], store=True, semkey="wkvs")
            if NSEQ > 0:
                finish_batch(2048, 0)
        if dbg_oa is not None:
            P.dma("sp", lambda e: e.dma_start(out=dbg_oa, in_=oaT[:, :, :]), reads=["oaT"], store=True)
        A.release(mR); P.barrier()
        sgm = A.alloc([128, 2, 512]); mtmp = A.alloc([128, 512])

        def merge_branch(srcT, srck, wproj, gcol0, first, sgm, mtmp):
            for cb in range(8):
                slg = load_w([(0, gcol0 + cb * 128, 128)], 128)
                slp = load_w([(0, cb * 128, 128)], 128, src=wproj)
                for ni, (n0, nn) in enumerate(NTL):
                    bg = nbank(); bp = nbank(); ssl = ni % 2
                    proj_fm(slg, 128, n0, nn, bg)
                    P.op("act", lambda e, bg=bg, nn=nn, ssl=ssl, sgm=sgm: e.activation(out=sgm[:, ssl, 0:nn], in_=ps[:, bg, 0:nn], func=AF.Sigmoid), reads=[("ps", bg)], writes=[("sgm", ssl)])
                    for kc in range(8):
                        P.op("pe", lambda e, kc=kc, bp=bp, slp=slp, n0=n0, nn=nn: e.matmul(ps[:, bp, 0:nn], lhsT=wb[:, slp, kc, 0:128], rhs=srcT[:, kc, n0:n0 + nn], start=(kc == 0), stop=(kc == 7)),
                             reads=[("wb", slp)] + srck, writes=[("ps", bp)])
                    if first:
                        P.op("dve", lambda e, bp=bp, nn=nn, ssl=ssl, cb=cb, n0=n0, sgm=sgm: e.tensor_tensor(out=mT[:, cb, n0:n0 + nn], in0=ps[:, bp, 0:nn], in1=sgm[:, ssl, 0:nn], op=ALU.mult),
                             reads=[("ps", bp), ("sgm", ssl)], writes=["mT"])
                    else:
                        P.op("dve", lambda e, bp=bp, nn=nn, ssl=ssl, sgm=sgm, mtmp=mtmp: e.tensor_tensor(out=mtmp[:, 0:nn], in0=ps[:, bp, 0:nn], in1=sgm[:, ssl, 0:nn], op=ALU.mult),
                             reads=[("ps", bp), ("sgm", ssl)], writes=["mtmp"])
                        P.op("dve", lambda e, nn=nn, cb=cb, n0=n0, mtmp=mtmp: e.tensor_tensor(out=mT[:, cb, n0:n0 + nn], in0=mT[:, cb, n0:n0 + nn], in1=mtmp[:, 0:nn], op=ALU.add),
                             reads=["mtmp", "mT"], writes=["mT"])

        merge_branch(oaT, ["oaT"], w_pa, C_GA, True, sgm, mtmp)
        A.release(mA); P.barrier()
        if STAGE < 3:
            return finish()
        KTd = A.alloc([128, 2, NT], BF16)
        Vaug = A.alloc([128, 17, 2, 66], BF16)
        wi_t = A.alloc([128, 17, 8])
        m1 = A.mark()
        knT = A.alloc([128, NT]); sq = A.alloc([128, 512]); rs = A.alloc([128, 512])
        otok = A.alloc([128, 2, 128]); vtok = A.alloc([128, 2, 128]); kitok = A.alloc([128, 2, 64])
        xl = A.alloc([128, 8, 17], BF16); shrow = A.alloc([17, 2, 128])

        cur = {"sq": sq, "rs": rs}

        def normed_block(parts, dst_fn, scale_ap, key):
            sq, rs = cur["sq"], cur["rs"]
            sl = load_w(parts, 128)
            for (n0, nn) in NTL:
                b = nbank()
                proj_fm(sl, 128, n0, nn, b)
                P.op("act", lambda e, b=b, nn=nn, sq=sq: e.activation(out=sq[:, 0:nn], in_=ps[:, b, 0:nn], func=AF.Square), reads=[("ps", b)], writes=["sq"])
                b2 = nbank()
                P.op("pe", lambda e, b2=b2, nn=nn, sq=sq: e.matmul(ps[:, b2, 0:nn], lhsT=bones[:, :], rhs=sq[:, 0:nn], start=True, stop=True), reads=["bones", "sq"], writes=[("ps", b2)])
                P.op("act", lambda e, b2=b2, nn=nn, rs=rs: e.activation(out=rs[:, 0:nn], in_=ps[:, b2, 0:nn], func=AF.Sqrt, bias=eps6[:, 0:1], scale=1.0 / 64), reads=[("ps", b2), "eps6"], writes=["rs"])
                P.op("dve", lambda e, nn=nn, rs=rs: e.reciprocal(out=rs[:, 0:nn], in_=rs[:, 0:nn]), reads=["rs"], writes=["rs"])
                P.op("dve", lambda e, b=b, n0=n0, nn=nn, rs=rs: e.scalar_tensor_tensor(out=dst_fn(n0, nn), in0=ps[:, b, 0:nn], scalar=scale_ap, in1=rs[:, 0:nn], op0=ALU.mult, op1=ALU.mult), reads=[("ps", b), "rs"], writes=[key])

        normed_block([(0, C_AK, 128)], lambda n0, nn: knT[:, n0:n0 + nn], gk2[:, 0:1], "knT")
        for kvh in range(2):
            normed_block([(0, C_AK + kvh * 64, 64), (64, C_AK + kvh * 64, 64)], lambda n0, nn, kvh=kvh: KTd[:, kvh, n0:n0 + nn], gkq[:, 0:1], ("KTd", kvh))
        for ti, (t0, n) in enumerate(TT):
            b = nbank(); sl = ti % 2
            P.op("pe", lambda e, b=b, t0=t0, n=n: e.transpose(ps[0:n, b, 0:128], knT[:, t0:t0 + n], ident[:, :]), reads=["knT", "ident"], writes=[("ps", b)])
            P.op("act", lambda e, b=b, n=n, sl=sl: e.copy(out=otok[0:n, sl, :], in_=ps[0:n, b, 0:128]), reads=[("ps", b)], writes=[("otok", sl)])
            dst = k_p[t0:t0 + n, :] if ti < 16 else k_s[:, :]
            P.dma("sp", lambda e, dst=dst, n=n, sl=sl: e.dma_start(out=dst, in_=otok[0:n, sl, :]), reads=[("otok", sl)], store=True)
        P.op("dve", lambda e: e.memset(Vaug[:, :, :, :], 1.0), writes=[("Vaug", ti) for ti in range(17)])
        sl_v = load_w([(0, C_AV, 128)], 128)
        for ti, (t0, n) in enumerate(TT):
            b = nbank(); sl = ti % 2
            proj_tm(sl_v, 128, t0, n, b)
            P.op("act", lambda e, b=b, n=n, sl=sl: e.copy(out=vtok[0:n, sl, :], in_=ps[0:n, b, 0:128]), reads=[("ps", b)], writes=[("vtok", sl)])
            P.op("act", lambda e, b=b, n=n, ti=ti: e.copy(out=Vaug[0:n, ti, :, 0:64], in_=ps[0:n, b, 0:128].rearrange("p (k d) -> p k d", k=2)), reads=[("ps", b)], writes=[("Vaug", ti)])
            dst = v_p[t0:t0 + n, :] if ti < 16 else v_s[:, :]
            P.dma("sp", lambda e, dst=dst, n=n, sl=sl: e.dma_start(out=dst, in_=vtok[0:n, sl, :]), reads=[("vtok", sl)], store=True)
        sl_k = load_w([(0, C_KI, 72)], 72)
        for ti, (t0, n) in enumerate(TT):
            b = nbank(); sl = ti % 2
            proj_tm(sl_k, 72, t0, n, b)
            P.op("act", lambda e, b=b, n=n, sl=sl: e.copy(out=kitok[0:n, sl, :], in_=ps[0:n, b, 0:64]), reads=[("ps", b)], writes=[("kitok", sl)])
            P.op("act", lambda e, b=b, n=n, ti=ti: e.copy(out=wi_t[0:n, ti, :], in_=ps[0:n, b, 64:72]), reads=[("ps", b)], writes=[("wi_t", ti)])
            dst = ki_p[t0:t0 + n, :] if ti < 16 else ki_s[:, :]
            P.dma("sp", lambda e, dst=dst, n=n, sl=sl: e.dma_start(out=dst, in_=kitok[0:n, sl, :]), reads=[("kitok", sl)], store=True)
        P.op("dve", lambda e: e.tensor_copy(out=xl[:, :, 0:1], in_=xnT[:, :, 2047:2048]), reads=["xnT"], writes=["xl"])
        P.op("dve", lambda e: e.tensor_copy(out=xl[:, :, 1:17], in_=xnT[:, :, 2048:2112].rearrange("p k (b t) -> p k b t", t=4)[:, :, :, 3]), reads=["xnT"], writes=["xl"])
        for cb in range(33):
            sl = load_w([(0, cb * 128, 128)], 128)
            b = nbank(); ssl = cb % 2
            for kc in range(8):
                P.op("pe", lambda e, kc=kc, sl=sl, b=b: e.matmul(ps[0:17, b, 0:128], lhsT=xl[:, kc, :], rhs=wb[:, sl, kc, 0:128], start=(kc == 0), stop=(kc == 7)),
                     reads=[("wb", sl), "xl"], writes=[("ps", b)])
            P.op("act", lambda e, b=b, ssl=ssl: e.copy(out=shrow[:, ssl, :], in_=ps[0:17, b, 0:128]), reads=[("ps", b)], writes=[("shrow", ssl)])
            P.dma("sp", lambda e, cb=cb, ssl=ssl: e.dma_start(out=sh_p[:, cb * 128:(cb + 1) * 128], in_=shrow[0:1, ssl, :]), reads=[("shrow", ssl)], store=True, semkey=("shp", ssl))
            P.dma("sp", lambda e, cb=cb, ssl=ssl: e.dma_start(out=sh_s[:, cb * 128:(cb + 1) * 128], in_=shrow[1:17, ssl, :]), reads=[("shrow", ssl)], store=True, semkey=("shs", ssl))
        A.release(m1); P.barrier()
        if STAGE < 6:
            return finish()

        m2 = A.mark()
        qT = A.alloc([128, 8, NT], BF16)
        obT = qT
        QK = [("qT", p) for p in range(8)]
        mq = A.mark()
        cur["sq"] = A.alloc([128, 512]); cur["rs"] = A.alloc([128, 512])
        for p8 in range(8):
            normed_block([(0, C_Q + p8 * 128, 128)], lambda n0, nn, p8=p8: qT[:, p8, n0:n0 + nn], 1.0, ("qT", p8))
        A.release(mq); P.barrier()
        qiTs = A.alloc([128, 4, 64], BF16); kiT2s = A.alloc([128, 64], BF16)
        relb = A.alloc([32, 16]); rb31 = A.alloc([32, 1, 16]); rbd = A.alloc([32, 16]); rbt = A.alloc([32, 16])
        rbh = A.alloc([32, 16], BF16); rbl = A.alloc([32, 16], BF16)
        negI = A.alloc([128, 128], BF16)
        m3 = A.mark()
        qiT = A.alloc([128, 4, NT], BF16)
        for p4 in range(4):
            sl = load_w([(0, C_QI + p4 * 128, 128)], 128)
            for (n0, nn) in NTL:
                b = nbank(); proj_fm(sl, 128, n0, nn, b)
                P.op("act", lambda e, b=b, p4=p4, n0=n0, nn=nn: e.copy(out=qiT[:, p4, n0:n0 + nn], in_=ps[:, b, 0:nn]), reads=[("ps", b)], writes=[("qiT", p4)])
        kiT2 = A.alloc([128, NT], BF16)
        sl = load_w([(0, C_KI, 64), (64, C_KI, 64)], 128)
        for (n0, nn) in NTL:
            b = nbank(); proj_fm(sl, 128, n0, nn, b)
            P.op("act", lambda e, b=b, n0=n0, nn=nn: e.copy(out=kiT2[:, n0:n0 + nn], in_=ps[:, b, 0:nn]), reads=[("ps", b)], writes=["kiT2"])
        P.op("act", lambda e: e.copy(out=qiTs[:, :, :], in_=qiT[:, :, 2048:2112]), reads=[("qiT", p) for p in range(4)], writes=["qiTs"])
        P.op("act", lambda e: e.copy(out=kiT2s[:, :], in_=kiT2[:, 2048:2112]), reads=["kiT2"], writes=["kiT2s"])
        P.dma("sp", lambda e: e.dma_start(out=relb[:, :], in_=rel_bias_d), writes=["relb"])
        P.dma("sp", lambda e: e.dma_start(out=rb31[:, :, :], in_=rel_bias_d[31:32, :].partition_broadcast(32)), writes=["rb31"])
        P.op("dve", lambda e: e.tensor_tensor(out=rbd[:, :], in0=relb[:, :], in1=rb31[:, 0, :], op=ALU.subtract), reads=["relb", "rb31"], writes=["rbd"])
        P.op("dve", lambda e: e.tensor_copy(out=rbh[:, :], in_=rbd[:, :]), reads=["rbd"], writes=["rbh"])
        P.op("dve", lambda e: e.tensor_copy(out=rbt[:, :], in_=rbh[:, :]), reads=["rbh"], writes=["rbt"])
        P.op("dve", lambda e: e.tensor_tensor(out=rbl[:, :], in0=rbd[:, :], in1=rbt[:, :], op=ALU.subtract), reads=["rbd", "rbt"], writes=["rbl"])
        TzT = A.alloc([128, 2, 128, 16])
        caus0 = A.alloc([128, 128]); causS = A.alloc([128, 128])
        acc = A.alloc([128, 2048]); work = A.alloc([128, 2048]); relu_t = A.alloc([128, 2, 512])
        ns2 = [A.alloc([128, 2048], BF16), A.alloc([128, 2048], BF16)]; mx = A.alloc([128, 8])
        PT = A.alloc([128, 2, 8, 128], BF16); tmpl = A.alloc([128, 8, 128])
        obn = A.alloc([128, 16, 64], BF16); rec = A.alloc([128, 16])
        ohst = work.bitcast(BF16)[0:32, 0:4096].rearrange("p (s q k) -> p s q k", s=2, q=16)
        P.dma("sp", lambda e: e.dma_start(out=caus0[:, :], in_=caus0_d), writes=["caus0"])
        P.dma("sp", lambda e: e.dma_start(out=causS[:, :], in_=causS_d), writes=["causS"])
        P.op("act", lambda e: e.mul(out=negI[:, :], in_=ident[:, :], mul=-30000.0), reads=["ident"], writes=["negI"])
        for ty in range(2):
            for qc in range(8):
                slot = (ty * 8 + qc) % 2
                bank = 3 + slot
                P.dma("sp", lambda e, ty=ty, qc=qc, slot=slot: e.dma_start(out=ohst[:, slot, :, :], in_=oh_d[:, ty, qc * 16:(qc + 1) * 16, :]), writes=["work"], semkey=("oh", slot))
                for ql in range(16):
                    P.op("pe", lambda e, slot=slot, bank=bank, ql=ql: e.matmul(ps[:, bank, ql * 16:(ql + 1) * 16], lhsT=ohst[:, slot, ql, :], rhs=rbh[:, :], start=True, stop=False),
                         reads=["work", "rbh"], writes=[("ps", bank)])
                    P.op("pe", lambda e, slot=slot, bank=bank, ql=ql: e.matmul(ps[:, bank, ql * 16:(ql + 1) * 16], lhsT=ohst[:, slot, ql, :], rhs=rbl[:, :], start=False, stop=True),
                         reads=["work", "rbl"], writes=[("ps", bank)])
                if ty == 0:
                    P.op("dve", lambda e, bank=bank, qc=qc: e.tensor_tensor(out=TzT[:, 0, qc * 16:(qc + 1) * 16, :], in0=ps[:, bank, 0:256].rearrange("s (q h) -> s q h", h=16),
                                                                            in1=caus0[:, qc * 16:(qc + 1) * 16].unsqueeze(2).to_broadcast([128, 16, 16]), op=ALU.add),
                         reads=[("ps", bank), "caus0"], writes=["TzT"])
                else:
                    P.op("dve", lambda e, bank=bank, qc=qc: e.tensor_copy(out=TzT[:, 1, qc * 16:(qc + 1) * 16, :], in_=ps[:, bank, 0:256].rearrange("s (q h) -> s q h", h=16)),
                         reads=[("ps", bank)], writes=["TzT"])

        NQB = int(os.environ.get("NQB", "16"))
        PV_FIRST = (0, 7, 14)
        icnt = [0]
        def idx_topk(i):
            q0 = i * 128
            notsel = ns2[i % 2]; nsn = f"notsel{i % 2}"
            nk = 128 * (i + 1)
            for c0 in range(0, nk, 512):
                w = min(512, nk - c0)
                ck = ("acc", c0 // 512)
                for h in range(8):
                    half, p4 = h % 2, h // 2
                    bank = 3 + (icnt[0] % 4); rsl = icnt[0] % 2; icnt[0] += 1
                    P.op("pe", lambda e, bank=bank, w=w, half=half, p4=p4, q0=q0, c0=c0: e.matmul(ps[:, bank, 0:w], lhsT=qiT[half * 64:(half + 1) * 64, p4, q0:q0 + 128],
                                                                                              rhs=kiT2[half * 64:(half + 1) * 64, c0:c0 + w], start=True, stop=True),
                         reads=[("qiT", p4), "kiT2"], writes=[("ps", bank)])
                    P.op("act", lambda e, bank=bank, w=w, rsl=rsl: e.activation(out=relu_t[:, rsl, 0:w], in_=ps[:, bank, 0:w], func=AF.Relu), reads=[("ps", bank)], writes=[("relu_t", rsl)])
                    if h == 0:
                        P.op("dve", lambda e, w=w, rsl=rsl, c0=c0, i=i: e.tensor_scalar(out=acc[:, c0:c0 + w], in0=relu_t[:, rsl, 0:w], scalar1=wi_t[:, i, 0:1], scalar2=None, op0=ALU.mult),
                             reads=[("relu_t", rsl), ("wi_t", i)], writes=[ck])
                    else:
                        P.op("dve", lambda e, w=w, rsl=rsl, c0=c0, i=i, h=h: e.scalar_tensor_tensor(out=acc[:, c0:c0 + w], in0=relu_t[:, rsl, 0:w], scalar=wi_t[:, i, h:h + 1], in1=acc[:, c0:c0 + w],
                                                                                                 op0=ALU.mult, op1=ALU.add),
                             reads=[("relu_t", rsl), ("wi_t", i), ck], writes=[ck])
            acck = [("acc", c) for c in range((nk + 511) // 512)]
            P.op("dve", lambda e, i=i: e.tensor_tensor(out=acc[:, i * 128:(i + 1) * 128], in0=acc[:, i * 128:(i + 1) * 128], in1=causS[:, :], op=ALU.add),
                 reads=acck + ["causS"], writes=acck)
            masked = i >= 2
            if masked:
                for r in range(32):
                    src = acc if r == 0 else work
                    P.op("dve", lambda e, src=src, nk=nk: e.max(out=mx[:, :], in_=src[:, 0:nk]), reads=acck + ["work"], writes=["mx"])
                    if r < 31:
                        P.op("dve", lambda e, src=src, nk=nk: e.match_replace(out=work[:, 0:nk], in_to_replace=mx[:, :], in_values=src[:, 0:nk], imm_value=-1e30),
                             reads=acck + ["mx", "work"], writes=["work"])
                P.op("dve", lambda e, nk=nk: e.tensor_scalar(out=notsel[:, 0:nk], in0=acc[:, 0:nk], scalar1=mx[:, 7:8], scalar2=None, op0=ALU.is_lt),
                     reads=acck + ["mx"], writes=[nsn])
        def attend(i):
            q0 = i * 128
            masked = i >= 2
            notsel = ns2[i % 2]; nsn = f"notsel{i % 2}"
            for j in range(i + 1):
                ty = i - j
                for g in range(2):
                    slot = g
                    bA, bB = (3, 4) if slot == 0 else (5, 6)
                    for hh in range(8):
                        half, p4 = hh // 4, hh % 4
                        pair = g * 4 + p4
                        bank = bA if half == 0 else bB
                        P.op("pe", lambda e, bank=bank, p4=p4, half=half, g=g, j=j, pair=pair, q0=q0, masked=masked: e.matmul(
                            ps[:, bank, p4 * 128:(p4 + 1) * 128], lhsT=KTd[half * 64:(half + 1) * 64, g, j * 128:(j + 1) * 128],
                            rhs=qT[half * 64:(half + 1) * 64, pair, q0:q0 + 128], start=True, stop=(not masked)),
                            reads=[("KTd", g), ("qT", pair)], writes=[("ps", bank)])
                        if masked:
                            P.op("pe", lambda e, bank=bank, p4=p4, j=j: e.matmul(ps[:, bank, p4 * 128:(p4 + 1) * 128], lhsT=notsel[:, j * 128:(j + 1) * 128], rhs=negI[:, :], start=False, stop=True),
                                 reads=[nsn, "negI"], writes=[("ps", bank)])
                    psv = ps[:, bA:bB + 1, :].rearrange("s e (p q) -> s e p q", q=128)
                    if ty <= 1:
                        P.op("dve", lambda e, psv=psv, ty=ty, g=g: e.tensor_tensor(out=tmpl[:, :, :].rearrange("s (e p) q -> s e p q", e=2), in0=psv,
                                                                                  in1=TzT[:, ty, :, 8 * g:8 * g + 8].rearrange("s q (p e) -> s e p q", e=2), op=ALU.add),
                             reads=[("ps", bA), ("ps", bB), "TzT"], writes=["tmpl"])
                        P.op("act", lambda e, slot=slot: e.activation(out=PT[:, slot, :, :], in_=tmpl[:, :, :], func=AF.Exp), reads=["tmpl"], writes=[("PT", slot)])
                    else:
                        P.op("act", lambda e, slot=slot, psv=psv: e.activation(out=PT[:, slot, :, :].rearrange("s (e p) q -> s e p q", e=2), in_=psv, func=AF.Exp),
                             reads=[("ps", bA), ("ps", bB)], writes=[("PT", slot)])
                    for hh in range(8):
                        head = 2 * (g * 4 + hh % 4) + hh // 4
                        bpv, col = head // 7, (head % 7) * 66
                        first = (j == 0 and head in PV_FIRST)
                        P.op("pe", lambda e, slot=slot, hh=hh, bpv=bpv, col=col, j=j, g=g, first=first, i=i: e.matmul(
                            ps[:, bpv, col:col + 65], lhsT=PT[:, slot, hh, :], rhs=Vaug[:, j, g, 0:65], start=first, stop=(j == i)),
                            reads=[("PT", slot), ("Vaug", j)], writes=[("ps", bpv)])
            for bk in range(3):
                nh = 7 if bk < 2 else 2
                P.op("dve", lambda e, bk=bk, nh=nh: e.reciprocal(out=rec[:, bk * 7:bk * 7 + nh], in_=ps[:, bk, 0:nh * 66].rearrange("q (h c) -> q h c", c=66)[:, :, 64]),
                     reads=[("ps", bk)], writes=["rec"])
                P.op("dve", lambda e, bk=bk, nh=nh: e.tensor_tensor(out=obn[:, bk * 7:bk * 7 + nh, :], in0=ps[:, bk, 0:nh * 66].rearrange("q (h c) -> q h c", c=66)[:, :, 0:64],
                                                                  in1=rec[:, bk * 7:bk * 7 + nh].unsqueeze(2).to_broadcast([128, nh, 64]), op=ALU.mult),
                     reads=[("ps", bk), "rec"], writes=["obn"])
            for p8 in range(8):
                P.op("pe", lambda e, p8=p8: e.transpose(psb[:, p8, :], obn[:, 2 * p8:2 * p8 + 2, :].rearrange("q h d -> q (h d)"), ident_b[:, :]), reads=["obn", "ident_b"], writes=["psb"])
            P.op("act", lambda e, q0=q0: e.copy(out=obT[:, :, q0:q0 + 128], in_=psb[:, :, :]), reads=["psb"], writes=QK)

        if NQB > 0:
            idx_topk(0)
        for i in range(NQB):
            if i + 1 < NQB:
                idx_topk(i + 1)
            attend(i)
        A.release(m3); P.barrier()
        NSB = int(os.environ.get("NSB", "16"))
        ptt = A.alloc([128, 16], I32); cmt = A.alloc([128, 1], I32); idx = A.alloc([128, 16], I32)
        score_all = A.alloc([64, 2176]); work_s = A.alloc([64, 2176]); nots = A.alloc([128, 2176], BF16); mxs = A.alloc([64, 8])
        masks = A.alloc([64, 64]); vm = A.alloc([64, 16, 4]); selh_f = A.alloc([128, 16, 32]); selh = A.alloc([128, 16, 32], BF16)
        wis = A.alloc([4, 16, 8])
        TzS = A.alloc([128, 16, 4, 16]); NBb = A.alloc([64, 4, 16])
        P.dma("sp", lambda e: e.dma_start(out=ptt[:, :], in_=ptrep_d), writes=["ptt"])
        P.dma("sp", lambda e: e.dma_start(out=cmt[:, :], in_=cmod_d), writes=["cmt"])
        P.dma("sp", lambda e: e.dma_start(out=masks[:, :], in_=masks_d), writes=["masks"])
        P.dma("sp", lambda e: e.dma_start(out=vm[:, :, :], in_=vm_d), writes=["vm"])
        P.dma("sp", lambda e: e.dma_start(out=selh_f[:, :, :], in_=selh_d), writes=["selh_f"])
        P.op("act", lambda e: e.copy(out=selh[:, :, :], in_=selh_f[:, :, :]), reads=["selh_f"], writes=["selh"])
        P.op("dve", lambda e: e.tensor_scalar(out=idx[:, :], in0=ptt[:, :], scalar1=8, scalar2=cmt[:, 0:1], op0=ALU.mult, op1=ALU.add), reads=["ptt", "cmt"], writes=["idx"])
        P.op("dve", lambda e: e.memset(nots[:, :], 0.0), writes=["nots"])
        for bq in range(NSB):
            P.dma("sp", lambda e, bq=bq: e.dma_start(out=wis[:, bq, :], in_=wi_t[4 * bq:4 * bq + 4, 16, :]), reads=[("wi_t", 16)], writes=["wis"], semkey="wis")
        m4 = A.mark()
        ohs_t = A.alloc([32, 2, 4, 4, 128], BF16) if False else A.alloc([32, 2, 2048], BF16)
        ohn_t = A.alloc([32, 4, 64], BF16)
        P.dma("sp", lambda e: e.dma_start(out=ohn_t[:, :, :], in_=ohn_d), writes=["ohn_t"])
        for uc in range(4):
            slot = uc % 2; bank = 3 + slot
            P.dma("sp", lambda e, uc=uc, slot=slot: e.dma_start(out=ohs_t[:, slot, :], in_=ohs_d[:, uc * 4:(uc + 1) * 4, :, :].rearrange("k u t p -> k (u t p)")), writes=[("ohs_t", slot)])
            for ut in range(16):
                for (rr, st_, sp_) in ((rbh, True, False), (rbl, False, True)):
                    P.op("pe", lambda e, slot=slot, bank=bank, ut=ut, rr=rr, st_=st_, sp_=sp_: e.matmul(ps[:, bank, ut * 16:(ut + 1) * 16], lhsT=ohs_t[:, slot, ut * 128:(ut + 1) * 128], rhs=rr[:, :], start=st_, stop=sp_),
                         reads=[("ohs_t", slot), "rbh", "rbl"], writes=[("ps", bank)])
            P.op("dve", lambda e, bank=bank, uc=uc: e.tensor_copy(out=TzS[:, uc * 4:(uc + 1) * 4, :, :], in_=ps[:, bank, 0:256].rearrange("p (u t h) -> p u t h", u=4, t=4)), reads=[("ps", bank)], writes=["TzS"])
        for t in range(4):
            for (rr, st_, sp_) in ((rbh, True, False), (rbl, False, True)):
                P.op("pe", lambda e, t=t, rr=rr, st_=st_, sp_=sp_: e.matmul(ps[0:64, 5, t * 16:(t + 1) * 16], lhsT=ohn_t[:, t, :], rhs=rr[:, :], start=st_, stop=sp_),
                     reads=["ohn_t", "rbh", "rbl"], writes=[("ps", 5)])
        P.op("dve", lambda e: e.tensor_copy(out=NBb[:, :, :], in_=ps[0:64, 5, 0:64].rearrange("p (t h) -> p t h", t=4)), reads=[("ps", 5)], writes=["NBb"])
        A.release(m4); P.barrier()
        KIb = A.alloc([128, 16, 64]); KId = A.alloc([128, 16, 2, 64], BF16); kiTs = A.alloc([128, 16, 128], BF16)
        accs = A.alloc([4, 2176]); relus = A.alloc([4, 2, 512])
        cki = cache_ki
        for bq in range(NSB):
            P.dma("pool", lambda e, bq=bq: e.indirect_dma_start(out=KIb[:, :, :].rearrange("p u d -> p (u d)"), out_offset=None, in_=cki,
                                                               in_offset=bass.IndirectOffsetOnAxis(ap=idx[:, bq:bq + 1], axis=0)), reads=["idx"], writes=["KIb"])
            P.op("act", lambda e: e.copy(out=KId[:, :, :, :], in_=KIb[:, :, :].unsqueeze(2).to_broadcast([128, 16, 2, 64])), reads=["KIb"], writes=["KId"])
            for ub in range(2):
                for u8 in range(8):
                    u = ub * 8 + u8
                    P.op("pe", lambda e, u=u, u8=u8: e.transpose(psb[:, u8, :], KId[:, u, :, :].rearrange("p a d -> p (a d)"), ident_b[:, :]), reads=["KId", "ident_b"], writes=["psb"])
                P.op("act", lambda e, ub=ub: e.copy(out=kiTs[:, ub * 8:(ub + 1) * 8, :], in_=psb[:, :, :]), reads=["psb"], writes=["kiTs"])
            for c in range(5):
                w = 512 if c < 4 else 64
                for h in range(8):
                    half, p4 = h % 2, h // 2
                    bank = 3 + (icnt[0] % 4); rsl = icnt[0] % 2; icnt[0] += 1
                    rhs_fn = (lambda c=c, half=half: kiTs[half * 64:(half + 1) * 64, 4 * c:4 * c + 4, :].rearrange("p u k -> p (u k)")) if c < 4 else (lambda half=half: kiT2s[half * 64:(half + 1) * 64, :])
                    P.op("pe", lambda e, bank=bank, w=w, half=half, p4=p4, bq=bq, rhs_fn=rhs_fn: e.matmul(ps[0:4, bank, 0:w], lhsT=qiTs[half * 64:(half + 1) * 64, p4, 4 * bq:4 * bq + 4], rhs=rhs_fn(), start=True, stop=True),
                         reads=["qiTs", "kiTs", "kiT2s"], writes=[("ps", bank)])
                    P.op("act", lambda e, bank=bank, w=w, rsl=rsl: e.activation(out=relus[:, rsl, 0:w], in_=ps[0:4, bank, 0:w], func=AF.Relu), reads=[("ps", bank)], writes=[("relus", rsl)])
                    if h == 0:
                        P.op("dve", lambda e, w=w, rsl=rsl, c=c, bq=bq: e.tensor_scalar(out=accs[:, c * 512:c * 512 + w], in0=relus[:, rsl, 0:w], scalar1=wis[:, bq, 0:1], scalar2=None, op0=ALU.mult),
                             reads=[("relus", rsl), "wis"], writes=["accs"])
                    else:
                        P.op("dve", lambda e, w=w, rsl=rsl, c=c, bq=bq, h=h: e.scalar_tensor_tensor(out=accs[:, c * 512:c * 512 + w], in0=relus[:, rsl, 0:w], scalar=wis[:, bq, h:h + 1], in1=accs[:, c * 512:c * 512 + w],
                                                                                                 op0=ALU.mult, op1=ALU.add),
                             reads=[("relus", rsl), "wis", "accs"], writes=["accs"])
            P.dma("sp", lambda e, bq=bq: e.dma_start(out=score_all[4 * bq:4 * bq + 4, 0:2112], in_=accs[:, 0:2112]), reads=["accs"], writes=["score_all"], semkey="sca")
        P.op("dve", lambda e: e.tensor_tensor(out=score_all[:, 2048:2112], in0=score_all[:, 2048:2112], in1=masks[:, :], op=ALU.add), reads=["score_all", "masks"], writes=["score_all"])
        for r in range(32):
            src = score_all if r == 0 else work_s
            P.op("dve", lambda e, src=src: e.max(out=mxs[:, :], in_=src[:, 0:2112]), reads=["score_all", "work_s"], writes=["mxs"])
            if r < 31:
                P.op("dve", lambda e, src=src: e.match_replace(out=work_s[:, 0:2112], in_to_replace=mxs[:, :], in_values=src[:, 0:2112], imm_value=-1e30),
                     reads=["score_all", "mxs", "work_s"], writes=["work_s"])
        P.op("dve", lambda e: e.tensor_scalar(out=nots[0:64, 0:2112], in0=score_all[:, 0:2112], scalar1=mxs[:, 7:8], scalar2=None, op0=ALU.is_lt), reads=["score_all", "mxs", "nots"], writes=["nots"])
        A.release(m4); P.barrier()
        Kb = A.alloc([128, 16, 128]); Vb = A.alloc([128, 16, 128]); Kd = A.alloc([128, 16, 2, 64], BF16)
        KTs = A.alloc([128, 2, 16, 128], BF16); Vs = A.alloc([128, 16, 2, 66], BF16)
        tmps = A.alloc([128, 2, 17, 32]); PTs = A.alloc([128, 2, 17, 32], BF16)
        recs = A.alloc([16, 4]); obs = A.alloc([16, 4, 64], BF16)
        P.op("dve", lambda e: e.memset(Vs[:, :, :, :], 1.0), writes=["Vs"])
        P.op("dve", lambda e: e.memset(tmps[:, :, :, :], 0.0), writes=["tmps"])
        for bq in range(NSB):
            c0 = 2048 + 4 * bq
            P.dma("pool", lambda e, bq=bq: e.indirect_dma_start(out=Kb[:, :, :].rearrange("p u d -> p (u d)"), out_offset=None, in_=cache_k,
                                                               in_offset=bass.IndirectOffsetOnAxis(ap=idx[:, bq:bq + 1], axis=0)), reads=["idx"], writes=["Kb"])
            P.dma("pool", lambda e, bq=bq: e.indirect_dma_start(out=Vb[:, :, :].rearrange("p u d -> p (u d)"), out_offset=None, in_=cache_v,
                                                               in_offset=bass.IndirectOffsetOnAxis(ap=idx[:, bq:bq + 1], axis=0)), reads=["idx"], writes=["Vb"])
            P.op("act", lambda e: e.copy(out=Vs[:, :, :, 0:64], in_=Vb[:, :, :].rearrange("p u (g d) -> p u g d", g=2)), reads=["Vb"], writes=["Vs"])
            for g in range(2):
                P.op("act", lambda e, g=g: e.copy(out=Kd[:, :, :, :], in_=Kb[:, :, g * 64:(g + 1) * 64].unsqueeze(2).to_broadcast([128, 16, 2, 64])), reads=["Kb"], writes=["Kd"])
                for ub in range(2):
                    for u8 in range(8):
                        u = ub * 8 + u8
                        P.op("pe", lambda e, u=u, u8=u8: e.transpose(psb[:, u8, :], Kd[:, u, :, :].rearrange("p a d -> p (a d)"), ident_b[:, :]), reads=["Kd", "ident_b"], writes=["psb"])
                    P.op("dve", lambda e, ub=ub, g=g: e.tensor_scalar(out=KTs[:, g, ub * 8:(ub + 1) * 8, :], in0=psb[:, :, :], scalar1=gq2[:, 0:1], scalar2=0.125, op0=ALU.mult, op1=ALU.mult),
                         reads=["psb", "gq2"], writes=["KTs"])
            for half in range(2):
                for u in range(17):
                    if u < 16:
                        out_fn = lambda g, half=half, u=u: ps[:, 3 + half, u * 32 + g * 16:u * 32 + g * 16 + 16]
                        mo = ps[:, 3 + half, u * 32:(u + 1) * 32]; wk = ("ps", 3 + half)
                        nl = nots[:, u * 128:(u + 1) * 128]
                    else:
                        out_fn = lambda g, half=half: ps[0:64, 5 + half, g * 16:g * 16 + 16]
                        mo = ps[0:64, 5 + half, 0:32]; wk = ("ps", 5 + half)
                        nl = nots[:, 2048:2112]
                    P.op("pe", lambda e, mo=mo, nl=nl, bq=bq: e.matmul(mo, lhsT=nl, rhs=selh[:, bq, :], start=True, stop=False), reads=["nots", "selh"], writes=[wk])
                    for g in range(2):
                        lt = KTs[half * 64:(half + 1) * 64, g, u, :] if u < 16 else KTd[half * 64:(half + 1) * 64, g, 2048:2112]
                        P.op("pe", lambda e, lt=lt, g=g, half=half, c0=c0, out_fn=out_fn: e.matmul(out_fn(g), lhsT=lt, rhs=qT[half * 64:(half + 1) * 64, g * 4:(g + 1) * 4, c0:c0 + 4], start=False, stop=(g == 1)),
                             reads=["KTs", ("KTd", g)] + QK, writes=[wk])
            for half in range(2):
                P.op("dve", lambda e, half=half: e.tensor_tensor(out=tmps[:, half, 0:16, :].rearrange("p u (a t) -> p u a t", t=4), in0=ps[:, 3 + half, :].rearrange("p (u a t) -> p u a t", u=16, t=4),
                                                                in1=TzS[:, :, :, half::2].rearrange("p u t a -> p u a t"), op=ALU.add), reads=[("ps", 3 + half), "TzS"], writes=["tmps"])
                P.op("dve", lambda e, half=half: e.tensor_tensor(out=tmps[0:64, half, 16, :].rearrange("p (a t) -> p a t", t=4), in0=ps[0:64, 5 + half, 0:32].rearrange("p (a t) -> p a t", t=4),
                                                                in1=NBb[:, :, half::2].rearrange("p t a -> p a t"), op=ALU.add), reads=[("ps", 5 + half), "NBb"], writes=["tmps"])
                P.op("dve", lambda e, half=half, bq=bq: e.tensor_tensor(out=tmps[0:64, half, 16, :].rearrange("p (a t) -> p a t", t=4), in0=tmps[0:64, half, 16, :].rearrange("p (a t) -> p a t", t=4),
                                                                       in1=vm[:, bq, :].unsqueeze(1).to_broadcast([64, 8, 4]), op=ALU.add), reads=["tmps", "vm"], writes=["tmps"])
            P.op("act", lambda e: e.activation(out=PTs[:, :, :, :], in_=tmps[:, :, :, :], func=AF.Exp), reads=["tmps"], writes=["PTs"])
            firstpv = True
            for u in range(17):
                for g in range(2):
                    for half in range(2):
                        if u < 16:
                            lt = PTs[:, half, u, g * 16:(g + 1) * 16]; rv = Vs[:, u, g, 0:65]
                        else:
                            lt = PTs[0:64, half, 16, g * 16:(g + 1) * 16]; rv = Vaug[0:64, 16, g, 0:65]
                        cc = (g * 2 + half) * 66
                        P.op("pe", lambda e, lt=lt, rv=rv, cc=cc, fp=firstpv, u=u: e.matmul(ps[0:16, 2, cc:cc + 65], lhsT=lt, rhs=rv, start=fp, stop=(u == 16)),
                             reads=["PTs", "Vs", ("Vaug", 16)], writes=[("ps", 2)])
                        firstpv = False
            P.op("dve", lambda e: e.reciprocal(out=recs[:, :], in_=ps[0:16, 2, 0:264].rearrange("q (h c) -> q h c", c=66)[:, :, 64]), reads=[("ps", 2)], writes=["recs"])
            P.op("dve", lambda e: e.tensor_tensor(out=obs[:, :, :], in0=ps[0:16, 2, 0:264].rearrange("q (h c) -> q h c", c=66)[:, :, 0:64],
                                                  in1=recs[:, :].unsqueeze(2).to_broadcast([16, 4, 64]), op=ALU.mult), reads=[("ps", 2), "recs"], writes=["obs"])
            for g in range(2):
                P.op("pe", lambda e, g=g: e.transpose(psb[:, g, 0:16], obs[:, 2 * g:2 * g + 2, :].rearrange("q h d -> q (h d)"), ident_b[0:16, 0:16]), reads=["obs", "ident_b"], writes=["psb"])
            P.op("act", lambda e, c0=c0: e.copy(out=obT[:, :, c0:c0 + 4].rearrange("p (g a) t -> p g a t", g=2), in_=psb[:, 0:2, 0:16].rearrange("p g (a t) -> p g a t", t=4)), reads=["psb"], writes=QK)
        if dbg_ob is not None:
            P.dma("sp", lambda e: e.dma_start(out=dbg_ob, in_=obT[:, :, :]), reads=QK, store=True)
        A.release(m3); P.barrier()
        sgm = A.alloc([128, 2, 512]); mtmp = A.alloc([128, 512]); gtmp = A.alloc([128, 2, 512], BF16)
        for p8 in range(8):
            sl = load_w([(0, C_AG + p8 * 128, 128)], 128)
            for ni, (n0, nn) in enumerate(NTL):
                bg = nbank(); ssl = ni % 2
                proj_fm(sl, 128, n0, nn, bg)
                P.op("act", lambda e, bg=bg, nn=nn, ssl=ssl: e.activation(out=gtmp[:, ssl, 0:nn], in_=ps[:, bg, 0:nn], func=AF.Silu), reads=[("ps", bg)], writes=[("gtmp", ssl)])
                P.op("dve", lambda e, nn=nn, ssl=ssl, p8=p8, n0=n0: e.tensor_tensor(out=obT[:, p8, n0:n0 + nn], in0=obT[:, p8, n0:n0 + nn], in1=gtmp[:, ssl, 0:nn], op=ALU.mult),
                     reads=[("gtmp", ssl), ("qT", p8)], writes=[("qT", p8)])
        merge_branch(obT, QK, w_pb, C_GB, False, sgm, mtmp)
        A.release(m2); P.barrier()
        wo = A.alloc([128, 8, D], BF16); xo = A.alloc([128, 2, D]); yo = A.alloc([128, 2, D])
        for cb in range(8):
            sl = load_w([(0, cb * 128, 128)], 128, src=w_out)
            P.op("act", lambda e, sl=sl, cb=cb: e.copy(out=wo[:, :, cb * 128:(cb + 1) * 128], in_=wb[:, sl, :, :]), reads=[("wb", sl)], writes=["wo"])
        for ti, (t0, n) in enumerate(TT):
            sl = ti % 2
            src = x_p[t0:t0 + n, :] if ti < 16 else x_s[:, :]
            P.dma("sp", lambda e, sl=sl, n=n, src=src: e.dma_start(out=xo[0:n, sl, :], in_=src), writes=[("xo", sl)])
            for hf in range(2):
                bk = nbank()
                for kc in range(8):
                    P.op("pe", lambda e, kc=kc, bk=bk, t0=t0, n=n, hf=hf: e.matmul(ps[0:n, bk, :], lhsT=mT[:, kc, t0:t0 + n], rhs=wo[:, kc, hf * 512:(hf + 1) * 512], start=(kc == 0), stop=(kc == 7)),
                         reads=["mT", "wo"], writes=[("ps", bk)])
                P.op("dve", lambda e, bk=bk, n=n, sl=sl, hf=hf: e.tensor_tensor(out=yo[0:n, sl, hf * 512:(hf + 1) * 512], in0=ps[0:n, bk, :], in1=xo[0:n, sl, hf * 512:(hf + 1) * 512], op=ALU.add),
                     reads=[("ps", bk), ("xo", sl)], writes=[("yo", sl)])
            dst = y_p[t0:t0 + n, :] if ti < 16 else y_s[:, :]
            P.dma("sp", lambda e, dst=dst, n=n, sl=sl: e.dma_start(out=dst, in_=yo[0:n, sl, :]), reads=[("yo", sl)], store=True)

        return finish()


_CACHE = {}


def kernel(**inputs):
    f32 = np.float32
    if "nc" not in _CACHE:
        _CACHE["nc"] = build_program()
    nc = _CACHE["nc"]
    x_prompt = np.asarray(inputs["x_prompt"], f32); x_sample = np.asarray(inputs["x_sample"], f32)
    bones = np.zeros((128, 128), f32); bones[:64, :64] = 1; bones[64:, 64:] = 1
    common = {
        "w_in": np.ascontiguousarray(np.asarray(inputs["w_in"], f32)[0]),
        "norm_g": np.asarray(inputs["norm_g"], f32),
        "q_norm_g": np.asarray(inputs["q_norm_g"], f32), "k_norm_g": np.asarray(inputs["k_norm_g"], f32),
        "ident": np.eye(128, dtype=f32), "bones": bones,
        "rel_bias": np.asarray(inputs["rel_bias"], f32),
    }
    import ml_dtypes
    qq = np.arange(128)[None, :, None]; sk = np.arange(128)[None, None, :]; ty = np.arange(2)[:, None, None]
    dist = 128 * ty + qq - sk
    bk = _t5_bucket_np(dist)
    oh = (bk[None] == np.arange(32)[:, None, None, None]) & (dist[None] >= 0)
    common["oh_p"] = oh.astype(f32).astype(ml_dtypes.bfloat16)
    sq_ = np.arange(128)
    common["caus0"] = np.where(sq_[:, None] <= sq_[None, :], 0.0, -30000.0).astype(f32)
    common["causS"] = np.where(sq_[None, :] <= sq_[:, None], 0.0, -1e30).astype(f32)
    uu = np.arange(16)[:, None, None]; tt = np.arange(4)[None, :, None]; pp = np.arange(128)[None, None, :]
    s_key = 1920 + (pp - 120) * 16 + uu
    dist_s = 2048 + tt - s_key
    ohs = (_t5_bucket_np(dist_s)[None] == np.arange(32)[:, None, None, None]) & (pp >= 120)[None]
    common["ohs"] = ohs.astype(f32).astype(ml_dtypes.bfloat16)
    kt = (np.arange(64) % 4)[None, :]; t4 = np.arange(4)[:, None]
    dn = t4 - kt
    ohn = (_t5_bucket_np(dn)[None] == np.arange(32)[:, None, None]) & (dn >= 0)[None]
    common["ohn"] = ohn.astype(f32).astype(ml_dtypes.bfloat16)
    kb_ = (np.arange(64) // 4); ktt = (np.arange(64) % 4)
    vm = np.where((kb_[:, None, None] == np.arange(16)[None, :, None]) & (ktt[:, None, None] <= np.arange(4)[None, None, :]), 0.0, -30000.0)
    common["vm"] = vm.astype(f32)
    selh = np.zeros((128, 16, 8, 4), f32)
    for bb in range(16):
        for t_ in range(4):
            selh[4 * bb + t_, bb, :, t_] = -30000.0
    common["selh"] = selh.reshape(128, 16, 32)
    common["masks"] = np.where((kb_[:, None] == kb_[None, :]) & (ktt[None, :] <= ktt[:, None]), 0.0, -1e30).astype(f32)
    common["cmod"] = (np.arange(128) % 8).astype(np.int32).reshape(128, 1)
    if STAGE >= 6:
        common["cache_k"] = np.asarray(inputs["cache_k"], f32).reshape(NPHYS * 8, 2048)
        common["cache_v"] = np.asarray(inputs["cache_v"], f32).reshape(NPHYS * 8, 2048)
        common["cache_ki"] = np.asarray(inputs["cache_kidx"], f32).reshape(NPHYS * 8, 1024)
    for nm in ("w_pa", "w_pb", "w_out"):
        common[nm] = np.ascontiguousarray(np.asarray(inputs[nm], f32)[0])
    for nm in ("shift_mu", "w0", "a0", "k_k", "k_a", "lnx_g", "lnx_b"):
        common[nm] = np.asarray(inputs[nm], f32).reshape(1, -1)
    common["r_k"] = np.asarray(inputs["r_k"], f32).reshape(1, 1024)
    common["w2"] = np.ascontiguousarray(np.asarray(inputs["w2"], f32)[0]); common["a2"] = np.ascontiguousarray(np.asarray(inputs["a2"], f32)[0])
    i64 = np.arange(64)
    common["mask_u"] = (i64[:, None] < i64[None, :]).astype(f32)
    common["mask_ui"] = (i64[:, None] <= i64[None, :]).astype(f32)
    common["mask_l"] = (i64[None, :] < i64[:, None]).astype(f32)
    seg = np.ones((64, 2, 4, 64), f32); seg[:, 0, :, 0] = 0.0; seg[:, 1, :, 0::4] = 0.0
    common["segmask"] = seg.reshape(64, 2, 256)
    state_wkv = np.asarray(inputs["state_wkv"], f32)[0]; state_shift = np.asarray(inputs["state_shift"], f32)[0]
    page_table = np.asarray(inputs["page_table"], np.int32)
    in_maps = []
    for c in range(NCORES):
        m = dict(common)
        m["state_wkv"] = np.ascontiguousarray(state_wkv[16 * c:16 * c + 16]); m["state_shift"] = np.ascontiguousarray(state_shift[16 * c:16 * c + 16])
        m["ptrep"] = np.ascontiguousarray(np.repeat(page_table[16 * c:16 * c + 16], 8, axis=1).T)
        m["x_p"] = np.ascontiguousarray(x_prompt[c])
        m["x_s"] = np.ascontiguousarray(x_sample[16 * c:16 * c + 16].reshape(64, D))
        in_maps.append(m)
    res = run_bass_kernel_spmd(nc, in_maps, core_ids=list(range(NCORES)))
    R = res.results
    if os.environ.get("DBG"):
        for k_ in R[0]:
            if k_.startswith("dd_") or k_.startswith("dbg_"):
                np.save(k_ + ".npy", np.asarray(R[0][k_]).astype(f32))
    cat = lambda name: np.stack([R[c][name] for c in range(NCORES)], 0)
    y_p = cat("y_p").reshape(8, 2048, 1024)
    y_s = cat("y_s").reshape(128, 4, 1024)
    k_p = cat("k_p").reshape(1, 8, 2048, 2, 64); v_p = cat("v_p").reshape(1, 8, 2048, 2, 64)
    ki_p = cat("ki_p").reshape(1, 8, 2048, 64)
    wkv_p = cat("wkv_p").reshape(1, 8, 16, 64, 64)
    sh_p = cat("sh_p").reshape(1, 8, 4224)
    k_s = cat("k_s").reshape(1, 128, 4, 2, 64); v_s = cat("v_s").reshape(1, 128, 4, 2, 64)
    ki_s = cat("ki_s").reshape(1, 128, 4, 64)
    wkv_s = cat("wkv_s").reshape(1, 128, 16, 64, 64)
    sh_s = cat("sh_s").reshape(1, 128, 4224)
    return (y_p, y_s, k_p, v_p, ki_p, wkv_p, sh_p, k_s, v_s, ki_s, wkv_s, sh_s)
```

```python
import math
from contextlib import ExitStack
import numpy as np
import concourse.bass as bass
import concourse.mybir as mybir
from concourse.bass_utils import run_bass_kernel_spmd

F32 = mybir.dt.float32
BF16 = mybir.dt.bfloat16
I32 = mybir.dt.int32
AF = mybir.ActivationFunctionType
ALU = mybir.AluOpType
AX = mybir.AxisListType

NCORES = 8
D = 1024
NP_ = 2048
NS = 64
NT = NP_ + NS
NCOLS = 9160
C_R, C_K, C_V, C_G, C_WD, C_AD = 0, 1024, 2048, 3072, 4096, 4160
A0 = 4224
C_Q, C_AK, C_AV, C_QI, C_KI, C_WI, C_AG = A0, A0 + 1024, A0 + 1152, A0 + 1280, A0 + 1792, A0 + 1856, A0 + 1864
C_GA, C_GB = 7112, 8136
NPHYS = 2560
import os
STAGE = int(os.environ.get('KSTAGE', '99'))

import os as _os
_SES = _os.environ.get("SES", "act,dve,pool")
SAME_ENGINE_SYNC = {"pe": False, "act": "act" in _SES, "dve": "dve" in _SES, "pool": "pool" in _SES, "sp": True}


BURST = int(_os.environ.get('BURST', '6'))


class _Op:
    __slots__ = ("eng", "fn", "deps", "is_dma", "dsem", "count", "needs_inc", "waits", "idx")


class Prog:
    def __init__(self, nc):
        self.nc = nc
        self.ops = {e: [] for e in ("pe", "act", "dve", "pool", "sp")}
        self.last_w = {}
        self.readers = {}
        self.dma_sem_names = {}
        self.all_ops = []
        self.store_ops = []
        self.bar = []
        self.last_dma = {}
        self.cur = None
        self.threads = {}

    def barrier(self):
        self.bar = [v[-1] for v in self.ops.values() if v] + list(self.last_dma.values())

    def _add(self, eng, fn, reads, writes, is_dma=False, dsem=None):
        op = _Op()
        op.eng = eng; op.fn = fn; op.is_dma = is_dma; op.dsem = dsem
        op.needs_inc = False; op.count = None; op.waits = []
        deps = []
        writes = writes + [r for r in reads if isinstance(r, tuple) and r[0] == "ps" and r not in writes]
        for r in reads:
            w = self.last_w.get(r)
            if w is not None:
                deps.append(w)
        for r in writes:
            w = self.last_w.get(r)
            if w is not None:
                deps.append(w)
            deps.extend(self.readers.get(r, ()))
        deps.extend(self.bar)
        op.deps = deps
        for r in reads:
            self.readers.setdefault(r, []).append(op)
        for r in writes:
            self.last_w[r] = op
            self.readers[r] = []
        self.ops[eng].append(op)
        self.all_ops.append(op)
        return op

    def op(self, eng, fn, reads=(), writes=()):
        if self.cur is not None:
            self.cur.append(("op", eng, fn, list(reads), list(writes), None, False))
            return None
        return self._add(eng, fn, list(reads), list(writes))

    def thread(self, name):
        self.cur = [] if name is not None else None
        if name is not None:
            self.threads[name] = self.cur

    def merge(self, names):
        self.cur = None
        qs = [self.threads.pop(n) for n in names if n in self.threads]
        idx = [0] * len(qs)
        alive = True
        while alive:
            alive = False
            for k, q in enumerate(qs):
                for _ in range(BURST):
                    if idx[k] >= len(q):
                        break
                    kind, eng, fn, reads, writes, semkey, store = q[idx[k]]
                    idx[k] += 1
                    alive = True
                    if kind == "op":
                        self._add(eng, fn, reads, writes)
                    else:
                        self._dma_now(eng, fn, reads, writes, semkey, store)

    def dma(self, q, fn, reads=(), writes=(), semkey=None, store=False):
        if semkey is None:
            semkey = ("w", writes[0]) if writes else ("r", reads[0])
        if self.cur is not None:
            self.cur.append(("dma", q, fn, list(reads), list(writes), semkey, store))
            return None
        return self._dma_now(q, fn, reads, writes, semkey, store)

    def _dma_now(self, q, fn, reads, writes, semkey, store):
        if semkey not in self.dma_sem_names:
            self.dma_sem_names[semkey] = len(self.dma_sem_names)
        op = self._add(q, fn, list(reads), list(writes), is_dma=True, dsem=semkey)
        self.last_dma[semkey] = op
        if store:
            self.store_ops.append(op)
        return op

    def emit(self):
        nc = self.nc
        def skip(d, op):
            return (not d.is_dma) and d.eng == op.eng and (not op.is_dma) and not SAME_ENGINE_SYNC[op.eng]
        for op in self.all_ops:
            for d in op.deps:
                if d.is_dma or d is op or skip(d, op):
                    continue
                d.needs_inc = True
        cnt = {e: 0 for e in self.ops}
        dcnt = {k: 0 for k in self.dma_sem_names}
        for op in self.all_ops:
            if op.is_dma:
                dcnt[op.dsem] += 16
                op.count = dcnt[op.dsem]
            elif op.needs_inc:
                cnt[op.eng] += 1
                op.count = cnt[op.eng]
        waited = {e: {} for e in self.ops}
        for op in self.all_ops:
            need = {}
            for d in op.deps:
                if d is op or d.count is None:
                    continue
                if d.is_dma:
                    key = ("d", d.dsem)
                else:
                    if skip(d, op):
                        continue
                    key = ("e", d.eng)
                if need.get(key, 0) < d.count:
                    need[key] = d.count
            w = waited[op.eng]
            for key, v in need.items():
                if w.get(key, 0) < v:
                    w[key] = v
                    op.waits.append((key, v))
        final_waits = {}
        for op in self.store_ops:
            key = ("d", op.dsem)
            final_waits[key] = max(final_waits.get(key, 0), op.count)
        with ExitStack() as es:
            sems = {}
            for e in self.ops:
                sems[("e", e)] = es.enter_context(nc.semaphore(f"s_{e}"))
            for k, i in self.dma_sem_names.items():
                sems[("d", k)] = es.enter_context(nc.semaphore(f"sd_{i}"))
            block = es.enter_context(nc.Block())

            def run(engname):
                def body(eng):
                    for op in self.ops[engname]:
                        for key, v in op.waits:
                            eng.wait_ge(sems[key], v)
                        ins = op.fn(eng)
                        if op.is_dma:
                            ins.then_inc(sems[("d", op.dsem)], 16)
                        elif op.needs_inc:
                            ins.then_inc(sems[("e", engname)], 1)
                    if engname == "sp":
                        for key, v in final_waits.items():
                            eng.wait_ge(sems[key], v)
                return body

            block.tensor(run("pe"))
            block.scalar(run("act"))
            block.vector(run("dve"))
            block.gpsimd(run("pool"))
            block.sync(run("sp"))
        return {e: len(v) for e, v in self.ops.items()}


class Arena:
    def __init__(self, nc, es, nbytes):
        self.t = es.enter_context(nc.sbuf_tensor("arena", [128, nbytes // 4], F32))
        self.off = 0
        self.peak = 0
        self.cap = nbytes

    def alloc(self, shape, dt=F32):
        esz = 4 if dt in (F32, I32) else 2
        n = 1
        for d in shape[1:]:
            n *= d
        nb = (n * esz + 31) // 32 * 32
        assert self.off + nb <= self.cap, ("arena overflow", self.off, nb, self.cap)
        v = self.t[0:shape[0], self.off // 4:(self.off + nb) // 4]
        if dt != F32:
            v = v.bitcast(dt)
        v = v[:, 0:n]
        if len(shape) == 3:
            v = v.rearrange("p (a b) -> p a b", a=shape[1])
        elif len(shape) == 4:
            v = v.rearrange("p (a b c) -> p a b c", a=shape[1], b=shape[2])
        self.off += nb
        self.peak = max(self.peak, self.off)
        return v

    def mark(self):
        return self.off

    def release(self, m):
        self.off = m


TT = [(i * 128, 128) for i in range(16)] + [(2048, 64)]
NTL = [(0, 512), (512, 512), (1024, 512), (1536, 512), (2048, 64)]


def _t5_bucket_np(d):
    d = np.maximum(d, 0)
    df = np.maximum(d, 1).astype(np.float32)
    large = 16 + (np.log(df / np.float32(16)) / np.float32(math.log(128 / 16)) * np.float32(16)).astype(np.int32)
    large = np.minimum(large, 31)
    return np.where(d < 16, d, large)


def build_program():
    nc = bass.Bass("TRN2", target_bir_lowering=False)
    di = lambda name, shape, dt=F32: nc.dram_tensor(name, list(shape), dt, kind="ExternalInput").ap()
    do = lambda name, shape, dt=F32: nc.dram_tensor(name, list(shape), dt, kind="ExternalOutput").ap()
    x_p = di("x_p", [NP_, D]); x_s = di("x_s", [NS, D])
    w_in = di("w_in", [D, NCOLS]); norm_g = di("norm_g", [1, D])
    qn_g = di("q_norm_g", [1, 64]); kn_g = di("k_norm_g", [1, 64])
    ident_d = di("ident", [128, 128]); bones_d = di("bones", [128, 128])
    rel_bias_d = di("rel_bias", [32, 16]); oh_d = di("oh_p", [32, 2, 128, 128], BF16)
    caus0_d = di("caus0", [128, 128]); causS_d = di("causS", [128, 128])
    dbg_ob = do("dbg_ob", [128, 8, NT], BF16) if os.environ.get("DBG") else None
    dbg_oa = do("dbg_oa", [128, 8, NT], BF16) if os.environ.get("DBG") else None
    w_pa = di("w_pa", [D, D]); w_pb = di("w_pb", [D, D]); w_out = di("w_out", [D, D])
    mu_d = di("shift_mu", [1, 4224]); w0_d = di("w0", [1, 1024]); a0_d = di("a0", [1, 1024]); w2_d = di("w2", [64, 1024]); a2_d = di("a2", [64, 1024])
    kk_d = di("k_k", [1, 1024]); ka_d = di("k_a", [1, 1024]); rk_d = di("r_k", [1, 1024]); lg_d = di("lnx_g", [1, 1024]); lb_d = di("lnx_b", [1, 1024])
    swkv_d = di("state_wkv", [16, 16, 64, 64]); ssh_d = di("state_shift", [16, 4224])
    mu64_d = di("mask_u", [64, 64]); mui64_d = di("mask_ui", [64, 64]); ml64_d = di("mask_l", [64, 64]); seg_d = di("segmask", [64, 2, 256])
    if STAGE >= 6:
        cache_k = di("cache_k", [NPHYS * 8, 2048]); cache_v = di("cache_v", [NPHYS * 8, 2048]); cache_ki = di("cache_ki", [NPHYS * 8, 1024])
    ptrep_d = di("ptrep", [128, 16], I32); cmod_d = di("cmod", [128, 1], I32)
    ohs_d = di("ohs", [32, 16, 4, 128], BF16); ohn_d = di("ohn", [32, 4, 64], BF16)
    vm_d = di("vm", [64, 16, 4]); selh_d = di("selh", [128, 16, 32]); masks_d = di("masks", [64, 64])
    y_p = do("y_p", [NP_, D]); y_s = do("y_s", [NS, D])
    k_p = do("k_p", [NP_, 128]); v_p = do("v_p", [NP_, 128]); ki_p = do("ki_p", [NP_, 64])
    k_s = do("k_s", [NS, 128]); v_s = do("v_s", [NS, 128]); ki_s = do("ki_s", [NS, 64])
    sh_p = do("sh_p", [1, 4224]); sh_s = do("sh_s", [16, 4224])
    wkv_p = do("wkv_p", [16, 64, 64]); wkv_s = do("wkv_s", [16, 16, 64, 64])

    es = ExitStack()
    with es:
        A = Arena(nc, es, 207 * 1024)
        ps = es.enter_context(nc.psum_tensor("ps", [128, 7, 512], F32))
        psb = es.enter_context(nc.psum_tensor("psb", [128, 8, 128], BF16))
        P = Prog(nc)

        dumps = {}

        def dbgdump(name, ap, shape):
            if not os.environ.get("DBG") or name in dumps:
                return
            d_ = do("dd_" + name, shape, BF16 if name == "oodd" else F32)
            dumps[name] = d_
            P.dma("sp", lambda e: e.dma_start(out=d_, in_=ap), reads=[name], store=True, semkey=("dd", name))

        def finish():
            counts = P.emit()
            print("ops:", counts, "sbuf peak", A.peak)
            return nc

        ident = A.alloc([128, 128]); ident_b = A.alloc([128, 128], BF16)
        bones = A.alloc([128, 128])
        gcol = A.alloc([128, 8])
        gq2 = A.alloc([128, 1]); gk2 = A.alloc([128, 1]); gkq = A.alloc([128, 1]); eps6 = A.alloc([128, 1])
        P.dma("sp", lambda e: e.dma_start(out=ident[:, :], in_=ident_d), writes=["ident"])
        P.dma("sp", lambda e: e.dma_start(out=bones[:, :], in_=bones_d), writes=["bones"])
        P.dma("sp", lambda e: e.dma_start(out=gcol[:, :], in_=norm_g.rearrange("o (k p) -> p (o k)", p=128), allow_slow_non_contiguous=True), writes=["gcol"])
        for hh in range(2):
            P.dma("sp", lambda e, hh=hh: e.dma_start(out=gq2[hh * 64:(hh + 1) * 64, :], in_=qn_g.rearrange("o d -> d o"), allow_slow_non_contiguous=True), writes=["gq2"], semkey="gq2")
            P.dma("sp", lambda e, hh=hh: e.dma_start(out=gk2[hh * 64:(hh + 1) * 64, :], in_=kn_g.rearrange("o d -> d o"), allow_slow_non_contiguous=True), writes=["gk2"], semkey="gk2")
        eps24 = A.alloc([128, 1]); epsln = A.alloc([128, 1])
        P.op("dve", lambda e: e.memset(eps24[:, :], 1e-24), writes=["eps24"])
        P.op("dve", lambda e: e.memset(epsln[:, :], 64e-5), writes=["epsln"])
        P.op("dve", lambda e: e.memset(eps6[:, :], 1e-6), writes=["eps6"])
        P.op("dve", lambda e: e.tensor_copy(out=ident_b[:, :], in_=ident[:, :]), reads=["ident"], writes=["ident_b"])
        P.op("dve", lambda e: e.tensor_scalar(out=gkq[:, :], in0=gk2[:, :], scalar1=gq2[:, 0:1], scalar2=0.125, op0=ALU.mult, op1=ALU.mult), reads=["gk2", "gq2"], writes=["gkq"])

        xnT = A.alloc([128, 8, NT], BF16)
        wst = A.alloc([128, 2, 8, 128]); wb = A.alloc([128, 2, 8, 128], BF16)
        m0 = A.mark()
        xin = A.alloc([128, 2, D]); xh = A.alloc([128, 2, D], BF16); junk = A.alloc([128, D])
        ss = A.alloc([128, 17]); rstd = A.alloc([128, 17])
        P.op("dve", lambda e: e.memset(ss[:, :], 0.0), writes=["ss"])
        for ti, (t0, n) in enumerate(TT):
            sl = ti % 2
            src = x_p[t0:t0 + n, :] if ti < 16 else x_s[:, :]
            P.dma("sp", lambda e, sl=sl, n=n, src=src: e.dma_start(out=xin[0:n, sl, :], in_=src), writes=[("xin", sl)])
            P.op("act", lambda e, sl=sl, n=n, ti=ti: e.activation(out=junk[0:n, :], in_=xin[0:n, sl, :], func=AF.Square, accum_out=ss[0:n, ti:ti + 1]),
                 reads=[("xin", sl), "ss"], writes=["junk", ("ss", ti)])
            P.op("act", lambda e, n=n, ti=ti: e.activation(out=rstd[0:n, ti:ti + 1], in_=ss[0:n, ti:ti + 1], func=AF.Sqrt, bias=eps6[0:n, 0:1], scale=1.0 / D),
                 reads=[("ss", ti), "eps6"], writes=[("rstd", ti)])
            P.op("dve", lambda e, n=n, ti=ti: e.reciprocal(out=rstd[0:n, ti:ti + 1], in_=rstd[0:n, ti:ti + 1]),
                 reads=[("rstd", ti)], writes=[("rstd", ti)])
            P.op("dve", lambda e, sl=sl, n=n, ti=ti: e.tensor_scalar(out=xh[0:n, sl, :], in0=xin[0:n, sl, :], scalar1=rstd[0:n, ti:ti + 1], scalar2=None, op0=ALU.mult),
                 reads=[("xin", sl), ("rstd", ti)], writes=[("xh", sl)])
            for kc in range(8):
                P.op("pe", lambda e, sl=sl, n=n, kc=kc: e.transpose(psb[:, kc, 0:n], xh[0:n, sl, kc * 128:(kc + 1) * 128], ident_b[0:n, 0:n]),
                     reads=[("xh", sl), "ident_b"], writes=["psb"])
            P.op("act", lambda e, t0=t0, n=n: e.copy(out=xnT[:, :, t0:t0 + n], in_=psb[:, :, 0:n]), reads=["psb"], writes=["xnT"])
        A.release(m0); P.barrier()
        if STAGE < 1:
            return finish()

        wcnt = [0]

        def load_w(parts, m, src=None):
            sl = wcnt[0] % 2
            wcnt[0] += 1
            srcw = w_in if src is None else src
            for (dc, sc, wd_) in parts:
                P.dma("sp", lambda e, sl=sl, dc=dc, sc=sc, wd_=wd_, srcw=srcw: e.dma_start(
                    out=wst[:, sl, :, dc:dc + wd_], in_=srcw[:, sc:sc + wd_].rearrange("(k p) m -> p k m", p=128)),
                    writes=[("wst", sl)])
            if src is None:
                P.op("pool", lambda e, sl=sl, m=m: e.tensor_tensor(out=wb[:, sl, :, 0:m], in0=wst[:, sl, :, 0:m],
                                                                  in1=gcol[:, :].unsqueeze(2).to_broadcast([128, 8, m]), op=ALU.mult),
                     reads=[("wst", sl), "gcol"], writes=[("wb", sl)])
            else:
                P.op("pool", lambda e, sl=sl, m=m: e.tensor_copy(out=wb[:, sl, :, 0:m], in_=wst[:, sl, :, 0:m]), reads=[("wst", sl)], writes=[("wb", sl)])
            return sl

        bankc = [0]

        bank_sets = {None: (0, 1, 2, 3, 4, 5, 6), "tb": (0, 1, 2), "ind": (3, 4), "dep": (5, 6)}
        bank_ctr = {}
        bank_grp = [None]

        def nbank():
            g_ = bank_grp[0]
            s_ = bank_sets[g_]
            k_ = bank_ctr.get(g_, 0)
            bank_ctr[g_] = k_ + 1
            return s_[k_ % len(s_)]

        def proj_fm(sl, m, n0, nn, bank):
            for kc in range(8):
                P.op("pe", lambda e, kc=kc: e.matmul(ps[0:m, bank, 0:nn], lhsT=wb[:, sl, kc, 0:m], rhs=xnT[:, kc, n0:n0 + nn], start=(kc == 0), stop=(kc == 7)),
                     reads=[("wb", sl), "xnT"], writes=[("ps", bank)])

        def proj_tm(sl, m, t0, n, bank, col0=0):
            for kc in range(8):
                P.op("pe", lambda e, kc=kc: e.matmul(ps[0:n, bank, col0:col0 + m], lhsT=xnT[:, kc, t0:t0 + n], rhs=wb[:, sl, kc, 0:m], start=(kc == 0), stop=(kc == 7)),
                     reads=[("wb", sl), "xnT"], writes=[("ps", bank)])


        mT = A.alloc([128, 8, NT], BF16)
        mA = A.mark()
        oaT = A.alloc([128, 8, NT], BF16)
        mR = A.mark()
        HG = 4; NG = 16 // HG; NB = 4 * HG + 2
        ones64 = A.alloc([64, 64]); MU = A.alloc([64, 64]); MUI = A.alloc([64, 64]); ML = A.alloc([64, 64]); segm = A.alloc([64, 2, HG * 64])
        P.op("dve", lambda e: e.memset(ones64[:, :], 1.0), writes=["ones64"])
        P.dma("sp", lambda e: e.dma_start(out=MU[:, :], in_=mu64_d), writes=["MU"])
        P.dma("sp", lambda e: e.dma_start(out=MUI[:, :], in_=mui64_d), writes=["MUI"])
        P.dma("sp", lambda e: e.dma_start(out=ML[:, :], in_=ml64_d), writes=["ML"])
        P.dma("sp", lambda e: e.dma_start(out=segm[:, :, :], in_=seg_d), writes=["segm"])
        wr = A.alloc([128, 8, NB * 64], BF16)
        w2g = A.alloc([64, HG * 64]); a2g = A.alloc([64, HG * 64])
        prm = A.alloc([64, HG, 8])
        mug = A.alloc([64, NB])
        zbuf = A.alloc([64, NB, 65]); dsh = A.alloc([64, HG, 64]); zraw = A.alloc([64, NB, 192], BF16)
        shs = A.alloc([16, NB * 64]); shT = A.alloc([64, NB, 16])
        NARR = 22
        MD = BF16
        arrs = [A.alloc([64, HG, 64], MD if i_ in (10, 11, 12, 13, 14, 15, 19, 20, 21) else F32) for i_ in range(NARR)]
        (sg, aa, clr, Ein, Einv, Eex, EC, kk_, kkn, kmod, at_, bt_, kt_, rt_, bh_, kh_, bon, t1, t2, Vt, Bt, Kt) = arrs
        vb2 = [A.alloc([64, HG, 64], MD), A.alloc([64, HG, 64], MD)]; Pb = A.alloc([64, HG, 64], MD)
        mats = [A.alloc([64, HG, 64], MD) for _ in range(8)]
        (N0, N1, NT0, NT1, R0, R1, U0, Uf) = mats
        outT = A.alloc([64, HG, 64])
        pm = lambda dt_=MD: [A.alloc([64, HG, 64], dt_), A.alloc([64, HG, 64], dt_)]
        at2 = [at_, A.alloc([64, HG, 64], MD), A.alloc([64, HG, 64], MD)]; rt2 = [rt_, A.alloc([64, HG, 64], MD), A.alloc([64, HG, 64], MD)]
        bon2 = [bon, A.alloc([64, HG, 64]), A.alloc([64, HG, 64])]; sg2 = pm(F32) + [A.alloc([64, HG, 64])]
        bt2 = [bt_, A.alloc([64, HG, 64], MD)]; kt2 = [kt_, A.alloc([64, HG, 64], MD)]; bh2 = [bh_, A.alloc([64, HG, 64], MD)]; kh2 = [kh_, A.alloc([64, HG, 64], MD)]
        f1 = A.alloc([64, HG, 64]); f2_ = A.alloc([64, HG, 64])
        Vt2 = [Vt, A.alloc([64, HG, 64], MD)]; Bt2 = [Bt, A.alloc([64, HG, 64], MD)]; Kt2 = [Kt, A.alloc([64, HG, 64], MD)]
        TT2 = pm(); Aak2 = pm(); Arb2 = pm(); Ark2 = pm()
        GC2 = [A.alloc([64, HG, 16]), A.alloc([64, HG, 16]), A.alloc([64, HG, 16])]
        Pst = A.alloc([64, HG, 64]); Ssb = A.alloc([64, HG, 64]); tw = A.alloc([64, 64]); adc = A.alloc([64, 64])
        oodd = A.alloc([64, HG // 2, 64], BF16)
        NCH = int(os.environ.get("NCH", "32"))
        NSEQ = int(os.environ.get("NSQ", "16"))
        LD = -0.6065306597126334

        def bc(ap2, C=64):
            return ap2.unsqueeze(2).to_broadcast([64, HG, C])

        for G in range(NG):
            blocks = [(kind * 1024 + (HG * G + h) * 64) for kind in range(4) for h in range(HG)]
            for bi in range(2 * HG + 1):
                if bi < 2 * HG:
                    parts = [(0, blocks[2 * bi], 64), (64, blocks[2 * bi + 1], 64)]
                else:
                    parts = [(0, C_WD, 128)]
                sl = load_w(parts, 128)
                P.op("act", lambda e, sl=sl, bi=bi: e.copy(out=wr[:, :, bi * 128:(bi + 1) * 128], in_=wb[:, sl, :, :]), reads=[("wb", sl)], writes=["wr"])
            P.dma("sp", lambda e, G=G: e.dma_start(out=w2g[:, :], in_=w2_d[:, G * HG * 64:(G + 1) * HG * 64]), writes=["w2g"])
            P.dma("sp", lambda e, G=G: e.dma_start(out=a2g[:, :], in_=a2_d[:, G * HG * 64:(G + 1) * HG * 64]), writes=["a2g"])
            for wi_, pd in enumerate((w0_d, a0_d, kk_d, ka_d, rk_d, lg_d, lb_d)):
                P.dma("sp", lambda e, wi_=wi_, pd=pd, G=G: e.dma_start(out=prm[:, :, wi_], in_=pd[0:1, G * HG * 64:(G + 1) * HG * 64].rearrange("o (h j) -> j (o h)", j=64), allow_slow_non_contiguous=True),
                      writes=["prm"], semkey="prm")
            for kind in range(4):
                P.dma("sp", lambda e, kind=kind, G=G: e.dma_start(out=mug[:, kind * HG:(kind + 1) * HG], in_=mu_d[0:1, kind * 1024 + G * HG * 64:kind * 1024 + (G + 1) * HG * 64].rearrange("o (h j) -> j (o h)", j=64),
                                                                allow_slow_non_contiguous=True), writes=["mug"], semkey="mug")
            P.dma("sp", lambda e: e.dma_start(out=mug[:, 4 * HG:4 * HG + 2], in_=mu_d[0:1, C_WD:C_WD + 128].rearrange("o (h j) -> j (o h)", j=64), allow_slow_non_contiguous=True), writes=["mug"], semkey="mug")
            for kind in range(4):
                P.dma("sp", lambda e, kind=kind, G=G: e.dma_start(out=shs[:, kind * HG * 64:(kind + 1) * HG * 64], in_=ssh_d[:, kind * 1024 + G * HG * 64:kind * 1024 + (G + 1) * HG * 64]), writes=["shs"], semkey="shs")
            P.dma("sp", lambda e: e.dma_start(out=shs[:, 4 * HG * 64:NB * 64], in_=ssh_d[:, C_WD:C_WD + 128]), writes=["shs"], semkey="shs")
            bk = nbank()
            for blk in range(NB):
                P.op("pe", lambda e, bk=bk, blk=blk: e.transpose(ps[0:64, bk, blk * 16:(blk + 1) * 16], shs[:, blk * 64:(blk + 1) * 64], ident[0:16, 0:16]), reads=["shs", "ident"], writes=[("ps", bk)])
            P.op("act", lambda e, bk=bk: e.copy(out=shT[:, :, :], in_=ps[0:64, bk, 0:NB * 16].rearrange("p (q b) -> p q b", b=16)), reads=[("ps", bk)], writes=["shT"])
            zraw_valid = set()
            P.op("dve", lambda e: e.memset(zbuf[:, :, 0:1], 0.0), writes=["zbuf"])
            P.op("dve", lambda e: e.memset(Pst[:, :, :], 0.0), writes=["Pst"])
            P.op("dve", lambda e: e.memset(Pb[:, :, :], 0.0), writes=["Pb"])

            def token_batch(col0, L, nseg, p, q2=0):
                sgi = 0 if L == 64 else 1
                at_ = at2[p]; rt_ = rt2[p]; bon = bon2[p]; GC = GC2[p]; sgate = sg2[p]
                bt_ = bt2[q2]; kt_ = kt2[q2]; bh_ = bh2[q2]; kh_ = kh2[q2]; vb = vb2[q2]
                n_bt = f"bt_{q2}"; n_kt = f"kt_{q2}"; n_bh = f"bh_{q2}"; n_kh = f"kh_{q2}"; n_vb = f"vb{q2}"
                n_at = f"at_{p}"; n_rt = f"rt_{p}"; n_bon = f"bon{p}"; n_gc = f"GC{p}"; n_sgt = f"sgate{p}"
                chi = col0 // 64
                if L == 64 and chi % 4 != 0 and (chi - chi % 4) in zraw_valid:
                    k_ = chi % 4 - 1
                    P.op("act", lambda e, k_=k_: e.copy(out=zbuf[:, :, 1:65], in_=zraw[:, :, k_ * 64:(k_ + 1) * 64]), reads=["zraw"], writes=["zbuf"])
                else:
                    nch_ = min(4, NCH - chi) if L == 64 else 1
                    W_ = 64 * nch_
                    for q4 in range(5):
                        nb = HG if q4 < 4 else 2
                        hpb = 512 // W_
                        bks = [nbank() for _ in range((nb + hpb - 1) // hpb)]
                        for hh in range(nb):
                            blk = q4 * HG + hh
                            bk = bks[hh // hpb]; c_ = (hh % hpb) * W_
                            for kc in range(8):
                                P.op("pe", lambda e, bk=bk, c_=c_, blk=blk, kc=kc, W_=W_: e.matmul(ps[0:64, bk, c_:c_ + W_], lhsT=wr[:, kc, blk * 64:(blk + 1) * 64], rhs=xnT[:, kc, col0:col0 + W_],
                                                                                             start=(kc == 0), stop=(kc == 7)), reads=["wr", "xnT"], writes=[("ps", bk)])
                        for bi_, bk in enumerate(bks):
                            h0 = bi_ * hpb; nh_ = min(hpb, nb - h0)
                            pv_ = ps[0:64, bk, 0:nh_ * W_].rearrange("p (h t) -> p h t", t=W_)
                            P.op("act", lambda e, q4=q4, h0=h0, nh_=nh_, pv_=pv_: e.copy(out=zbuf[:, q4 * HG + h0:q4 * HG + h0 + nh_, 1:65], in_=pv_[:, :, 0:64]), reads=[("ps", bk)], writes=["zbuf"])
                            if nch_ > 1:
                                P.op("act", lambda e, q4=q4, h0=h0, nh_=nh_, pv_=pv_, W_=W_: e.copy(out=zraw[:, q4 * HG + h0:q4 * HG + h0 + nh_, 0:W_ - 64], in_=pv_[:, :, 64:W_]), reads=[("ps", bk)], writes=["zraw"])
                    if nch_ > 1:
                        zraw_valid.add(chi)
                if L == 4:
                    zv4 = zbuf[:, :, 1:65].rearrange("p q (b t) -> p q b t", t=4)
                    for q4 in range(5):
                        nb = HG if q4 < 4 else 2
                        sl_ = slice(q4 * HG, q4 * HG + nb)
                        P.op("pool", lambda e, sl_=sl_: e.tensor_copy(out=dsh[:, 0:sl_.stop - sl_.start, :].rearrange("p q (b t) -> p q b t", t=4)[:, :, :, 1:4], in_=zv4[:, sl_, :, 0:3]), reads=["zbuf"], writes=["dsh"])
                        P.op("pool", lambda e, sl_=sl_: e.tensor_copy(out=dsh[:, 0:sl_.stop - sl_.start, :].rearrange("p q (b t) -> p q b t", t=4)[:, :, :, 0], in_=shT[:, sl_, :]), reads=["shT"], writes=["dsh"])
                        n_ = sl_.stop - sl_.start
                        P.op("dve", lambda e, sl_=sl_, n_=n_: e.tensor_tensor(out=dsh[:, 0:n_, :], in0=dsh[:, 0:n_, :], in1=zbuf[:, sl_, 1:65], op=ALU.subtract), reads=["dsh", "zbuf"], writes=["dsh"])
                        P.op("dve", lambda e, sl_=sl_, n_=n_: e.tensor_tensor(out=dsh[:, 0:n_, :], in0=dsh[:, 0:n_, :], in1=mug[:, sl_].unsqueeze(2).to_broadcast([64, n_, 64]), op=ALU.mult), reads=["dsh", "mug"], writes=["dsh"])
                        P.op("dve", lambda e, sl_=sl_, n_=n_: e.tensor_tensor(out=zbuf[:, sl_, 1:65], in0=zbuf[:, sl_, 1:65], in1=dsh[:, 0:n_, :], op=ALU.add), reads=["dsh", "zbuf"], writes=["zbuf"])
                else:
                    for q4 in range(5):
                        nb = HG if q4 < 4 else 2
                        sl_ = slice(q4 * HG, q4 * HG + nb)
                        n_ = nb
                        P.op("dve", lambda e, sl_=sl_, n_=n_: e.tensor_tensor(out=dsh[:, 0:n_, :], in0=zbuf[:, sl_, 0:64], in1=zbuf[:, sl_, 1:65], op=ALU.subtract), reads=["zbuf"], writes=["dsh"])
                        P.op("dve", lambda e, sl_=sl_, n_=n_: e.tensor_tensor(out=dsh[:, 0:n_, :], in0=dsh[:, 0:n_, :], in1=mug[:, sl_].unsqueeze(2).to_broadcast([64, n_, 64]), op=ALU.mult), reads=["dsh", "mug"], writes=["dsh"])
                        P.op("pool", lambda e, sl_=sl_: e.tensor_copy(out=zbuf[:, sl_, 0:1], in_=zbuf[:, sl_, 64:65]), reads=["zbuf", "dsh"], writes=["zbuf"])
                        P.op("dve", lambda e, sl_=sl_, n_=n_: e.tensor_tensor(out=zbuf[:, sl_, 1:65], in0=zbuf[:, sl_, 1:65], in1=dsh[:, 0:n_, :], op=ALU.add), reads=["dsh", "zbuf"], writes=["zbuf"])
                zr = zbuf[:, 0:HG, 1:65]; zk = zbuf[:, HG:2 * HG, 1:65]; zv = zbuf[:, 2 * HG:3 * HG, 1:65]
                P.op("pool", lambda e: e.tensor_copy(out=vb[:, :, :], in_=zv), reads=["zbuf"], writes=[n_vb])
                P.op("act", lambda e: e.activation(out=tw[:, :], in_=zbuf[:, 4 * HG, 1:65], func=AF.Tanh), reads=["zbuf"], writes=["tw"])
                P.op("act", lambda e: e.copy(out=adc[:, :], in_=zbuf[:, 4 * HG + 1, 1:65]), reads=["zbuf"], writes=["adc"])
                bu = nbank(); ba = nbank()
                for h in range(HG):
                    P.op("pe", lambda e, h=h, bu=bu: e.matmul(ps[0:64, bu, h * 64:(h + 1) * 64], lhsT=w2g[:, h * 64:(h + 1) * 64], rhs=tw[:, :], start=True, stop=True), reads=["w2g", "tw"], writes=[("ps", bu)])
                for h in range(HG):
                    P.op("pe", lambda e, h=h, ba=ba: e.matmul(ps[0:64, ba, h * 64:(h + 1) * 64], lhsT=a2g[:, h * 64:(h + 1) * 64], rhs=adc[:, :], start=True, stop=True), reads=["a2g", "adc"], writes=[("ps", ba)])
                v3 = lambda bk_: ps[0:64, bk_, 0:HG * 64].rearrange("p (h t) -> p h t", t=64)
                P.op("dve", lambda e, bu=bu: e.tensor_tensor(out=t1[:, :, :], in0=v3(bu), in1=bc(prm[:, :, 0]), op=ALU.add), reads=[("ps", bu), "prm"], writes=["t1"])
                P.op("act", lambda e: e.activation(out=sg[:, :, :], in_=t1[:, :, :], func=AF.Sigmoid), reads=["t1"], writes=["sg"])
                P.op("dve", lambda e, ba=ba: e.tensor_tensor(out=t2[:, :, :], in0=v3(ba), in1=bc(prm[:, :, 1]), op=ALU.add), reads=[("ps", ba), "prm"], writes=["t2"])
                P.op("act", lambda e: e.activation(out=aa[:, :, :], in_=t2[:, :, :], func=AF.Sigmoid), reads=["t2"], writes=["aa"])
                f2 = lambda a_: a_[:, :, :].rearrange("p h t -> p (h t)")
                P.op("dve", lambda e: e.tensor_tensor_scan(out=f2(clr), data0=segm[:, sgi, :], data1=f2(sg), initial=0.0, op0=ALU.mult, op1=ALU.add), reads=["sg", "segm"], writes=["clr"])
                P.op("act", lambda e: e.activation(out=Ein[:, :, :], in_=clr[:, :, :], func=AF.Exp, scale=LD), reads=["clr"], writes=["Ein"])
                P.op("act", lambda e: e.activation(out=Einv[:, :, :], in_=clr[:, :, :], func=AF.Exp, scale=-LD), reads=["clr"], writes=["Einv"])
                P.op("pool", lambda e: e.tensor_tensor(out=t1[:, :, :], in0=clr[:, :, :], in1=sg[:, :, :], op=ALU.subtract), reads=["clr", "sg", "t1"], writes=["t1"])
                P.op("act", lambda e: e.activation(out=Eex[:, :, :], in_=t1[:, :, :], func=AF.Exp, scale=LD), reads=["t1"], writes=["Eex"])
                seg4 = lambda a_: a_[:, :, :].rearrange("p h (s l) -> p h s l", l=L)
                P.op("pool", lambda e: e.tensor_tensor(out=seg4(t2), in0=seg4(clr)[:, :, :, L - 1:L].to_broadcast([64, HG, nseg, L]), in1=seg4(clr), op=ALU.subtract), reads=["clr", "t2"], writes=["t2"])
                P.op("act", lambda e: e.activation(out=EC[:, :, :], in_=t2[:, :, :], func=AF.Exp, scale=LD), reads=["t2"], writes=["EC"])
                P.op("act", lambda e: e.activation(out=GC[:, :, 0:nseg], in_=seg4(clr)[:, :, :, L - 1], func=AF.Exp, scale=LD), reads=["clr"], writes=[n_gc])
                P.op("act", lambda e: e.activation(out=sgate[:, :, :], in_=zbuf[:, 3 * HG:4 * HG, 1:65], func=AF.Silu), reads=["zbuf"], writes=[n_sgt])
                P.op("pool", lambda e: e.tensor_tensor(out=kk_[:, :, :], in0=zk, in1=bc(prm[:, :, 2]), op=ALU.mult), reads=["zbuf", "prm"], writes=["kk_"])
                P.op("act", lambda e: e.activation(out=t1[:, :, :], in_=kk_[:, :, :], func=AF.Square), reads=["kk_", "t1"], writes=["t1"])
                bs = nbank()
                P.op("pe", lambda e, bs=bs: e.matmul(ps[0:64, bs, 0:HG * 64], lhsT=ones64[:, :], rhs=f2(t1), start=True, stop=True), reads=["ones64", "t1"], writes=[("ps", bs)])
                P.op("act", lambda e, bs=bs: e.activation(out=t2[:, :, :], in_=v3(bs), func=AF.Sqrt, bias=eps24[0:64, 0:1], scale=1.0), reads=[("ps", bs), "eps24", "t2"], writes=["t2"])
                P.op("dve", lambda e: e.reciprocal(out=t2[:, :, :], in_=t2[:, :, :]), reads=["t2"], writes=["t2"])
                P.op("dve", lambda e: e.tensor_tensor(out=kkn[:, :, :], in0=kk_[:, :, :], in1=t2[:, :, :], op=ALU.mult), reads=["kk_", "t2"], writes=["kkn"])
                P.op("dve", lambda e: e.scalar_tensor_tensor(out=t1[:, :, :], in0=aa[:, :, :], scalar=-1.0, in1=bc(prm[:, :, 3]), op0=ALU.add, op1=ALU.mult), reads=["aa", "prm", "t1"], writes=["t1"])
                P.op("dve", lambda e: e.scalar_tensor_tensor(out=kmod[:, :, :], in0=t1[:, :, :], scalar=1.0, in1=zk, op0=ALU.add, op1=ALU.mult), reads=["t1", "zbuf"], writes=["kmod"])
                P.op("dve", lambda e: e.scalar_tensor_tensor(out=at_[:, :, :], in0=kkn[:, :, :], scalar=-1.0, in1=Eex[:, :, :], op0=ALU.mult, op1=ALU.mult), reads=["kkn", "Eex"], writes=[n_at])
                P.op("pool", lambda e: e.tensor_tensor(out=t2[:, :, :], in0=kkn[:, :, :], in1=aa[:, :, :], op=ALU.mult), reads=["kkn", "aa", "t2"], writes=["t2"])
                P.op("pool", lambda e: e.tensor_tensor(out=bt_[:, :, :], in0=t2[:, :, :], in1=Einv[:, :, :], op=ALU.mult), reads=["t2", "Einv"], writes=[n_bt])
                P.op("pool", lambda e: e.tensor_tensor(out=bh_[:, :, :], in0=t2[:, :, :], in1=EC[:, :, :], op=ALU.mult), reads=["t2", "EC"], writes=[n_bh])
                P.op("dve", lambda e: e.tensor_tensor(out=kt_[:, :, :], in0=kmod[:, :, :], in1=Einv[:, :, :], op=ALU.mult), reads=["kmod", "Einv"], writes=[n_kt])
                P.op("pool", lambda e: e.tensor_tensor(out=kh_[:, :, :], in0=kmod[:, :, :], in1=EC[:, :, :], op=ALU.mult), reads=["kmod", "EC"], writes=[n_kh])
                P.op("dve", lambda e: e.tensor_tensor(out=rt_[:, :, :], in0=zr, in1=Ein[:, :, :], op=ALU.mult), reads=["zbuf", "Ein"], writes=[n_rt])
                P.op("pool", lambda e: e.tensor_tensor(out=t1[:, :, :], in0=zr, in1=kmod[:, :, :], op=ALU.mult), reads=["zbuf", "kmod", "t1"], writes=["t1"])
                P.op("pool", lambda e: e.tensor_tensor(out=t1[:, :, :], in0=t1[:, :, :], in1=bc(prm[:, :, 4]), op=ALU.mult), reads=["t1", "prm"], writes=["t1"])
                bb_ = nbank()
                P.op("pe", lambda e, bb_=bb_: e.matmul(ps[0:64, bb_, 0:HG * 64], lhsT=ones64[:, :], rhs=f2(t1), start=True, stop=True), reads=["ones64", "t1"], writes=[("ps", bb_)])
                P.op("dve", lambda e, bb_=bb_: e.tensor_tensor(out=bon[:, :, :], in0=v3(bb_), in1=zv, op=ALU.mult), reads=[("ps", bb_), "zbuf"], writes=[n_bon])

            mmv = lambda bk_, rows: ps[0:rows, bk_, 0:HG * 64].rearrange("p (h t) -> p h t", t=64)

            def mm8(bk_, rows, cols, lhs_fn, rhs_fn, rd):
                for h in range(HG):
                    P.op("pe", lambda e, h=h: e.matmul(mmv(bk_, rows)[:, h, 0:cols], lhsT=lhs_fn(h), rhs=rhs_fn(h), start=True, stop=True), reads=rd, writes=[("ps", bk_)])

            def indep(c0, C, nsq, p, gci, pa=None, q2=0):
                cs = slice(c0, c0 + C)
                pa = p if pa is None else pa
                at_ = at2[pa]; rt_ = rt2[pa]; Vt = Vt2[p]; Bt = Bt2[p]; Kt = Kt2[p]; TT = TT2[p]; AakT = Aak2[p]; ArbT = Arb2[p]; ArkT = Ark2[p]; GC = GC2[pa]
                n_at = f"at_{pa}"; n_rt = f"rt_{pa}"
                bt_ = bt2[q2]; kt_ = kt2[q2]; bh_ = bh2[q2]; kh_ = kh2[q2]; vb = vb2[q2]
                n_bt = f"bt_{q2}"; n_kt = f"kt_{q2}"; n_bh = f"bh_{q2}"; n_kh = f"kh_{q2}"; n_vb = f"vb{q2}"
                for si, (src, dst, nm) in enumerate(((vb, Vt, f"Vt{p}"), (bh_, Bt, f"Bt{p}"), (kh_, Kt, f"Kt{p}"))):
                    hs, cs_ = (0, si * 64) if si < 2 else (4, 0)
                    for h in range(HG):
                        P.op("pe", lambda e, h=h, src=src, hs=hs, cs_=cs_: e.transpose(psb[0:C, hs + h, cs_:cs_ + 64], src[:, h, cs], ident_b[0:64, 0:64]), reads=[n_vb, n_bh, n_kh, "ident_b"], writes=["psb"])
                    P.op("act", lambda e, dst=dst, hs=hs, cs_=cs_: e.copy(out=dst[0:C, :, :], in_=psb[0:C, hs:hs + HG, cs_:cs_ + 64]), reads=["psb"], writes=[nm])
                A_ = lambda a_: (lambda h: a_[:, h, cs])
                mb = lambda m_: m_[0:C, 0:C].unsqueeze(1).to_broadcast([C, HG, C])
                bk = nbank(); mm8(bk, C, C, A_(bt_), A_(at_), [n_bt, n_at])
                P.op("dve", lambda e, bk=bk: e.tensor_tensor(out=NT0[0:C, :, 0:C], in0=mmv(bk, C)[:, :, 0:C], in1=mb(MU), op=ALU.mult), reads=[("ps", bk), "MU"], writes=["NT0"])
                bk = nbank(); mm8(bk, C, C, A_(at_), A_(bt_), [n_bt, n_at])
                P.op("dve", lambda e, bk=bk: e.tensor_tensor(out=N0[0:C, :, 0:C], in0=mmv(bk, C)[:, :, 0:C], in1=mb(ML), op=ALU.mult), reads=[("ps", bk), "ML"], writes=["N0"])
                for (lf, rf, dst, nm, msk, mn) in ((kt_, at_, AakT, f"AakT{p}", MU, "MU"), (bt_, rt_, ArbT, f"ArbT{p}", MUI, "MUI"), (kt_, rt_, ArkT, f"ArkT{p}", MUI, "MUI")):
                    bk = nbank(); mm8(bk, C, C, A_(lf), A_(rf), [n_kt, n_bt, n_at, n_rt])
                    P.op("dve", lambda e, bk=bk, dst=dst, msk=msk: e.tensor_tensor(out=dst[0:C, :, 0:C], in0=mmv(bk, C)[:, :, 0:C], in1=mb(msk), op=ALU.mult), reads=[("ps", bk), mn], writes=[nm])
                Rs = [(R0, "R0"), (R1, "R1")]
                Ns = [(N0, "N0"), (N1, "N1")]; NTs_ = [(NT0, "NT0"), (NT1, "NT1")]
                rdst0 = TT if nsq == 0 else R0
                P.op("pool", lambda e, rdst0=rdst0: e.tensor_tensor(out=rdst0[0:C, :, 0:C], in0=NT0[0:C, :, 0:C], in1=ident[0:C, 0:C].unsqueeze(1).to_broadcast([C, HG, C]), op=ALU.add),
                     reads=["NT0", "ident"], writes=[f"TT{p}" if nsq == 0 else "R0"])
                for k in range(1, nsq + 1):
                    (Nc, Ncn), (Nn, Nnn) = Ns[(k - 1) % 2], Ns[k % 2]
                    (NTc, NTcn), (NTn, NTnn) = NTs_[(k - 1) % 2], NTs_[k % 2]
                    (Rc, Rcn) = Rs[(k - 1) % 2]
                    (Rn, Rnn) = (TT, f"TT{p}") if k == nsq else Rs[k % 2]
                    bk = nbank()
                    mm8(bk, C, C, lambda h, NTc=NTc: NTc[0:C, h, 0:C], lambda h, Nc=Nc: Nc[0:C, h, 0:C], [NTcn, Ncn])
                    P.op("act", lambda e, bk=bk, Nn=Nn: e.copy(out=Nn[0:C, :, 0:C], in_=mmv(bk, C)[:, :, 0:C]), reads=[("ps", bk)], writes=[Nnn])
                    if k < nsq:
                        bk = nbank()
                        mm8(bk, C, C, lambda h, Nc=Nc: Nc[0:C, h, 0:C], lambda h, NTc=NTc: NTc[0:C, h, 0:C], [NTcn, Ncn])
                        P.op("act", lambda e, bk=bk, NTn=NTn: e.copy(out=NTn[0:C, :, 0:C], in_=mmv(bk, C)[:, :, 0:C]), reads=[("ps", bk)], writes=[NTnn])
                    bk = nbank()
                    mm8(bk, C, C, lambda h, Nn=Nn: Nn[0:C, h, 0:C], lambda h, Rc=Rc: Rc[0:C, h, 0:C], [Nnn, Rcn])
                    P.op("dve", lambda e, bk=bk, Rc=Rc, Rn=Rn: e.tensor_tensor(out=Rn[0:C, :, 0:C], in0=mmv(bk, C)[:, :, 0:C], in1=Rc[0:C, :, 0:C], op=ALU.add), reads=[("ps", bk), Rcn], writes=[Rnn])

            def dep(c0, C, p, pa=None, gci=0):
                cs = slice(c0, c0 + C)
                pa = p if pa is None else pa
                at_ = at2[pa]; rt_ = rt2[pa]; Vt = Vt2[p]; Bt = Bt2[p]; Kt = Kt2[p]; TT = TT2[p]; AakT = Aak2[p]; ArbT = Arb2[p]; ArkT = Ark2[p]; GC = GC2[pa]
                n_at = f"at_{pa}"; n_rt = f"rt_{pa}"
                bk = nbank()
                for h in range(HG):
                    P.op("pe", lambda e, h=h, bk=bk: e.matmul(mmv(bk, C)[:, h, :], lhsT=at_[:, h, cs], rhs=Pb[:, h, :], start=True, stop=False), reads=[n_at, "Pb"], writes=[("ps", bk)])
                    P.op("pe", lambda e, h=h, bk=bk: e.matmul(mmv(bk, C)[:, h, :], lhsT=AakT[0:C, h, 0:C], rhs=Vt[0:C, h, :], start=False, stop=True), reads=[f"AakT{p}", f"Vt{p}"], writes=[("ps", bk)])
                P.op("act", lambda e, bk=bk: e.copy(out=U0[0:C, :, :], in_=mmv(bk, C)), reads=[("ps", bk)], writes=["U0"])
                bk = nbank()
                mm8(bk, C, 64, lambda h: TT[0:C, h, 0:C], lambda h: U0[0:C, h, :], [f"TT{p}", "U0"])
                P.op("dve", lambda e, bk=bk: e.tensor_copy(out=Uf[0:C, :, :], in_=mmv(bk, C)), reads=[("ps", bk)], writes=["Uf"])
                bk = nbank(); bk2 = nbank()
                for h in range(HG):
                    P.op("pe", lambda e, h=h, bk=bk: e.matmul(mmv(bk, 64)[:, h, 0:C], lhsT=Pb[:, h, :], rhs=rt_[:, h, cs], start=True, stop=False), reads=["Pb", n_rt], writes=[("ps", bk)])
                    P.op("pe", lambda e, h=h, bk=bk: e.matmul(mmv(bk, 64)[:, h, 0:C], lhsT=Uf[0:C, h, :], rhs=ArbT[0:C, h, 0:C], start=False, stop=False), reads=["Uf", f"ArbT{p}"], writes=[("ps", bk)])
                    P.op("pe", lambda e, h=h, bk=bk: e.matmul(mmv(bk, 64)[:, h, 0:C], lhsT=Vt[0:C, h, :], rhs=ArkT[0:C, h, 0:C], start=False, stop=True), reads=[f"Vt{p}", f"ArkT{p}"], writes=[("ps", bk)])
                for h in range(HG):
                    P.op("pe", lambda e, h=h, bk2=bk2: e.matmul(mmv(bk2, 64)[:, h, :], lhsT=Bt[0:C, h, :], rhs=Uf[0:C, h, :], start=True, stop=False), reads=[f"Bt{p}", "Uf"], writes=[("ps", bk2)])
                    P.op("pe", lambda e, h=h, bk2=bk2: e.matmul(mmv(bk2, 64)[:, h, :], lhsT=Kt[0:C, h, :], rhs=Vt[0:C, h, :], start=False, stop=True), reads=[f"Kt{p}", f"Vt{p}"], writes=[("ps", bk2)])
                P.op("dve", lambda e: e.tensor_tensor(out=Pst[:, :, :], in0=Pst[:, :, :], in1=GC[:, :, gci:gci + 1].to_broadcast([64, HG, 64]), op=ALU.mult), reads=["Pst", f"GC{pa}"], writes=["Pst"])
                P.op("dve", lambda e, bk2=bk2: e.tensor_tensor(out=Pst[:, :, :], in0=Pst[:, :, :], in1=mmv(bk2, 64), op=ALU.add), reads=[("ps", bk2), "Pst"], writes=["Pst"])
                P.op("act", lambda e: e.copy(out=Pb[:, :, :], in_=Pst[:, :, :]), reads=["Pst"], writes=["Pb"])
                P.op("dve", lambda e, bk=bk: e.tensor_copy(out=outT[:, :, cs], in_=mmv(bk, 64)[:, :, 0:C]), reads=[("ps", bk)], writes=["outT"])

            def finish_batch(col0, p, G=G):
                bon = bon2[p]; sgate = sg2[p]
                f2 = lambda a_: a_[:, :, :].rearrange("p h t -> p (h t)")
                v3 = lambda bk_: ps[0:64, bk_, 0:HG * 64].rearrange("p (h t) -> p h t", t=64)
                b1 = nbank()
                P.op("pe", lambda e, b1=b1: e.matmul(ps[0:64, b1, 0:HG * 64], lhsT=ones64[:, :], rhs=f2(outT), start=True, stop=True), reads=["ones64", "outT"], writes=[("ps", b1)])
                P.op("dve", lambda e, b1=b1: e.scalar_tensor_tensor(out=f1[:, :, :], in0=v3(b1), scalar=-1.0 / 64, in1=outT[:, :, :], op0=ALU.mult, op1=ALU.add), reads=[("ps", b1), "outT", "f1"], writes=["f1"])
                P.op("act", lambda e: e.activation(out=f2_[:, :, :], in_=f1[:, :, :], func=AF.Square), reads=["f1", "f2_"], writes=["f2_"])
                b2 = nbank()
                P.op("pe", lambda e, b2=b2: e.matmul(ps[0:64, b2, 0:HG * 64], lhsT=ones64[:, :], rhs=f2(f2_), start=True, stop=True), reads=["ones64", "f2_"], writes=[("ps", b2)])
                P.op("act", lambda e, b2=b2: e.activation(out=f2_[:, :, :], in_=v3(b2), func=AF.Sqrt, bias=epsln[0:64, 0:1], scale=1.0 / 64), reads=[("ps", b2), "epsln", "f2_"], writes=["f2_"])
                P.op("dve", lambda e: e.reciprocal(out=f2_[:, :, :], in_=f2_[:, :, :]), reads=["f2_"], writes=["f2_"])
                P.op("dve", lambda e: e.tensor_tensor(out=f1[:, :, :], in0=f1[:, :, :], in1=f2_[:, :, :], op=ALU.mult), reads=["f1", "f2_"], writes=["f1"])
                P.op("pool", lambda e: e.tensor_tensor(out=f1[:, :, :], in0=f1[:, :, :], in1=bc(prm[:, :, 5]), op=ALU.mult), reads=["f1", "prm"], writes=["f1"])
                P.op("pool", lambda e: e.tensor_tensor(out=f1[:, :, :], in0=f1[:, :, :], in1=bc(prm[:, :, 6]), op=ALU.add), reads=["f1", "prm"], writes=["f1"])
                P.op("pool", lambda e: e.tensor_tensor(out=f1[:, :, :], in0=f1[:, :, :], in1=bon[:, :, :], op=ALU.add), reads=["f1", f"bon{p}"], writes=["f1"])
                ev = lambda a_: a_[:, :, :].rearrange("p (q e) t -> p q e t", e=2)
                P.op("dve", lambda e: e.tensor_tensor(out=oaT[0:64, (HG // 2) * G:(HG // 2) * (G + 1), col0:col0 + 64], in0=ev(f1)[:, :, 0, :], in1=ev(sgate)[:, :, 0, :], op=ALU.mult), reads=["f1", f"sgate{p}"], writes=["oaT"])
                P.op("dve", lambda e: e.tensor_tensor(out=oodd[:, :, :], in0=ev(f1)[:, :, 1, :], in1=ev(sgate)[:, :, 1, :], op=ALU.mult), reads=["f1", f"sgate{p}"], writes=["oodd"])
                P.dma("sp", lambda e: e.dma_start(out=oaT[64:128, (HG // 2) * G:(HG // 2) * (G + 1), col0:col0 + 64], in_=oodd[:, :, :]), reads=["oodd"], writes=["oaT"], semkey="oodd")

            def TB(c):
                bank_grp[0] = "tb"; token_batch(c * 64, 64, 1, c % 3, c % 2); bank_grp[0] = None

            def IND(c):
                bank_grp[0] = "ind"; indep(0, 64, 5, c % 2, 0, pa=c % 3, q2=c % 2); bank_grp[0] = None

            def DEPF(c):
                bank_grp[0] = "dep"; dep(0, 64, c % 2, pa=c % 3); finish_batch(c * 64, c % 3); bank_grp[0] = None

            if NCH > 0:
                TB(0); IND(0)
            if NCH > 1:
                TB(1)
            for ch in range(NCH):
                names = []
                if ch + 2 < NCH:
                    P.thread("tb"); TB(ch + 2); names.append("tb")
                if ch + 1 < NCH:
                    P.thread("ind"); IND(ch + 1); names.append("ind")
                P.thread("dep"); DEPF(ch); names.append("dep")
                P.merge(names)
            bk = nbank()
            for h in range(HG):
                P.op("pe", lambda e, h=h, bk=bk: e.transpose(ps[0:64, bk, h * 64:(h + 1) * 64], Pst[:, h, :], ident[0:64, 0:64]), reads=["Pst", "ident"], writes=[("ps", bk)])
            P.op("act", lambda e, bk=bk: e.copy(out=Ssb[:, :, :], in_=ps[0:64, bk, 0:HG * 64].rearrange("p (h t) -> p h t", t=64)), reads=[("ps", bk)], writes=["Ssb"])
            P.dma("sp", lambda e, G=G: e.dma_start(out=wkv_p[HG * G:HG * G + HG, :, :].rearrange("h i j -> i h j"), in_=Ssb[:, :, :]), reads=["Ssb"], store=True, semkey="wkvp")
            if NSEQ > 0:
                token_batch(2048, 4, 16, 0, 0)
            for bq in range(NSEQ):
                p = bq % 2
                indep(4 * bq, 4, 1, p, bq, pa=0)
                P.dma("sp", lambda e, bq=bq, G=G: e.dma_start(out=Ssb[:, :, :], in_=swkv_d[bq, HG * G:HG * G + HG, :, :].rearrange("h i j -> i h j")), writes=["Ssb"], semkey="ssb")
                bk = nbank()
                for h in range(HG):
                    P.op("pe", lambda e, h=h, bk=bk: e.transpose(ps[0:64, bk, h * 64:(h + 1) * 64], Ssb[:, h, :], ident[0:64, 0:64]), reads=["Ssb", "ident"], writes=[("ps", bk)])
                P.op("act", lambda e, bk=bk: e.copy(out=Pst[:, :, :], in_=ps[0:64, bk, 0:HG * 64].rearrange("p (h t) -> p h t", t=64)), reads=[("ps", bk)], writes=["Pst"])
                P.op("act", lambda e: e.copy(out=Pb[:, :, :], in_=Pst[:, :, :]), reads=["Pst"], writes=["Pb"])
                dep(4 * bq, 4, p, pa=0, gci=bq)
                bk = nbank()
                for h in range(HG):
                    P.op("pe", lambda e, h=h, bk=bk: e.transpose(ps[0:64, bk, h * 64:(h + 1) * 64], Pst[:, h, :], ident[0:64, 0:64]), reads=["Pst", "ident"], writes=[("ps", bk)])
                P.op("act", lambda e, bk=bk: e.copy(out=Ssb[:, :, :], in_=ps[0:64, bk, 0:HG * 64].rearrange("p (h t) -> p h t", t=64)), reads=[("ps", bk)], writes=["Ssb"])
                P.dma("sp", lambda e, bq=bq, G=G: e.dma_start(out=wkv_s[bq, HG * G:HG * G + HG, :, :].rearrange("h i j -> i h j"), in_=Ssb[:, :, :]), reads=["Ssb"], store=True, semkey="wkvs")
            if NSEQ > 0:
                finish_batch(2048, 0)
        if dbg_oa is not None:
            P.dma("sp", lambda e: e.dma_start(out=dbg_oa, in_=oaT[:, :, :]), reads=["oaT"], store=True)
        A.release(mR); P.barrier()
        sgm = A.alloc([128, 2, 512]); mtmp = A.alloc([128, 512])

        def merge_branch(srcT, srck, wproj, gcol0, first, sgm, mtmp):
            for cb in range(8):
                slg = load_w([(0, gcol0 + cb * 128, 128)], 128)
                slp = load_w([(0, cb * 128, 128)], 128, src=wproj)
                for ni, (n0, nn) in enumerate(NTL):
                    bg = nbank(); bp = nbank(); ssl = ni % 2
                    proj_fm(slg, 128, n0, nn, bg)
                    P.op("act", lambda e, bg=bg, nn=nn, ssl=ssl, sgm=sgm: e.activation(out=sgm[:, ssl, 0:nn], in_=ps[:, bg, 0:nn], func=AF.Sigmoid), reads=[("ps", bg)], writes=[("sgm", ssl)])
                    for kc in range(8):
                        P.op("pe", lambda e, kc=kc, bp=bp, slp=slp, n0=n0, nn=nn: e.matmul(ps[:, bp, 0:nn], lhsT=wb[:, slp, kc, 0:128], rhs=srcT[:, kc, n0:n0 + nn], start=(kc == 0), stop=(kc == 7)),
                             reads=[("wb", slp)] + srck, writes=[("ps", bp)])
                    if first:
                        P.op("dve", lambda e, bp=bp, nn=nn, ssl=ssl, cb=cb, n0=n0, sgm=sgm: e.tensor_tensor(out=mT[:, cb, n0:n0 + nn], in0=ps[:, bp, 0:nn], in1=sgm[:, ssl, 0:nn], op=ALU.mult),
                             reads=[("ps", bp), ("sgm", ssl)], writes=["mT"])
                    else:
                        P.op("dve", lambda e, bp=bp, nn=nn, ssl=ssl, sgm=sgm, mtmp=mtmp: e.tensor_tensor(out=mtmp[:, 0:nn], in0=ps[:, bp, 0:nn], in1=sgm[:, ssl, 0:nn], op=ALU.mult),
                             reads=[("ps", bp), ("sgm", ssl)], writes=["mtmp"])
                        P.op("dve", lambda e, nn=nn, cb=cb, n0=n0, mtmp=mtmp: e.tensor_tensor(out=mT[:, cb, n0:n0 + nn], in0=mT[:, cb, n0:n0 + nn], in1=mtmp[:, 0:nn], op=ALU.add),
                             reads=["mtmp", "mT"], writes=["mT"])

        merge_branch(oaT, ["oaT"], w_pa, C_GA, True, sgm, mtmp)
        A.release(mA); P.barrier()
        if STAGE < 3:
            return finish()
        KTd = A.alloc([128, 2, NT], BF16)
        Vaug = A.alloc([128, 17, 2, 66], BF16)
        wi_t = A.alloc([128, 17, 8])
        m1 = A.mark()
        knT = A.alloc([128, NT]); sq = A.alloc([128, 512]); rs = A.alloc([128, 512])
        otok = A.alloc([128, 2, 128]); vtok = A.alloc([128, 2, 128]); kitok = A.alloc([128, 2, 64])
        xl = A.alloc([128, 8, 17], BF16); shrow = A.alloc([17, 2, 128])

        cur = {"sq": sq, "rs": rs}

        def normed_block(parts, dst_fn, scale_ap, key):
            sq, rs = cur["sq"], cur["rs"]
            sl = load_w(parts, 128)
            for (n0, nn) in NTL:
                b = nbank()
                proj_fm(sl, 128, n0, nn, b)
                P.op("act", lambda e, b=b, nn=nn, sq=sq: e.activation(out=sq[:, 0:nn], in_=ps[:, b, 0:nn], func=AF.Square), reads=[("ps", b)], writes=["sq"])
                b2 = nbank()
                P.op("pe", lambda e, b2=b2, nn=nn, sq=sq: e.matmul(ps[:, b2, 0:nn], lhsT=bones[:, :], rhs=sq[:, 0:nn], start=True, stop=True), reads=["bones", "sq"], writes=[("ps", b2)])
                P.op("act", lambda e, b2=b2, nn=nn, rs=rs: e.activation(out=rs[:, 0:nn], in_=ps[:, b2, 0:nn], func=AF.Sqrt, bias=eps6[:, 0:1], scale=1.0 / 64), reads=[("ps", b2), "eps6"], writes=["rs"])
                P.op("dve", lambda e, nn=nn, rs=rs: e.reciprocal(out=rs[:, 0:nn], in_=rs[:, 0:nn]), reads=["rs"], writes=["rs"])
                P.op("dve", lambda e, b=b, n0=n0, nn=nn, rs=rs: e.scalar_tensor_tensor(out=dst_fn(n0, nn), in0=ps[:, b, 0:nn], scalar=scale_ap, in1=rs[:, 0:nn], op0=ALU.mult, op1=ALU.mult), reads=[("ps", b), "rs"], writes=[key])

        normed_block([(0, C_AK, 128)], lambda n0, nn: knT[:, n0:n0 + nn], gk2[:, 0:1], "knT")
        for kvh in range(2):
            normed_block([(0, C_AK + kvh * 64, 64), (64, C_AK + kvh * 64, 64)], lambda n0, nn, kvh=kvh: KTd[:, kvh, n0:n0 + nn], gkq[:, 0:1], ("KTd", kvh))
        for ti, (t0, n) in enumerate(TT):
            b = nbank(); sl = ti % 2
            P.op("pe", lambda e, b=b, t0=t0, n=n: e.transpose(ps[0:n, b, 0:128], knT[:, t0:t0 + n], ident[:, :]), reads=["knT", "ident"], writes=[("ps", b)])
            P.op("act", lambda e, b=b, n=n, sl=sl: e.copy(out=otok[0:n, sl, :], in_=ps[0:n, b, 0:128]), reads=[("ps", b)], writes=[("otok", sl)])
            dst = k_p[t0:t0 + n, :] if ti < 16 else k_s[:, :]
            P.dma("sp", lambda e, dst=dst, n=n, sl=sl: e.dma_start(out=dst, in_=otok[0:n, sl, :]), reads=[("otok", sl)], store=True)
        P.op("dve", lambda e: e.memset(Vaug[:, :, :, :], 1.0), writes=[("Vaug", ti) for ti in range(17)])
        sl_v = load_w([(0, C_AV, 128)], 128)
        for ti, (t0, n) in enumerate(TT):
            b = nbank(); sl = ti % 2
            proj_tm(sl_v, 128, t0, n, b)
            P.op("act", lambda e, b=b, n=n, sl=sl: e.copy(out=vtok[0:n, sl, :], in_=ps[0:n, b, 0:128]), reads=[("ps", b)], writes=[("vtok", sl)])
            P.op("act", lambda e, b=b, n=n, ti=ti: e.copy(out=Vaug[0:n, ti, :, 0:64], in_=ps[0:n, b, 0:128].rearrange("p (k d) -> p k d", k=2)), reads=[("ps", b)], writes=[("Vaug", ti)])
            dst = v_p[t0:t0 + n, :] if ti < 16 else v_s[:, :]
            P.dma("sp", lambda e, dst=dst, n=n, sl=sl: e.dma_start(out=dst, in_=vtok[0:n, sl, :]), reads=[("vtok", sl)], store=True)
        sl_k = load_w([(0, C_KI, 72)], 72)
        for ti, (t0, n) in enumerate(TT):
            b = nbank(); sl = ti % 2
            proj_tm(sl_k, 72, t0, n, b)
            P.op("act", lambda e, b=b, n=n, sl=sl: e.copy(out=kitok[0:n, sl, :], in_=ps[0:n, b, 0:64]), reads=[("ps", b)], writes=[("kitok", sl)])
            P.op("act", lambda e, b=b, n=n, ti=ti: e.copy(out=wi_t[0:n, ti, :], in_=ps[0:n, b, 64:72]), reads=[("ps", b)], writes=[("wi_t", ti)])
            dst = ki_p[t0:t0 + n, :] if ti < 16 else ki_s[:, :]
            P.dma("sp", lambda e, dst=dst, n=n, sl=sl: e.dma_start(out=dst, in_=kitok[0:n, sl, :]), reads=[("kitok", sl)], store=True)
        P.op("dve", lambda e: e.tensor_copy(out=xl[:, :, 0:1], in_=xnT[:, :, 2047:2048]), reads=["xnT"], writes=["xl"])
        P.op("dve", lambda e: e.tensor_copy(out=xl[:, :, 1:17], in_=xnT[:, :, 2048:2112].rearrange("p k (b t) -> p k b t", t=4)[:, :, :, 3]), reads=["xnT"], writes=["xl"])
        for cb in range(33):
            sl = load_w([(0, cb * 128, 128)], 128)
            b = nbank(); ssl = cb % 2
            for kc in range(8):
                P.op("pe", lambda e, kc=kc, sl=sl, b=b: e.matmul(ps[0:17, b, 0:128], lhsT=xl[:, kc, :], rhs=wb[:, sl, kc, 0:128], start=(kc == 0), stop=(kc == 7)),
                     reads=[("wb", sl), "xl"], writes=[("ps", b)])
            P.op("act", lambda e, b=b, ssl=ssl: e.copy(out=shrow[:, ssl, :], in_=ps[0:17, b, 0:128]), reads=[("ps", b)], writes=[("shrow", ssl)])
            P.dma("sp", lambda e, cb=cb, ssl=ssl: e.dma_start(out=sh_p[:, cb * 128:(cb + 1) * 128], in_=shrow[0:1, ssl, :]), reads=[("shrow", ssl)], store=True, semkey=("shp", ssl))
            P.dma("sp", lambda e, cb=cb, ssl=ssl: e.dma_start(out=sh_s[:, cb * 128:(cb + 1) * 128], in_=shrow[1:17, ssl, :]), reads=[("shrow", ssl)], store=True, semkey=("shs", ssl))
        A.release(m1); P.barrier()
        if STAGE < 6:
            return finish()

        m2 = A.mark()
        qT = A.alloc([128, 8, NT], BF16)
        obT = qT
        QK = [("qT", p) for p in range(8)]
        mq = A.mark()
        cur["sq"] = A.alloc([128, 512]); cur["rs"] = A.alloc([128, 512])
        for p8 in range(8):
            normed_block([(0, C_Q + p8 * 128, 128)], lambda n0, nn, p8=p8: qT[:, p8, n0:n0 + nn], 1.0, ("qT", p8))
        A.release(mq); P.barrier()
        qiTs = A.alloc([128, 4, 64], BF16); kiT2s = A.alloc([128, 64], BF16)
        relb = A.alloc([32, 16]); rb31 = A.alloc([32, 1, 16]); rbd = A.alloc([32, 16]); rbt = A.alloc([32, 16])
        rbh = A.alloc([32, 16], BF16); rbl = A.alloc([32, 16], BF16)
        negI = A.alloc([128, 128], BF16)
        m3 = A.mark()
        qiT = A.alloc([128, 4, NT], BF16)
        for p4 in range(4):
            sl = load_w([(0, C_QI + p4 * 128, 128)], 128)
            for (n0, nn) in NTL:
                b = nbank(); proj_fm(sl, 128, n0, nn, b)
                P.op("act", lambda e, b=b, p4=p4, n0=n0, nn=nn: e.copy(out=qiT[:, p4, n0:n0 + nn], in_=ps[:, b, 0:nn]), reads=[("ps", b)], writes=[("qiT", p4)])
        kiT2 = A.alloc([128, NT], BF16)
        sl = load_w([(0, C_KI, 64), (64, C_KI, 64)], 128)
        for (n0, nn) in NTL:
            b = nbank(); proj_fm(sl, 128, n0, nn, b)
            P.op("act", lambda e, b=b, n0=n0, nn=nn: e.copy(out=kiT2[:, n0:n0 + nn], in_=ps[:, b, 0:nn]), reads=[("ps", b)], writes=["kiT2"])
        P.op("act", lambda e: e.copy(out=qiTs[:, :, :], in_=qiT[:, :, 2048:2112]), reads=[("qiT", p) for p in range(4)], writes=["qiTs"])
        P.op("act", lambda e: e.copy(out=kiT2s[:, :], in_=kiT2[:, 2048:2112]), reads=["kiT2"], writes=["kiT2s"])
        P.dma("sp", lambda e: e.dma_start(out=relb[:, :], in_=rel_bias_d), writes=["relb"])
        P.dma("sp", lambda e: e.dma_start(out=rb31[:, :, :], in_=rel_bias_d[31:32, :].partition_broadcast(32)), writes=["rb31"])
        P.op("dve", lambda e: e.tensor_tensor(out=rbd[:, :], in0=relb[:, :], in1=rb31[:, 0, :], op=ALU.subtract), reads=["relb", "rb31"], writes=["rbd"])
        P.op("dve", lambda e: e.tensor_copy(out=rbh[:, :], in_=rbd[:, :]), reads=["rbd"], writes=["rbh"])
        P.op("dve", lambda e: e.tensor_copy(out=rbt[:, :], in_=rbh[:, :]), reads=["rbh"], writes=["rbt"])
        P.op("dve", lambda e: e.tensor_tensor(out=rbl[:, :], in0=rbd[:, :], in1=rbt[:, :], op=ALU.subtract), reads=["rbd", "rbt"], writes=["rbl"])
        TzT = A.alloc([128, 2, 128, 16])
        caus0 = A.alloc([128, 128]); causS = A.alloc([128, 128])
        acc = A.alloc([128, 2048]); work = A.alloc([128, 2048]); relu_t = A.alloc([128, 2, 512])
        ns2 = [A.alloc([128, 2048], BF16), A.alloc([128, 2048], BF16)]; mx = A.alloc([128, 8])
        PT = A.alloc([128, 2, 8, 128], BF16); tmpl = A.alloc([128, 8, 128])
        obn = A.alloc([128, 16, 64], BF16); rec = A.alloc([128, 16])
        ohst = work.bitcast(BF16)[0:32, 0:4096].rearrange("p (s q k) -> p s q k", s=2, q=16)
        P.dma("sp", lambda e: e.dma_start(out=caus0[:, :], in_=caus0_d), writes=["caus0"])
        P.dma("sp", lambda e: e.dma_start(out=causS[:, :], in_=causS_d), writes=["causS"])
        P.op("act", lambda e: e.mul(out=negI[:, :], in_=ident[:, :], mul=-30000.0), reads=["ident"], writes=["negI"])
        for ty in range(2):
            for qc in range(8):
                slot = (ty * 8 + qc) % 2
                bank = 3 + slot
                P.dma("sp", lambda e, ty=ty, qc=qc, slot=slot: e.dma_start(out=ohst[:, slot, :, :], in_=oh_d[:, ty, qc * 16:(qc + 1) * 16, :]), writes=["work"], semkey=("oh", slot))
                for ql in range(16):
                    P.op("pe", lambda e, slot=slot, bank=bank, ql=ql: e.matmul(ps[:, bank, ql * 16:(ql + 1) * 16], lhsT=ohst[:, slot, ql, :], rhs=rbh[:, :], start=True, stop=False),
                         reads=["work", "rbh"], writes=[("ps", bank)])
                    P.op("pe", lambda e, slot=slot, bank=bank, ql=ql: e.matmul(ps[:, bank, ql * 16:(ql + 1) * 16], lhsT=ohst[:, slot, ql, :], rhs=rbl[:, :], start=False, stop=True),
                         reads=["work", "rbl"], writes=[("ps", bank)])
                if ty == 0:
                    P.op("dve", lambda e, bank=bank, qc=qc: e.tensor_tensor(out=TzT[:, 0, qc * 16:(qc + 1) * 16, :], in0=ps[:, bank, 0:256].rearrange("s (q h) -> s q h", h=16),
                                                                            in1=caus0[:, qc * 16:(qc + 1) * 16].unsqueeze(2).to_broadcast([128, 16, 16]), op=ALU.add),
                         reads=[("ps", bank), "caus0"], writes=["TzT"])
                else:
                    P.op("dve", lambda e, bank=bank, qc=qc: e.tensor_copy(out=TzT[:, 1, qc * 16:(qc + 1) * 16, :], in_=ps[:, bank, 0:256].rearrange("s (q h) -> s q h", h=16)),
                         reads=[("ps", bank)], writes=["TzT"])

        NQB = int(os.environ.get("NQB", "16"))
        PV_FIRST = (0, 7, 14)
        icnt = [0]
        def idx_topk(i):
            q0 = i * 128
            notsel = ns2[i % 2]; nsn = f"notsel{i % 2}"
            nk = 128 * (i + 1)
            for c0 in range(0, nk, 512):
                w = min(512, nk - c0)
                ck = ("acc", c0 // 512)
                for h in range(8):
                    half, p4 = h % 2, h // 2
                    bank = 3 + (icnt[0] % 4); rsl = icnt[0] % 2; icnt[0] += 1
                    P.op("pe", lambda e, bank=bank, w=w, half=half, p4=p4, q0=q0, c0=c0: e.matmul(ps[:, bank, 0:w], lhsT=qiT[half * 64:(half + 1) * 64, p4, q0:q0 + 128],
                                                                                              rhs=kiT2[half * 64:(half + 1) * 64, c0:c0 + w], start=True, stop=True),
                         reads=[("qiT", p4), "kiT2"], writes=[("ps", bank)])
                    P.op("act", lambda e, bank=bank, w=w, rsl=rsl: e.activation(out=relu_t[:, rsl, 0:w], in_=ps[:, bank, 0:w], func=AF.Relu), reads=[("ps", bank)], writes=[("relu_t", rsl)])
                    if h == 0:
                        P.op("dve", lambda e, w=w, rsl=rsl, c0=c0, i=i: e.tensor_scalar(out=acc[:, c0:c0 + w], in0=relu_t[:, rsl, 0:w], scalar1=wi_t[:, i, 0:1], scalar2=None, op0=ALU.mult),
                             reads=[("relu_t", rsl), ("wi_t", i)], writes=[ck])
                    else:
                        P.op("dve", lambda e, w=w, rsl=rsl, c0=c0, i=i, h=h: e.scalar_tensor_tensor(out=acc[:, c0:c0 + w], in0=relu_t[:, rsl, 0:w], scalar=wi_t[:, i, h:h + 1], in1=acc[:, c0:c0 + w],
                                                                                                 op0=ALU.mult, op1=ALU.add),
                             reads=[("relu_t", rsl), ("wi_t", i), ck], writes=[ck])
            acck = [("acc", c) for c in range((nk + 511) // 512)]
            P.op("dve", lambda e, i=i: e.tensor_tensor(out=acc[:, i * 128:(i + 1) * 128], in0=acc[:, i * 128:(i + 1) * 128], in1=causS[:, :], op=ALU.add),
                 reads=acck + ["causS"], writes=acck)
            masked = i >= 2
            if masked:
                for r in range(32):
                    src = acc if r == 0 else work
                    P.op("dve", lambda e, src=src, nk=nk: e.max(out=mx[:, :], in_=src[:, 0:nk]), reads=acck + ["work"], writes=["mx"])
                    if r < 31:
                        P.op("dve", lambda e, src=src, nk=nk: e.match_replace(out=work[:, 0:nk], in_to_replace=mx[:, :], in_values=src[:, 0:nk], imm_value=-1e30),
                             reads=acck + ["mx", "work"], writes=["work"])
                P.op("dve", lambda e, nk=nk: e.tensor_scalar(out=notsel[:, 0:nk], in0=acc[:, 0:nk], scalar1=mx[:, 7:8], scalar2=None, op0=ALU.is_lt),
                     reads=acck + ["mx"], writes=[nsn])
        def attend(i):
            q0 = i * 128
            masked = i >= 2
            notsel = ns2[i % 2]; nsn = f"notsel{i % 2}"
            for j in range(i + 1):
                ty = i - j
                for g in range(2):
                    slot = g
                    bA, bB = (3, 4) if slot == 0 else (5, 6)
                    for hh in range(8):
                        half, p4 = hh // 4, hh % 4
                        pair = g * 4 + p4
                        bank = bA if half == 0 else bB
                        P.op("pe", lambda e, bank=bank, p4=p4, half=half, g=g, j=j, pair=pair, q0=q0, masked=masked: e.matmul(
                            ps[:, bank, p4 * 128:(p4 + 1) * 128], lhsT=KTd[half * 64:(half + 1) * 64, g, j * 128:(j + 1) * 128],
                            rhs=qT[half * 64:(half + 1) * 64, pair, q0:q0 + 128], start=True, stop=(not masked)),
                            reads=[("KTd", g), ("qT", pair)], writes=[("ps", bank)])
                        if masked:
                            P.op("pe", lambda e, bank=bank, p4=p4, j=j: e.matmul(ps[:, bank, p4 * 128:(p4 + 1) * 128], lhsT=notsel[:, j * 128:(j + 1) * 128], rhs=negI[:, :], start=False, stop=True),
                                 reads=[nsn, "negI"], writes=[("ps", bank)])
                    psv = ps[:, bA:bB + 1, :].rearrange("s e (p q) -> s e p q", q=128)
                    if ty <= 1:
                        P.op("dve", lambda e, psv=psv, ty=ty, g=g: e.tensor_tensor(out=tmpl[:, :, :].rearrange("s (e p) q -> s e p q", e=2), in0=psv,
                                                                                  in1=TzT[:, ty, :, 8 * g:8 * g + 8].rearrange("s q (p e) -> s e p q", e=2), op=ALU.add),
                             reads=[("ps", bA), ("ps", bB), "TzT"], writes=["tmpl"])
                        P.op("act", lambda e, slot=slot: e.activation(out=PT[:, slot, :, :], in_=tmpl[:, :, :], func=AF.Exp), reads=["tmpl"], writes=[("PT", slot)])
                    else:
                        P.op("act", lambda e, slot=slot, psv=psv: e.activation(out=PT[:, slot, :, :].rearrange("s (e p) q -> s e p q", e=2), in_=psv, func=AF.Exp),
                             reads=[("ps", bA), ("ps", bB)], writes=[("PT", slot)])
                    for hh in range(8):
                        head = 2 * (g * 4 + hh % 4) + hh // 4
                        bpv, col = head // 7, (head % 7) * 66
                        first = (j == 0 and head in PV_FIRST)
                        P.op("pe", lambda e, slot=slot, hh=hh, bpv=bpv, col=col, j=j, g=g, first=first, i=i: e.matmul(
                            ps[:, bpv, col:col + 65], lhsT=PT[:, slot, hh, :], rhs=Vaug[:, j, g, 0:65], start=first, stop=(j == i)),
                            reads=[("PT", slot), ("Vaug", j)], writes=[("ps", bpv)])
            for bk in range(3):
                nh = 7 if bk < 2 else 2
                P.op("dve", lambda e, bk=bk, nh=nh: e.reciprocal(out=rec[:, bk * 7:bk * 7 + nh], in_=ps[:, bk, 0:nh * 66].rearrange("q (h c) -> q h c", c=66)[:, :, 64]),
                     reads=[("ps", bk)], writes=["rec"])
                P.op("dve", lambda e, bk=bk, nh=nh: e.tensor_tensor(out=obn[:, bk * 7:bk * 7 + nh, :], in0=ps[:, bk, 0:nh * 66].rearrange("q (h c) -> q h c", c=66)[:, :, 0:64],
                                                                  in1=rec[:, bk * 7:bk * 7 + nh].unsqueeze(2).to_broadcast([128, nh, 64]), op=ALU.mult),
                     reads=[("ps", bk), "rec"], writes=["obn"])
            for p8 in range(8):
                P.op("pe", lambda e, p8=p8: e.transpose(psb[:, p8, :], obn[:, 2 * p8:2 * p8 + 2, :].rearrange("q h d -> q (h d)"), ident_b[:, :]), reads=["obn", "ident_b"], writes=["psb"])
            P.op("act", lambda e, q0=q0: e.copy(out=obT[:, :, q0:q0 + 128], in_=psb[:, :, :]), reads=["psb"], writes=QK)

        if NQB > 0:
            idx_topk(0)
        for i in range(NQB):
            if i + 1 < NQB:
                idx_topk(i + 1)
            attend(i)
        A.release(m3); P.barrier()
        NSB = int(os.environ.get("NSB", "16"))
        ptt = A.alloc([128, 16], I32); cmt = A.alloc([128, 1], I32); idx = A.alloc([128, 16], I32)
        score_all = A.alloc([64, 2176]); work_s = A.alloc([64, 2176]); nots = A.alloc([128, 2176], BF16); mxs = A.alloc([64, 8])
        masks = A.alloc([64, 64]); vm = A.alloc([64, 16, 4]); selh_f = A.alloc([128, 16, 32]); selh = A.alloc([128, 16, 32], BF16)
        wis = A.alloc([4, 16, 8])
        TzS = A.alloc([128, 16, 4, 16]); NBb = A.alloc([64, 4, 16])
        P.dma("sp", lambda e: e.dma_start(out=ptt[:, :], in_=ptrep_d), writes=["ptt"])
        P.dma("sp", lambda e: e.dma_start(out=cmt[:, :], in_=cmod_d), writes=["cmt"])
        P.dma("sp", lambda e: e.dma_start(out=masks[:, :], in_=masks_d), writes=["masks"])
        P.dma("sp", lambda e: e.dma_start(out=vm[:, :, :], in_=vm_d), writes=["vm"])
        P.dma("sp", lambda e: e.dma_start(out=selh_f[:, :, :], in_=selh_d), writes=["selh_f"])
        P.op("act", lambda e: e.copy(out=selh[:, :, :], in_=selh_f[:, :, :]), reads=["selh_f"], writes=["selh"])
        P.op("dve", lambda e: e.tensor_scalar(out=idx[:, :], in0=ptt[:, :], scalar1=8, scalar2=cmt[:, 0:1], op0=ALU.mult, op1=ALU.add), reads=["ptt", "cmt"], writes=["idx"])
        P.op("dve", lambda e: e.memset(nots[:, :], 0.0), writes=["nots"])
        for bq in range(NSB):
            P.dma("sp", lambda e, bq=bq: e.dma_start(out=wis[:, bq, :], in_=wi_t[4 * bq:4 * bq + 4, 16, :]), reads=[("wi_t", 16)], writes=["wis"], semkey="wis")
        m4 = A.mark()
        ohs_t = A.alloc([32, 2, 4, 4, 128], BF16) if False else A.alloc([32, 2, 2048], BF16)
        ohn_t = A.alloc([32, 4, 64], BF16)
        P.dma("sp", lambda e: e.dma_start(out=ohn_t[:, :, :], in_=ohn_d), writes=["ohn_t"])
        for uc in range(4):
            slot = uc % 2; bank = 3 + slot
            P.dma("sp", lambda e, uc=uc, slot=slot: e.dma_start(out=ohs_t[:, slot, :], in_=ohs_d[:, uc * 4:(uc + 1) * 4, :, :].rearrange("k u t p -> k (u t p)")), writes=[("ohs_t", slot)])
            for ut in range(16):
                for (rr, st_, sp_) in ((rbh, True, False), (rbl, False, True)):
                    P.op("pe", lambda e, slot=slot, bank=bank, ut=ut, rr=rr, st_=st_, sp_=sp_: e.matmul(ps[:, bank, ut * 16:(ut + 1) * 16], lhsT=ohs_t[:, slot, ut * 128:(ut + 1) * 128], rhs=rr[:, :], start=st_, stop=sp_),
                         reads=[("ohs_t", slot), "rbh", "rbl"], writes=[("ps", bank)])
            P.op("dve", lambda e, bank=bank, uc=uc: e.tensor_copy(out=TzS[:, uc * 4:(uc + 1) * 4, :, :], in_=ps[:, bank, 0:256].rearrange("p (u t h) -> p u t h", u=4, t=4)), reads=[("ps", bank)], writes=["TzS"])
        for t in range(4):
            for (rr, st_, sp_) in ((rbh, True, False), (rbl, False, True)):
                P.op("pe", lambda e, t=t, rr=rr, st_=st_, sp_=sp_: e.matmul(ps[0:64, 5, t * 16:(t + 1) * 16], lhsT=ohn_t[:, t, :], rhs=rr[:, :], start=st_, stop=sp_),
                     reads=["ohn_t", "rbh", "rbl"], writes=[("ps", 5)])
        P.op("dve", lambda e: e.tensor_copy(out=NBb[:, :, :], in_=ps[0:64, 5, 0:64].rearrange("p (t h) -> p t h", t=4)), reads=[("ps", 5)], writes=["NBb"])
        A.release(m4); P.barrier()
        KIb = A.alloc([128, 16, 64]); KId = A.alloc([128, 16, 2, 64], BF16); kiTs = A.alloc([128, 16, 128], BF16)
        accs = A.alloc([4, 2176]); relus = A.alloc([4, 2, 512])
        cki = cache_ki
        for bq in range(NSB):
            P.dma("pool", lambda e, bq=bq: e.indirect_dma_start(out=KIb[:, :, :].rearrange("p u d -> p (u d)"), out_offset=None, in_=cki,
                                                               in_offset=bass.IndirectOffsetOnAxis(ap=idx[:, bq:bq + 1], axis=0)), reads=["idx"], writes=["KIb"])
            P.op("act", lambda e: e.copy(out=KId[:, :, :, :], in_=KIb[:, :, :].unsqueeze(2).to_broadcast([128, 16, 2, 64])), reads=["KIb"], writes=["KId"])
            for ub in range(2):
                for u8 in range(8):
                    u = ub * 8 + u8
                    P.op("pe", lambda e, u=u, u8=u8: e.transpose(psb[:, u8, :], KId[:, u, :, :].rearrange("p a d -> p (a d)"), ident_b[:, :]), reads=["KId", "ident_b"], writes=["psb"])
                P.op("act", lambda e, ub=ub: e.copy(out=kiTs[:, ub * 8:(ub + 1) * 8, :], in_=psb[:, :, :]), reads=["psb"], writes=["kiTs"])
            for c in range(5):
                w = 512 if c < 4 else 64
                for h in range(8):
                    half, p4 = h % 2, h // 2
                    bank = 3 + (icnt[0] % 4); rsl = icnt[0] % 2; icnt[0] += 1
                    rhs_fn = (lambda c=c, half=half: kiTs[half * 64:(half + 1) * 64, 4 * c:4 * c + 4, :].rearrange("p u k -> p (u k)")) if c < 4 else (lambda half=half: kiT2s[half * 64:(half + 1) * 64, :])
                    P.op("pe", lambda e, bank=bank, w=w, half=half, p4=p4, bq=bq, rhs_fn=rhs_fn: e.matmul(ps[0:4, bank, 0:w], lhsT=qiTs[half * 64:(half + 1) * 64, p4, 4 * bq:4 * bq + 4], rhs=rhs_fn(), start=True, stop=True),
                         reads=["qiTs", "kiTs", "kiT2s"], writes=[("ps", bank)])
                    P.op("act", lambda e, bank=bank, w=w, rsl=rsl: e.activation(out=relus[:, rsl, 0:w], in_=ps[0:4, bank, 0:w], func=AF.Relu), reads=[("ps", bank)], writes=[("relus", rsl)])
                    if h == 0:
                        P.op("dve", lambda e, w=w, rsl=rsl, c=c, bq=bq: e.tensor_scalar(out=accs[:, c * 512:c * 512 + w], in0=relus[:, rsl, 0:w], scalar1=wis[:, bq, 0:1], scalar2=None, op0=ALU.mult),
                             reads=[("relus", rsl), "wis"], writes=["accs"])
                    else:
                        P.op("dve", lambda e, w=w, rsl=rsl, c=c, bq=bq, h=h: e.scalar_tensor_tensor(out=accs[:, c * 512:c * 512 + w], in0=relus[:, rsl, 0:w], scalar=wis[:, bq, h:h + 1], in1=accs[:, c * 512:c * 512 + w],
                                                                                                 op0=ALU.mult, op1=ALU.add),
                             reads=[("relus", rsl), "wis", "accs"], writes=["accs"])
            P.dma("sp", lambda e, bq=bq: e.dma_start(out=score_all[4 * bq:4 * bq + 4, 0:2112], in_=accs[:, 0:2112]), reads=["accs"], writes=["score_all"], semkey="sca")
        P.op("dve", lambda e: e.tensor_tensor(out=score_all[:, 2048:2112], in0=score_all[:, 2048:2112], in1=masks[:, :], op=ALU.add), reads=["score_all", "masks"], writes=["score_all"])
        for r in range(32):
            src = score_all if r == 0 else work_s
            P.op("dve", lambda e, src=src: e.max(out=mxs[:, :], in_=src[:, 0:2112]), reads=["score_all", "work_s"], writes=["mxs"])
            if r < 31:
                P.op("dve", lambda e, src=src: e.match_replace(out=work_s[:, 0:2112], in_to_replace=mxs[:, :], in_values=src[:, 0:2112], imm_value=-1e30),
                     reads=["score_all", "mxs", "work_s"], writes=["work_s"])
        P.op("dve", lambda e: e.tensor_scalar(out=nots[0:64, 0:2112], in0=score_all[:, 0:2112], scalar1=mxs[:, 7:8], scalar2=None, op0=ALU.is_lt), reads=["score_all", "mxs", "nots"], writes=["nots"])
        A.release(m4); P.barrier()
        Kb = A.alloc([128, 16, 128]); Vb = A.alloc([128, 16, 128]); Kd = A.alloc([128, 16, 2, 64], BF16)
        KTs = A.alloc([128, 2, 16, 128], BF16); Vs = A.alloc([128, 16, 2, 66], BF16)
        tmps = A.alloc([128, 2, 17, 32]); PTs = A.alloc([128, 2, 17, 32], BF16)
        recs = A.alloc([16, 4]); obs = A.alloc([16, 4, 64], BF16)
        P.op("dve", lambda e: e.memset(Vs[:, :, :, :], 1.0), writes=["Vs"])
        P.op("dve", lambda e: e.memset(tmps[:, :, :, :], 0.0), writes=["tmps"])
        for bq in range(NSB):
            c0 = 2048 + 4 * bq
            P.dma("pool", lambda e, bq=bq: e.indirect_dma_start(out=Kb[:, :, :].rearrange("p u d -> p (u d)"), out_offset=None, in_=cache_k,
                                                               in_offset=bass.IndirectOffsetOnAxis(ap=idx[:, bq:bq + 1], axis=0)), reads=["idx"], writes=["Kb"])
            P.dma("pool", lambda e, bq=bq: e.indirect_dma_start(out=Vb[:, :, :].rearrange("p u d -> p (u d)"), out_offset=None, in_=cache_v,
                                                               in_offset=bass.IndirectOffsetOnAxis(ap=idx[:, bq:bq + 1], axis=0)), reads=["idx"], writes=["Vb"])
            P.op("act", lambda e: e.copy(out=Vs[:, :, :, 0:64], in_=Vb[:, :, :].rearrange("p u (g d) -> p u g d", g=2)), reads=["Vb"], writes=["Vs"])
            for g in range(2):
                P.op("act", lambda e, g=g: e.copy(out=Kd[:, :, :, :], in_=Kb[:, :, g * 64:(g + 1) * 64].unsqueeze(2).to_broadcast([128, 16, 2, 64])), reads=["Kb"], writes=["Kd"])
                for ub in range(2):
                    for u8 in range(8):
                        u = ub * 8 + u8
                        P.op("pe", lambda e, u=u, u8=u8: e.transpose(psb[:, u8, :], Kd[:, u, :, :].rearrange("p a d -> p (a d)"), ident_b[:, :]), reads=["Kd", "ident_b"], writes=["psb"])
                    P.op("dve", lambda e, ub=ub, g=g: e.tensor_scalar(out=KTs[:, g, ub * 8:(ub + 1) * 8, :], in0=psb[:, :, :], scalar1=gq2[:, 0:1], scalar2=0.125, op0=ALU.mult, op1=ALU.mult),
                         reads=["psb", "gq2"], writes=["KTs"])
            for half in range(2):
                for u in range(17):
                    if u < 16:
                        out_fn = lambda g, half=half, u=u: ps[:, 3 + half, u * 32 + g * 16:u * 32 + g * 16 + 16]
                        mo = ps[:, 3 + half, u * 32:(u + 1) * 32]; wk = ("ps", 3 + half)
                        nl = nots[:, u * 128:(u + 1) * 128]
                    else:
                        out_fn = lambda g, half=half: ps[0:64, 5 + half, g * 16:g * 16 + 16]
                        mo = ps[0:64, 5 + half, 0:32]; wk = ("ps", 5 + half)
                        nl = nots[:, 2048:2112]
                    P.op("pe", lambda e, mo=mo, nl=nl, bq=bq: e.matmul(mo, lhsT=nl, rhs=selh[:, bq, :], start=True, stop=False), reads=["nots", "selh"], writes=[wk])
                    for g in range(2):
                        lt = KTs[half * 64:(half + 1) * 64, g, u, :] if u < 16 else KTd[half * 64:(half + 1) * 64, g, 2048:2112]
                        P.op("pe", lambda e, lt=lt, g=g, half=half, c0=c0, out_fn=out_fn: e.matmul(out_fn(g), lhsT=lt, rhs=qT[half * 64:(half + 1) * 64, g * 4:(g + 1) * 4, c0:c0 + 4], start=False, stop=(g == 1)),
                             reads=["KTs", ("KTd", g)] + QK, writes=[wk])
            for half in range(2):
                P.op("dve", lambda e, half=half: e.tensor_tensor(out=tmps[:, half, 0:16, :].rearrange("p u (a t) -> p u a t", t=4), in0=ps[:, 3 + half, :].rearrange("p (u a t) -> p u a t", u=16, t=4),
                                                                in1=TzS[:, :, :, half::2].rearrange("p u t a -> p u a t"), op=ALU.add), reads=[("ps", 3 + half), "TzS"], writes=["tmps"])
                P.op("dve", lambda e, half=half: e.tensor_tensor(out=tmps[0:64, half, 16, :].rearrange("p (a t) -> p a t", t=4), in0=ps[0:64, 5 + half, 0:32].rearrange("p (a t) -> p a t", t=4),
                                                                in1=NBb[:, :, half::2].rearrange("p t a -> p a t"), op=ALU.add), reads=[("ps", 5 + half), "NBb"], writes=["tmps"])
                P.op("dve", lambda e, half=half, bq=bq: e.tensor_tensor(out=tmps[0:64, half, 16, :].rearrange("p (a t) -> p a t", t=4), in0=tmps[0:64, half, 16, :].rearrange("p (a t) -> p a t", t=4),
                                                                       in1=vm[:, bq, :].unsqueeze(1).to_broadcast([64, 8, 4]), op=ALU.add), reads=["tmps", "vm"], writes=["tmps"])
            P.op("act", lambda e: e.activation(out=PTs[:, :, :, :], in_=tmps[:, :, :, :], func=AF.Exp), reads=["tmps"], writes=["PTs"])
            firstpv = True
            for u in range(17):
                for g in range(2):
                    for half in range(2):
                        if u < 16:
                            lt = PTs[:, half, u, g * 16:(g + 1) * 16]; rv = Vs[:, u, g, 0:65]
                        else:
                            lt = PTs[0:64, half, 16, g * 16:(g + 1) * 16]; rv = Vaug[0:64, 16, g, 0:65]
                        cc = (g * 2 + half) * 66
                        P.op("pe", lambda e, lt=lt, rv=rv, cc=cc, fp=firstpv, u=u: e.matmul(ps[0:16, 2, cc:cc + 65], lhsT=lt, rhs=rv, start=fp, stop=(u == 16)),
                             reads=["PTs", "Vs", ("Vaug", 16)], writes=[("ps", 2)])
                        firstpv = False
            P.op("dve", lambda e: e.reciprocal(out=recs[:, :], in_=ps[0:16, 2, 0:264].rearrange("q (h c) -> q h c", c=66)[:, :, 64]), reads=[("ps", 2)], writes=["recs"])
            P.op("dve", lambda e: e.tensor_tensor(out=obs[:, :, :], in0=ps[0:16, 2, 0:264].rearrange("q (h c) -> q h c", c=66)[:, :, 0:64],
                                                  in1=recs[:, :].unsqueeze(2).to_broadcast([16, 4, 64]), op=ALU.mult), reads=[("ps", 2), "recs"], writes=["obs"])
            for g in range(2):
                P.op("pe", lambda e, g=g: e.transpose(psb[:, g, 0:16], obs[:, 2 * g:2 * g + 2, :].rearrange("q h d -> q (h d)"), ident_b[0:16, 0:16]), reads=["obs", "ident_b"], writes=["psb"])
            P.op("act", lambda e, c0=c0: e.copy(out=obT[:, :, c0:c0 + 4].rearrange("p (g a) t -> p g a t", g=2), in_=psb[:, 0:2, 0:16].rearrange("p g (a t) -> p g a t", t=4)), reads=["psb"], writes=QK)
        if dbg_ob is not None:
            P.dma("sp", lambda e: e.dma_start(out=dbg_ob, in_=obT[:, :, :]), reads=QK, store=True)
        A.release(m3); P.barrier()
        sgm = A.alloc([128, 2, 512]); mtmp = A.alloc([128, 512]); gtmp = A.alloc([128, 2, 512], BF16)
        for p8 in range(8):
            sl = load_w([(0, C_AG + p8 * 128, 128)], 128)
            for ni, (n0, nn) in enumerate(NTL):
                bg = nbank(); ssl = ni % 2
                proj_fm(sl, 128, n0, nn, bg)
                P.op("act", lambda e, bg=bg, nn=nn, ssl=ssl: e.activation(out=gtmp[:, ssl, 0:nn], in_=ps[:, bg, 0:nn], func=AF.Silu), reads=[("ps", bg)], writes=[("gtmp", ssl)])
                P.op("dve", lambda e, nn=nn, ssl=ssl, p8=p8, n0=n0: e.tensor_tensor(out=obT[:, p8, n0:n0 + nn], in0=obT[:, p8, n0:n0 + nn], in1=gtmp[:, ssl, 0:nn], op=ALU.mult),
                     reads=[("gtmp", ssl), ("qT", p8)], writes=[("qT", p8)])
        merge_branch(obT, QK, w_pb, C_GB, False, sgm, mtmp)
        A.release(m2); P.barrier()
        wo = A.alloc([128, 8, D], BF16); xo = A.alloc([128, 2, D]); yo = A.alloc([128, 2, D])
        for cb in range(8):
            sl = load_w([(0, cb * 128, 128)], 128, src=w_out)
            P.op("act", lambda e, sl=sl, cb=cb: e.copy(out=wo[:, :, cb * 128:(cb + 1) * 128], in_=wb[:, sl, :, :]), reads=[("wb", sl)], writes=["wo"])
        for ti, (t0, n) in enumerate(TT):
            sl = ti % 2
            src = x_p[t0:t0 + n, :] if ti < 16 else x_s[:, :]
            P.dma("sp", lambda e, sl=sl, n=n, src=src: e.dma_start(out=xo[0:n, sl, :], in_=src), writes=[("xo", sl)])
            for hf in range(2):
                bk = nbank()
                for kc in range(8):
                    P.op("pe", lambda e, kc=kc, bk=bk, t0=t0, n=n, hf=hf: e.matmul(ps[0:n, bk, :], lhsT=mT[:, kc, t0:t0 + n], rhs=wo[:, kc, hf * 512:(hf + 1) * 512], start=(kc == 0), stop=(kc == 7)),
                         reads=["mT", "wo"], writes=[("ps", bk)])
                P.op("dve", lambda e, bk=bk, n=n, sl=sl, hf=hf: e.tensor_tensor(out=yo[0:n, sl, hf * 512:(hf + 1) * 512], in0=ps[0:n, bk, :], in1=xo[0:n, sl, hf * 512:(hf + 1) * 512], op=ALU.add),
                     reads=[("ps", bk), ("xo", sl)], writes=[("yo", sl)])
            dst = y_p[t0:t0 + n, :] if ti < 16 else y_s[:, :]
            P.dma("sp", lambda e, dst=dst, n=n, sl=sl: e.dma_start(out=dst, in_=yo[0:n, sl, :]), reads=[("yo", sl)], store=True)

        return finish()


_CACHE = {}


def kernel(**inputs):
    f32 = np.float32
    if "nc" not in _CACHE:
        _CACHE["nc"] = build_program()
    nc = _CACHE["nc"]
    x_prompt = np.asarray(inputs["x_prompt"], f32); x_sample = np.asarray(inputs["x_sample"], f32)
    bones = np.zeros((128, 128), f32); bones[:64, :64] = 1; bones[64:, 64:] = 1
    common = {
        "w_in": np.ascontiguousarray(np.asarray(inputs["w_in"], f32)[0]),
        "norm_g": np.asarray(inputs["norm_g"], f32),
        "q_norm_g": np.asarray(inputs["q_norm_g"], f32), "k_norm_g": np.asarray(inputs["k_norm_g"], f32),
        "ident": np.eye(128, dtype=f32), "bones": bones,
        "rel_bias": np.asarray(inputs["rel_bias"], f32),
    }
    import ml_dtypes
    qq = np.arange(128)[None, :, None]; sk = np.arange(128)[None, None, :]; ty = np.arange(2)[:, None, None]
    dist = 128 * ty + qq - sk
    bk = _t5_bucket_np(dist)
    oh = (bk[None] == np.arange(32)[:, None, None, None]) & (dist[None] >= 0)
    common["oh_p"] = oh.astype(f32).astype(ml_dtypes.bfloat16)
    sq_ = np.arange(128)
    common["caus0"] = np.where(sq_[:, None] <= sq_[None, :], 0.0, -30000.0).astype(f32)
    common["causS"] = np.where(sq_[None, :] <= sq_[:, None], 0.0, -1e30).astype(f32)
    uu = np.arange(16)[:, None, None]; tt = np.arange(4)[None, :, None]; pp = np.arange(128)[None, None, :]
    s_key = 1920 + (pp - 120) * 16 + uu
    dist_s = 2048 + tt - s_key
    ohs = (_t5_bucket_np(dist_s)[None] == np.arange(32)[:, None, None, None]) & (pp >= 120)[None]
    common["ohs"] = ohs.astype(f32).astype(ml_dtypes.bfloat16)
    kt = (np.arange(64) % 4)[None, :]; t4 = np.arange(4)[:, None]
    dn = t4 - kt
    ohn = (_t5_bucket_np(dn)[None] == np.arange(32)[:, None, None]) & (dn >= 0)[None]
    common["ohn"] = ohn.astype(f32).astype(ml_dtypes.bfloat16)
    kb_ = (np.arange(64) // 4); ktt = (np.arange(64) % 4)
    vm = np.where((kb_[:, None, None] == np.arange(16)[None, :, None]) & (ktt[:, None, None] <= np.arange(4)[None, None, :]), 0.0, -30000.0)
    common["vm"] = vm.astype(f32)
    selh = np.zeros((128, 16, 8, 4), f32)
    for bb in range(16):
        for t_ in range(4):
            selh[4 * bb + t_, bb, :, t_] = -30000.0
    common["selh"] = selh.reshape(128, 16, 32)
    common["masks"] = np.where((kb_[:, None] == kb_[None, :]) & (ktt[None, :] <= ktt[:, None]), 0.0, -1e30).astype(f32)
    common["cmod"] = (np.arange(128) % 8).astype(np.int32).reshape(128, 1)
    if STAGE >= 6:
        common["cache_k"] = np.asarray(inputs["cache_k"], f32).reshape(NPHYS * 8, 2048)
        common["cache_v"] = np.asarray(inputs["cache_v"], f32).reshape(NPHYS * 8, 2048)
        common["cache_ki"] = np.asarray(inputs["cache_kidx"], f32).reshape(NPHYS * 8, 1024)
    for nm in ("w_pa", "w_pb", "w_out"):
        common[nm] = np.ascontiguousarray(np.asarray(inputs[nm], f32)[0])
    for nm in ("shift_mu", "w0", "a0", "k_k", "k_a", "lnx_g", "lnx_b"):
        common[nm] = np.asarray(inputs[nm], f32).reshape(1, -1)
    common["r_k"] = np.asarray(inputs["r_k"], f32).reshape(1, 1024)
    common["w2"] = np.ascontiguousarray(np.asarray(inputs["w2"], f32)[0]); common["a2"] = np.ascontiguousarray(np.asarray(inputs["a2"], f32)[0])
    i64 = np.arange(64)
    common["mask_u"] = (i64[:, None] < i64[None, :]).astype(f32)
    common["mask_ui"] = (i64[:, None] <= i64[None, :]).astype(f32)
    common["mask_l"] = (i64[None, :] < i64[:, None]).astype(f32)
    seg = np.ones((64, 2, 4, 64), f32); seg[:, 0, :, 0] = 0.0; seg[:, 1, :, 0::4] = 0.0
    common["segmask"] = seg.reshape(64, 2, 256)
    state_wkv = np.asarray(inputs["state_wkv"], f32)[0]; state_shift = np.asarray(inputs["state_shift"], f32)[0]
    page_table = np.asarray(inputs["page_table"], np.int32)
    in_maps = []
    for c in range(NCORES):
        m = dict(common)
        m["state_wkv"] = np.ascontiguousarray(state_wkv[16 * c:16 * c + 16]); m["state_shift"] = np.ascontiguousarray(state_shift[16 * c:16 * c + 16])
        m["ptrep"] = np.ascontiguousarray(np.repeat(page_table[16 * c:16 * c + 16], 8, axis=1).T)
        m["x_p"] = np.ascontiguousarray(x_prompt[c])
        m["x_s"] = np.ascontiguousarray(x_sample[16 * c:16 * c + 16].reshape(64, D))
        in_maps.append(m)
    res = run_bass_kernel_spmd(nc, in_maps, core_ids=list(range(NCORES)))
    R = res.results
    if os.environ.get("DBG"):
        for k_ in R[0]:
            if k_.startswith("dd_") or k_.startswith("dbg_"):
                np.save(k_ + ".npy", np.asarray(R[0][k_]).astype(f32))
    cat = lambda name: np.stack([R[c][name] for c in range(NCORES)], 0)
    y_p = cat("y_p").reshape(8, 2048, 1024)
    y_s = cat("y_s").reshape(128, 4, 1024)
    k_p = cat("k_p").reshape(1, 8, 2048, 2, 64); v_p = cat("v_p").reshape(1, 8, 2048, 2, 64)
    ki_p = cat("ki_p").reshape(1, 8, 2048, 64)
    wkv_p = cat("wkv_p").reshape(1, 8, 16, 64, 64)
    sh_p = cat("sh_p").reshape(1, 8, 4224)
    k_s = cat("k_s").reshape(1, 128, 4, 2, 64); v_s = cat("v_s").reshape(1, 128, 4, 2, 64)
    ki_s = cat("ki_s").reshape(1, 128, 4, 64)
    wkv_s = cat("wkv_s").reshape(1, 128, 16, 64, 64)
    sh_s = cat("sh_s").reshape(1, 128, 4224)
    return (y_p, y_s, k_p, v_p, ki_p, wkv_p, sh_p, k_s, v_s, ki_s, wkv_s, sh_s)
```
